# Optimizing a Trainium2 kernel written in Bass

```python
import jax, jax.numpy as jnp
from jax import lax
import numpy as np

D_MODEL = 1024
BATCH = 2
SEQ = 8192
DEPTH = 1

N_META = 16
D_MIX = D_MODEL
D_CONV = D_MIX // 2
CONV_HEADS = 8
CONV_WIDTH = 3
D_RET = D_MIX - D_CONV
RET_HEADS = 4
RET_HEAD_DIM = D_RET // RET_HEADS
CHUNK = 128
ROPE_BASE = 10000.0
EPS = 1e-6
N_PROJ = 8

kernel_name = "hymba_conv_retention_hybrid"


def rms_norm(x, g):
    xf = x.astype(jnp.float32)
    y = xf * lax.rsqrt(jnp.mean(xf * xf, axis=-1, keepdims=True) + EPS)
    return (y * g.astype(jnp.float32)).astype(x.dtype)


def rotary(t, pos):
    half = t.shape[-1] // 2
    freqs = 1.0 / (ROPE_BASE ** (jnp.arange(half, dtype=jnp.float32) / half))
    ang = pos.astype(jnp.float32)[:, None] * freqs[None, :]
    cos = jnp.cos(ang)[None, :, None, :]
    sin = jnp.sin(ang)[None, :, None, :]
    t1, t2 = t[..., :half], t[..., half:]
    return jnp.concatenate([t1 * cos - t2 * sin, t1 * sin + t2 * cos], axis=-1)


def short_conv_branch(h, b_gate, c_gate, conv_w):
    u = c_gate * h
    kern = conv_w.reshape(CONV_WIDTH, 1, D_CONV).astype(u.dtype)
    conv = lax.conv_general_dilated(
        u, kern, window_strides=(1,), padding=[(CONV_WIDTH - 1, 0)],
        dimension_numbers=('NWC', 'WIO', 'NWC'), feature_group_count=D_CONV)
    return b_gate * conv


def retention_chunkwise(q, k, v):
    bsz, L, H, d = q.shape
    pad = CHUNK - N_META
    P = L + pad
    n_chunks = P // CHUNK

    def to_chunks(t):
        t = jnp.pad(t, ((0, 0), (pad, 0), (0, 0), (0, 0)))
        return t.reshape(bsz, n_chunks, CHUNK, H, d).transpose(0, 3, 1, 2, 4)

    qc, kc, vc = to_chunks(q), to_chunks(k), to_chunks(v)
    log_g = jnp.log(1.0 - 2.0 ** (-5.0 - jnp.arange(H, dtype=jnp.float32)))
    idx = jnp.arange(CHUNK, dtype=jnp.float32)
    diff = idx[:, None] - idx[None, :]
    decay = jnp.where(diff[None] >= 0, jnp.exp(diff[None] * log_g[:, None, None]), 0.0)

    scores = jnp.einsum('bhnid,bhnjd->bhnij', qc, kc) * decay[None, :, None]
    inner = jnp.einsum('bhnij,bhnje->bhnie', scores, vc)

    zeta = jnp.exp((CHUNK - 1 - idx)[None, :] * log_g[:, None])
    upd = jnp.einsum('bhnjd,bhnje->nbhde', kc * zeta[None, :, None, :, None], vc)
    chunk_decay = jnp.exp(CHUNK * log_g)[None, :, None, None]

    def step(state, u):
        return chunk_decay * state + u, state

    init = jnp.zeros((bsz, H, d, d), jnp.float32)
    _, states = lax.scan(step, init, upd)
    xi = jnp.exp((idx + 1.0)[None, :] * log_g[:, None])
    cross = jnp.einsum('bhnid,nbhde->bhnie', qc * xi[None, :, None, :, None], states)

    out = (inner + cross).transpose(0, 2, 3, 1, 4).reshape(bsz, P, H, d)
    return out[:, pad:]


def head_group_norm(o, g):
    mu = jnp.mean(o, axis=-1, keepdims=True)
    var = jnp.mean(jnp.square(o - mu), axis=-1, keepdims=True)
    y = (o - mu) * lax.rsqrt(var + EPS)
    bsz, L = o.shape[:2]
    return y.reshape(bsz, L, D_RET) * g.astype(jnp.float32)


def mixer_layer(h, norm_g, w_in, conv_w, ret_norm_g, w_out, pos):
    bsz, L, _ = h.shape
    hn = rms_norm(h, norm_g)
    proj = jnp.einsum('bld,de->ble', hn, w_in)
    cx, cb, cc, cg, q, k, v, rg = jnp.split(proj, N_PROJ, axis=-1)

    conv_out = short_conv_branch(cx, cb, cc, conv_w) * jax.nn.silu(cg)

    shp = (bsz, L, RET_HEADS, RET_HEAD_DIM)
    qf = rotary(q.reshape(shp).astype(jnp.float32), pos) * (RET_HEAD_DIM ** -0.5)
    kf = rotary(k.reshape(shp).astype(jnp.float32), pos)
    vf = v.reshape(shp).astype(jnp.float32)
    ret = head_group_norm(retention_chunkwise(qf, kf, vf), ret_norm_g)
    ret_out = ret.astype(h.dtype) * jax.nn.silu(rg)

    mixed = jnp.concatenate([conv_out, ret_out], axis=-1)
    return h + jnp.einsum('ble,ed->bld', mixed, w_out)


def setup_inputs(seed: int = 0) -> dict:
    key = jax.random.key(seed)
    ks = jax.random.split(key, 8)
    x = jax.random.normal(ks[0], (BATCH, SEQ, D_MODEL), jnp.float32)
    meta = jax.random.normal(ks[1], (N_META, D_MODEL), jnp.float32)
    norm1_g = 1.0 + 0.02 * jax.random.normal(ks[2], (D_MODEL,), jnp.float32)
    w_in = jax.random.normal(ks[3], (D_MODEL, N_PROJ * D_CONV), jnp.float32) * D_MODEL ** -0.5
    conv_w = jax.random.normal(ks[4], (CONV_WIDTH, D_CONV), jnp.float32) * CONV_WIDTH ** -0.5
    ret_norm_g = 1.0 + 0.02 * jax.random.normal(ks[5], (D_RET,), jnp.float32)
    w_out = jax.random.normal(ks[6], (D_MIX, D_MODEL), jnp.float32) * D_MIX ** -0.5
    final_g = 1.0 + 0.02 * jax.random.normal(ks[7], (D_MODEL,), jnp.float32)
    return {"x": x, "meta": meta, "norm1_g": norm1_g, "w_in": w_in, "conv_w": conv_w,
            "ret_norm_g": ret_norm_g, "w_out": w_out, "final_g": final_g}


def reference(x, meta, norm1_g, w_in, conv_w, ret_norm_g, w_out, final_g):
    bsz = x.shape[0]
    meta_b = jnp.broadcast_to(meta.astype(x.dtype)[None], (bsz, N_META, D_MODEL))
    h = jnp.concatenate([meta_b, x], axis=1)
    pos = jnp.arange(h.shape[1], dtype=jnp.int32)
    for _ in range(DEPTH):
        h = mixer_layer(h, norm1_g, w_in, conv_w, ret_norm_g, w_out, pos)
    h = rms_norm(h, final_g)
    return h[:, N_META:]
```

```python
import numpy as np
from contextlib import ExitStack
import concourse.bass as bass
import concourse.mybir as mybir
from concourse.bass_utils import run_bass_kernel_spmd

F32 = mybir.dt.float32
BF16 = mybir.dt.bfloat16
ALU = mybir.AluOpType
AF = mybir.ActivationFunctionType

D = 1024
SEQ = 8192
BATCH = 2
NMETA = 16
NCORE = 8
TOK = 2048
NT = 16
NH = 4
EPS = 1e-6
ENGS = ["sync", "scalar", "vector", "gpsimd", "tensor"]

GAMMA = [1.0 - 2.0 ** (-5.0 - h) for h in range(NH)]
CD = [g ** 128 for g in GAMMA]

C_CX, C_CB, C_CC, C_CG, C_Q, C_K, C_V, C_RG = [i * 512 for i in range(8)]
R_CX, R_CB, R_CC, R_CG, R_Q, R_RG = 0, 512, 1024, 1536, 2048, 2560


class Buf:
    def __init__(self, name, excl=False):
        self.name = name
        self.w = None
        self.r = []
        self.excl = excl


def _cost(eng, n):
    if eng == "vector":
        return 0.12 + n / 900.0
    if eng == "scalar":
        return 0.22 + n / 1200.0
    if eng == "gpsimd":
        return 0.35 + n / 330.0
    return 0.25


SEM_LAT = 0.12
import os
NOSCHED = bool(int(os.environ.get("KERNEL_NOSCHED", "0")))
SCHED_CP = bool(int(os.environ.get("KERNEL_CP", "1")))
SCHED_W = int(os.environ.get("KERNEL_W", "96"))
DMA_BPUS = float(os.environ.get("KERNEL_DMA_BPUS", "150e3"))


class Prog:
    def __init__(self, nc, stack):
        self.nc = nc
        self.stack = stack
        self.ops = []
        self.esem = {e: stack.enter_context(nc.semaphore("es_" + e)) for e in ENGS if e != "sync"}
        self.dsem = {}

    def sb(self, name, shape, dt):
        return self.stack.enter_context(self.nc.sbuf_tensor(name, shape, dt))

    def ps(self, name, shape, dt):
        return self.stack.enter_context(self.nc.psum_tensor(name, shape, dt))

    def _add(self, op, reads, writes, extra):
        idx = len(self.ops)
        deps = {}
        for b in reads:
            if b.w is not None:
                deps[b.w] = "raw"
            if b.excl:
                for r in b.r:
                    deps.setdefault(r, "war")
        for b in writes:
            if b.w is not None:
                deps[b.w] = "raw"
            for r in b.r:
                deps.setdefault(r, "war")
        for x in extra:
            if x is not None:
                deps[x] = "raw"
        deps.pop(idx, None)
        op["deps"] = deps
        op["idx"] = idx
        self.ops.append(op)
        for b in reads:
            if b not in writes:
                b.r.append(idx)
        for b in writes:
            b.w = idx
            b.r = []
        return idx

    def op(self, eng, fns, reads=(), writes=(), extra=(), n=512, c=None):
        if not isinstance(fns, (list, tuple)):
            fns = [fns]
        cost = c if c is not None else _cost(eng, n)
        return self._add({"eng": eng, "fns": list(fns), "kind": "op", "busy": cost, "lat": cost}, reads, writes, extra)

    def dma(self, eng, slot, out, in_, reads=(), writes=(), extra=(), nbytes=65536):
        busy = {"sync": 0.6, "scalar": 0.8}.get(eng, 1.0)
        lat = busy + 2.0 + nbytes / DMA_BPUS
        return self._add({"eng": eng, "kind": "dma", "slot": slot, "out": out, "in_": in_, "busy": busy, "lat": lat},
                         reads, writes, extra)

    def raw(self, eng, fn, sem, reads=(), writes=(), busy=1.0, lat=50.0):
        return self._add({"eng": eng, "kind": "raw", "fn": fn, "sem": sem, "busy": busy, "lat": lat},
                         reads, writes, ())

    def schedule(self):
        ops = self.ops
        N = len(ops)
        if NOSCHED:
            order = {e: [] for e in ENGS}
            for o in ops:
                order[o["eng"]].append(o["idx"])
            self.order = order
            self.makespan = 0.0
            return order
        succ = [[] for _ in range(N)]
        for o in ops:
            for d in o["deps"]:
                succ[d].append(o["idx"])
        blevel = [0.0] * N
        for i in range(N - 1, -1, -1):
            m = 0.0
            for j in succ[i]:
                if blevel[j] > m:
                    m = blevel[j]
            blevel[i] = ops[i]["lat"] + SEM_LAT + m
        pending = {e: [] for e in ENGS}
        for o in ops:
            pending[o["eng"]].append(o["idx"])
        fin = [None] * N
        free = {e: 0.0 for e in ENGS}
        order = {e: [] for e in ENGS}
        head = {e: 0 for e in ENGS}
        done = [False] * N
        W = SCHED_W
        nleft = N
        while nleft:
            best = None
            for e in ENGS:
                lst = pending[e]
                h = head[e]
                while h < len(lst) and done[lst[h]]:
                    h += 1
                head[e] = h
                cand = None
                cnt = 0
                for j in range(h, len(lst)):
                    i = lst[j]
                    if done[i]:
                        continue
                    cnt += 1
                    if cnt > W:
                        break
                    o = ops[i]
                    est = 0.0
                    ok = True
                    for d in o["deps"]:
                        f = fin[d]
                        if f is None:
                            ok = False
                            break
                        if f + SEM_LAT > est:
                            est = f + SEM_LAT
                    if not ok:
                        continue
                    st = est if est > free[e] else free[e]
                    key = (st, -blevel[i] if SCHED_CP else i, i)
                    if cand is None or key < cand:
                        cand = key
                if cand is not None and (best is None or (cand[0], cand[2]) < (best[0], best[1])):
                    best = (cand[0], cand[2], e)
            assert best is not None, "scheduler deadlock"
            st, i, e = best
            o = ops[i]
            fin[i] = st + o["lat"]
            free[e] = st + o["busy"]
            done[i] = True
            order[e].append(i)
            nleft -= 1
        self.order = order
        self.makespan = max(f for f in fin)
        return order

    def run(self):
        order = self.schedule()
        ops = self.ops
        tok = [None] * len(ops)
        dcnt = {}
        for e in ENGS:
            cnt = 0
            for i in order[e]:
                o = ops[i]
                if o["kind"] == "op":
                    cnt += 1
                    tok[i] = (self.esem[e], cnt, "e_" + e)
                elif o["kind"] == "dma":
                    slot = o["slot"]
                    if slot not in self.dsem:
                        self.dsem[slot] = self.stack.enter_context(self.nc.semaphore("ds_" + slot))
                        dcnt[slot] = 0
                    dcnt[slot] += 16
                    tok[i] = (self.dsem[slot], dcnt[slot], "d_" + slot)
                else:
                    tok[i] = (o["sem"], 1, "r_%d" % i)

        def emit_engine(e, eng_obj):
            waited = {}
            for i in order[e]:
                o = ops[i]
                need = {}
                for d, kind in o["deps"].items():
                    od = ops[d]
                    if e == "tensor" and od["eng"] == "tensor" and od["kind"] == "op":
                        continue
                    sem, val, key = tok[d]
                    if waited.get(key, 0) >= val:
                        continue
                    if need.get(key, (None, 0))[1] < val:
                        need[key] = (sem, val)
                for key, (sem, val) in need.items():
                    waited[key] = val
                    eng_obj.wait_ge(sem, val)
                if o["kind"] == "op":
                    for f in o["fns"][:-1]:
                        f(eng_obj)
                    o["fns"][-1](eng_obj).then_inc(tok[i][0], 1)
                elif o["kind"] == "dma":
                    eng_obj.dma_start(out=o["out"], in_=o["in_"]).then_inc(tok[i][0], 16)
                else:
                    o["fn"](eng_obj).then_inc(o["sem"])

        final = getattr(self, "final", None)

        def emit_final(e, eng_obj):
            if final is not None and final[0] == e:
                seen = {}
                for d in final[1]:
                    sem, val, key = tok[d]
                    if seen.get(key, (None, 0))[1] < val:
                        seen[key] = (sem, val)
                for key, (sem, val) in seen.items():
                    eng_obj.wait_ge(sem, val)

        with self.nc.Block() as block:
            @block.sync
            def _(e):
                emit_engine("sync", e)
                emit_final("sync", e)

            @block.scalar
            def _(e):
                emit_engine("scalar", e)

            @block.vector
            def _(e):
                emit_engine("vector", e)

            @block.gpsimd
            def _(e):
                emit_engine("gpsimd", e)

            @block.tensor
            def _(e):
                emit_engine("tensor", e)


def build_nc():
    nc = bass.Bass("TRN2", target_bir_lowering=False)
    dr = lambda name, shape, kind="ExternalInput": nc.dram_tensor(name, shape, F32, kind=kind).ap()
    x_d = dr("x", [TOK, D])
    xprev_d = dr("xprev", [16, D])
    meta_d = dr("meta", [16, D])
    win_d = dr("w_in", [D, 4096])
    wout_d = dr("w_out", [D, D])
    g1_d = dr("norm1_g", [1, D])
    gf_d = dr("final_g", [1, D])
    convw_d = dr("convw", [128, 12])
    gret_d = dr("gret", [128, 4])
    cs_d = dr("cs", [17 * 128, 128])
    ab_d = dr("ab", [128, 8])
    mask_d = dr("maskT", [128, 128])
    ident_d = dr("ident", [128, 128])
    coef_d = dr("coef", [1, 20])
    y_d = dr("y", [TOK, D], kind="ExternalOutput")
    aloc_d = dr("a_loc", [128, 512], kind="Internal")
    aall_d = dr("a_all", [4 * 128, 512], kind="Internal")

    win_v = win_d.rearrange("(k p) n -> p k n", p=128)
    wout_v = wout_d.rearrange("(k p) n -> p k n", p=128)
    cs_v = cs_d.rearrange("(t p) f -> p t f", p=128)

    with ExitStack() as st:
        P = Prog(nc, st)
        wkv = P.sb("wkv", [128, 8, 1024], BF16)
        wrest = P.sb("wrest", [128, 8, 3072], BF16)
        wout = P.sb("wout", [128, 8, 1024], BF16)
        hnT = P.sb("hnT", [128, 8, 16 + TOK], BF16)
        ktok = P.sb("ktok", [128, NT, 512], BF16)
        vtok = P.sb("vtok", [128, NT, 512], BF16)
        xt = [P.sb(f"xt{i}", [128, D], F32) for i in range(3)]
        xn_mixr = P.sb("xn_mixr", [128, 2048], BF16)
        junk = P.sb("junk", [128, D], BF16)
        gbc = P.sb("gbc", [128, D], F32)
        cst = [P.sb(f"cs{i}", [128, 128], F32) for i in range(4)]
        ks = P.sb("ks", [128, 512], F32)
        rA = P.sb("rA", [128, 512], F32)
        rB = P.sb("rB", [128, 512], F32)
        qrot = P.sb("qrot", [128, 512], BF16)
        qkT = P.sb("qkT", [128, 8, 128], BF16)
        sT = P.sb("sT", [128, 4, 128], BF16)
        ybuf = P.sb("ybuf", [128, 4, 128], BF16)
        c1 = [P.sb(f"c1_{i}", [128, 512], F32) for i in range(2)]
        tb = [P.sb(f"tb_{i}", [128, 512], F32) for i in range(2)]
        uw = [P.sb(f"uw{i}", [128, 514], F32) for i in range(2)]
        uhalo = P.sb("uhalo", [128, 4, 2], F32)
        mixc = [P.sb(f"mixc{i}", [128, 4, 512], BF16) for i in range(2)]
        Rst = P.sb("Rst", [128, 4, 128], F32)
        U0 = P.sb("U0", [128, 4, 128], F32)
        Sbf = [P.sb(f"Sbf{i}", [128, 4, 128], BF16) for i in range(2)]
        identb = P.sb("identb", [128, 128], BF16)
        maskT = P.sb("maskT_sb", [128, 128], F32)
        ab = P.sb("ab_sb", [128, 8], F32)
        coef = P.sb("coef_sb", [128, 20], F32)
        convw = P.sb("convw_sb", [128, 12], F32)
        gret = P.sb("gret_sb", [128, 4], F32)
        mhalf = P.sb("mhalf", [128, 4], F32)
        ssq = P.sb("ssq", [128, 64], F32)
        ms = P.sb("ms", [128, 64], F32)
        rstd = P.sb("rstd", [128, 64], F32)
        cdb = P.sb("cdb", [128, 4], F32)
        bns = P.sb("bns", [128, 2, 4, 6], F32)
        mv = P.sb("mv", [128, 2, 4, 2], F32)
        gve = P.sb("gve", [128, 2, 4], F32)
        grs = P.sb("grs", [128, 2, 4], F32)
        gnm = P.sb("gnm", [128, 2, 4], F32)
        uh_sb = P.sb("uh_sb", [128, 16], F32)

        identf = junk[:, 0:256].bitcast(F32)
        xt_p1 = xt + [mixc[1][:].rearrange("p a b -> p (a b)").bitcast(F32)]
        xn = [xn_mixr[:, 0:1024], xn_mixr[:, 1024:2048]]
        mixr = xn_mixr[:].rearrange("p (h t) -> p h t", h=4)
        wkv_flat = wkv[:].rearrange("p a b -> p (a b)")
        AG = wkv_flat[:, 0:4096].bitcast(F32).rearrange("p (r f) -> p r f", r=4)
        sgr = [wkv_flat[:, 4096:6144].rearrange("p (h t) -> p h t", h=4),
               wkv_flat[:, 6144:8192].rearrange("p (h t) -> p h t", h=4)]
        hnT_m = qkT
        ktok_m = qrot
        vtok_m = sT[:].rearrange("p a b -> p (a b)")

        bank = [P.ps(f"bank{i}", [128, 512], F32) for i in range(8)]
        bankb = [Buf(f"bank{i}", excl=True) for i in range(8)]

        def bf_view(i):
            return bank[i][:].bitcast(BF16).rearrange("p (k t) -> p k t", k=8)

        def h_view(i):
            return bank[i][:].rearrange("p (h t) -> p h t", h=4)

        B = {}

        def buf(name):
            if name not in B:
                B[name] = Buf(name)
            return B[name]

        MM512 = 0.27
        MM128 = 0.09
        TR = 0.12

        P.dma("sync", "c_ident", identf, ident_d[:, :], writes=[buf("junk")])
        P.dma("sync", "c_g", gbc[:], g1_d[0:1, :].broadcast_to([128, D]), writes=[buf("gbc")], nbytes=524288)
        P.dma("scalar", "c_ab", ab[:], ab_d[:, :], writes=[buf("ab")])
        P.dma("scalar", "c_mask", maskT[:], mask_d[:, :], writes=[buf("maskT")])
        P.dma("scalar", "c_coef", coef[:], coef_d[0:1, :].broadcast_to([128, 20]), writes=[buf("coef")])
        P.dma("scalar", "c_convw", convw[:], convw_d[:, :], writes=[buf("convw")])
        P.dma("scalar", "c_gret", gret[:], gret_d[:, :], writes=[buf("gret")])
        P.dma("gpsimd", "w_k", wkv[:, :, 0:512], win_v[:, :, C_K:C_K + 512], writes=[buf("wk")], nbytes=2 << 20)
        P.dma("gpsimd", "w_v", wkv[:, :, 512:1024], win_v[:, :, C_V:C_V + 512], writes=[buf("wv")], nbytes=2 << 20)
        P.op("gpsimd", lambda e: e.memset(mhalf[:], -0.5), writes=[buf("mhalf")], n=4)
        P.op("gpsimd", [lambda e, h=h: e.memset(cdb[:, h:h + 1], float(CD[h])) for h in range(4)],
             writes=[buf("cdb")], n=16)
        P.op("vector", lambda e: e.tensor_copy(out=identb[:], in_=identf), reads=[buf("junk")],
             writes=[buf("identb")], n=128)
        wpieces = []
        wparts = {}

        def add_piece(nm, dst, src, nb):
            key = nm + "_%d" % len(wparts.setdefault(nm, []))
            wparts[nm].append(buf(key))
            wpieces.append((key, dst, src, nb))

        for (nm, ro, co) in [("wcc", R_CC, C_CC), ("wcx", R_CX, C_CX), ("wcb", R_CB, C_CB), ("wcg", R_CG, C_CG)]:
            for qk in range(4):
                add_piece(nm, wrest[:, 2 * qk:2 * qk + 2, ro:ro + 512], win_v[:, 2 * qk:2 * qk + 2, co:co + 512], 1 << 19)
        n_paced = len(wpieces)
        for (nm, ro, co) in [("wrg", R_RG, C_RG), ("wq", R_Q, C_Q)]:
            for hk in range(2):
                add_piece(nm, wrest[:, 4 * hk:4 * hk + 4, ro:ro + 512], win_v[:, 4 * hk:4 * hk + 4, co:co + 512], 1 << 20)
        for half in range(2):
            for hk in range(2):
                add_piece(f"wout{half}", wout[:, 4 * hk:4 * hk + 4, half * 512:(half + 1) * 512],
                          wout_v[:, 4 * hk:4 * hk + 4, half * 512:(half + 1) * 512], 1 << 20)
        assert n_paced == NT

        def wb(nm):
            return list(wparts[nm])

        wstate = {"n": 0}

        def load_weight_piece(after_op):
            key, dst, src, nb = wpieces.pop(0)
            P.dma("gpsimd", "w_" + key, dst, src, writes=[buf(key)], extra=[after_op], nbytes=nb)
            wstate["n"] += 1
            if wstate["n"] == NT:
                while wpieces:
                    key, dst, src, nb = wpieces.pop(0)
                    P.dma("gpsimd", "w_" + key, dst, src, writes=[buf(key)], extra=[after_op], nbytes=nb)

        cnt = {"slot": 0, "cs": 0}

        def rms_rstd(src_ap, srcbuf, col, extra_reads=()):
            P.op("scalar", lambda e: e.activation(out=junk[:], in_=src_ap, func=AF.Square,
                                                  accum_out=ssq[:, col:col + 1]),
                 reads=[srcbuf] + list(extra_reads), writes=[buf(f"ssq{col}"), buf("junk")], n=1024)
            P.op("scalar", lambda e: e.activation(out=ms[:, col:col + 1], in_=ssq[:, col:col + 1], func=AF.Identity,
                                                  scale=1.0 / D, bias=EPS),
                 reads=[buf(f"ssq{col}")], writes=[buf(f"ms{col}")], n=1)
            P.op("gpsimd", lambda e: e.tensor_tensor(out=rstd[:, col:col + 1], in0=ms[:, col:col + 1],
                                                     in1=mhalf[:, 0:1], op=ALU.pow),
                 reads=[buf(f"ms{col}"), buf("mhalf")], writes=[buf(f"rstd{col}")], c=1.0)
            return rstd[:, col:col + 1], buf(f"rstd{col}")

        def rotary(psum_i, scale_bc, csl, csbuf, dst_ap, dstbuf):
            pv = h_view(psum_i)
            ks4 = ks[:].rearrange("p (h t f) -> p h t f", h=4, t=2)
            a4 = rA[:].rearrange("p (h t f) -> p h t f", h=4, t=2)
            b4 = rB[:].rearrange("p (h t f) -> p h t f", h=4, t=2)
            d4 = dst_ap.rearrange("p (h t f) -> p h t f", h=4, t=2)
            cosb = cst[csl][:, 0:64].unsqueeze(1).unsqueeze(1).broadcast_to([128, 4, 2, 64])
            sinb = cst[csl][:, 64:128].unsqueeze(1).unsqueeze(1).broadcast_to([128, 4, 2, 64])
            ksh = ks[:].rearrange("p (h t) -> p h t", h=4)
            P.op("vector", lambda e: e.tensor_tensor(out=ksh, in0=pv, in1=scale_bc, op=ALU.mult),
                 reads=[bankb[psum_i], buf("ab")], writes=[buf("ks")], n=512)
            P.op("vector", lambda e: e.tensor_tensor(out=a4, in0=ks4, in1=cosb, op=ALU.mult),
                 reads=[buf("ks"), csbuf], writes=[buf("rA")], n=512)
            P.op("vector", lambda e: e.tensor_tensor(out=b4, in0=ks4[:, :, ::-1, :], in1=sinb, op=ALU.mult),
                 reads=[buf("ks"), csbuf], writes=[buf("rB")], n=512)
            P.op("vector", lambda e: e.tensor_tensor(out=d4[:, :, 0, :], in0=a4[:, :, 0, :], in1=b4[:, :, 0, :],
                                                     op=ALU.subtract),
                 reads=[buf("rA"), buf("rB")], writes=[dstbuf], n=256)
            P.op("gpsimd", lambda e: e.tensor_tensor(out=d4[:, :, 1, :], in0=a4[:, :, 1, :], in1=b4[:, :, 1, :],
                                                     op=ALU.add),
                 reads=[buf("rA"), buf("rB"), dstbuf], writes=[dstbuf], n=256)

        a_bc = ab[:, 0:4].unsqueeze(2).broadcast_to([128, 4, 128])
        b_bc = ab[:, 4:8].unsqueeze(2).broadcast_to([128, 4, 128])

        def p1_A(kind, pset, idx):
            bT = 4 * pset
            sl = cnt["slot"] % 4
            cnt["slot"] += 1
            xb = buf(f"xt{sl}")
            xs_t = xt_p1[sl]
            xs = xs_t if sl == 3 else xs_t[:]
            xsel = cnt["slot"] % 2
            xnb = buf(f"xn{xsel}")
            xna = xn[xsel]
            ctx = {"kind": kind, "pset": pset}
            if kind in ("M", "H"):
                P.op("vector", lambda e: e.memset(xs[:], 0.0), writes=[xb], c=0.6)
                src = meta_d if kind == "M" else xprev_d
                P.dma("sync", f"xl{sl}", xs[112:128, :], src[:, :], reads=[xb], writes=[xb])
            else:
                xop = P.dma("sync", f"xl{sl}", xs[:], x_d[kind * 128:(kind + 1) * 128, :], writes=[xb],
                            extra=([cnt["x0op"]] if kind in (2, 3) else []), nbytes=524288)
                if kind == 0:
                    cnt["x0op"] = xop
                load_weight_piece(xop)
            if kind != "H":
                csl = cnt["cs"] % 4
                cnt["cs"] += 1
                csb = buf(f"cs{csl}")
                ti = 0 if kind == "M" else kind + 1
                P.dma("sync", f"csl{csl}", cst[csl][:], cs_v[:, ti, :], writes=[csb])
                ctx["csl"], ctx["csb"] = csl, csb
            rs, rsb = rms_rstd(xs[:], xb, idx)
            P.op("vector", lambda e: e.scalar_tensor_tensor(out=xna, in0=xs[:], scalar=rs, in1=gbc[:],
                                                            op0=ALU.mult, op1=ALU.mult),
                 reads=[xb, rsb, buf("gbc")], writes=[xnb], n=1024)
            pT = bf_view(bT)
            P.op("tensor", [lambda e, kc=kc: e.transpose(out=pT[:, kc, :], in_=xna[:, kc * 128:(kc + 1) * 128],
                                                          identity=identb[:]) for kc in range(8)],
                 reads=[xnb, buf("identb")], writes=[bankb[bT]], c=8 * TR)
            if kind == "H":
                P.op("scalar", lambda e: e.activation(out=hnT[:, :, 0:16], in_=pT[:, :, 112:128], func=AF.Copy),
                     reads=[bankb[bT]], writes=[buf("hnT_h")], n=128)
                return None
            if kind == "M":
                hsrc, hb = hnT_m[:], buf("qkT")
            else:
                hsrc, hb = hnT[:, :, 16 + kind * 128:16 + (kind + 1) * 128], buf(f"hnT{kind}")
            P.op("scalar", lambda e: e.activation(out=hsrc, in_=pT, func=AF.Copy), reads=[bankb[bT]], writes=[hb],
                 n=1024)
            ctx["hsrc"], ctx["hb"] = hsrc, hb
            return ctx

        def p1_B(ctx):
            kind, pset = ctx["kind"], ctx["pset"]
            bK, bV, bU = 4 * pset + 1, 4 * pset + 2, 4 * pset + 3
            hsrc, hb, csl, csb = ctx["hsrc"], ctx["hb"], ctx["csl"], ctx["csb"]
            if kind == "M":
                kdst, kb = ktok_m[:], buf("qrot")
                vdst, vb = vtok_m, buf("sT")
            else:
                kdst, kb = ktok[:, kind, :], buf(f"ktok{kind}")
                vdst, vb = vtok[:, kind, :], buf(f"vtok{kind}")
            P.op("tensor", [lambda e, kc=kc: e.matmul(bank[bK][:], lhsT=hsrc[:, kc, :], rhs=wkv[:, kc, 0:512],
                                                       start=(kc == 0), stop=(kc == 7)) for kc in range(8)],
                 reads=[hb, buf("wk")], writes=[bankb[bK]], c=8 * MM512)
            P.op("tensor", [lambda e, kc=kc: e.matmul(bank[bV][:], lhsT=hsrc[:, kc, :], rhs=wkv[:, kc, 512:1024],
                                                       start=(kc == 0), stop=(kc == 7)) for kc in range(8)],
                 reads=[hb, buf("wv")], writes=[bankb[bV]], c=8 * MM512)
            rotary(bK, b_bc, csl, csb, kdst, kb)
            P.op("scalar", lambda e: e.activation(out=vdst, in_=bank[bV][:], func=AF.Copy),
                 reads=[bankb[bV]], writes=[vb], n=512)
            pU = h_view(bU)
            P.op("tensor", [lambda e, h=h: e.matmul(pU[:, h, :], lhsT=kdst[:, h * 128:(h + 1) * 128],
                                                     rhs=vdst[:, h * 128:(h + 1) * 128], start=True, stop=True)
                            for h in range(4)],
                 reads=[kb, vb], writes=[bankb[bU]], c=4 * MM128)
            if kind == "M":
                P.op("vector", lambda e: e.tensor_copy(out=U0[:], in_=pU), reads=[bankb[bU]], writes=[buf("U0")], n=512)
            elif kind == 0:
                P.op("vector", lambda e: e.tensor_copy(out=Rst[:], in_=pU), reads=[bankb[bU]], writes=[buf("R")], n=512)
            else:
                P.op("vector", [lambda e, h=h: e.scalar_tensor_tensor(out=Rst[:, h, :], in0=Rst[:, h, :],
                                                                       scalar=float(CD[h]), in1=pU[:, h, :],
                                                                       op0=ALU.mult, op1=ALU.add)
                                for h in range(4)],
                     reads=[bankb[bU], buf("R")], writes=[buf("R")], c=0.9)

        order = ["H"] + list(range(NT)) + ["M"]
        pend = []
        for i, kind in enumerate(order):
            ctx = p1_A(kind, i % 2, i)
            if ctx is not None:
                pend.append(ctx)
            if i >= 2 and pend:
                p1_B(pend.pop(0))
        while pend:
            p1_B(pend.pop(0))
        assert not wpieces

        def alias_after(dst, srcs):
            for sname in srcs:
                sb_ = buf(sname)
                dst.r = dst.r + list(sb_.r) + ([sb_.w] if sb_.w is not None else [])
        for nm in ("AG0", "AG1", "AG2", "AG3", "sgr0", "sgr1"):
            alias_after(buf(nm), ["wk", "wv"])
        alias_after(buf("mixr"), ["xn0", "xn1"])
        alias_after(buf("mixc1"), ["xt3"])
        P.dma("sync", "aloc", aloc_d[:, :], Rst[:].rearrange("p h t -> p (h t)"), reads=[buf("R")],
              writes=[buf("aloc")], nbytes=262144)
        cc_sem = st.enter_context(nc.semaphore("cc_sem"))
        P.raw("gpsimd", lambda e: e.collective_compute("AllGather", ALU.bypass,
                                                         replica_groups=[[0, 1, 2, 3], [4, 5, 6, 7]],
                                                         ins=[aloc_d[:, :]], outs=[aall_d[:, :]]),
              cc_sem, reads=[buf("aloc")], writes=[buf("aall")])
        P.dma("sync", "c_g", gbc[:], gf_d[0:1, :].broadcast_to([128, D]), writes=[buf("gbc")], nbytes=524288)
        for r in range(4):
            P.dma("sync", f"agl{r}", AG[:, r, :], aall_d[r * 128:(r + 1) * 128, :], reads=[buf("aall")],
                  writes=[buf(f"AG{r}")], nbytes=1 << 18)
        P.op("vector", [lambda e, h=h: e.tensor_scalar(out=Rst[:, h, :], in0=U0[:, h, :], scalar1=coef[:, h:h + 1],
                                                        scalar2=None, op0=ALU.mult) for h in range(4)],
             reads=[buf("U0"), buf("coef")], writes=[buf("R")], c=0.9)
        AGh = AG.rearrange("p r (h t) -> p r h t", h=4)
        for r in range(4):
            P.op("vector", [lambda e, r=r, h=h: e.scalar_tensor_tensor(out=Rst[:, h, :], in0=AGh[:, r, h, :],
                                                                        scalar=coef[:, 4 + 4 * r + h:5 + 4 * r + h],
                                                                        in1=Rst[:, h, :], op0=ALU.mult, op1=ALU.add)
                            for h in range(4)],
                 reads=[buf(f"AG{r}"), buf("coef"), buf("R")], writes=[buf("R")], c=0.9)
        Sb = [buf("S0"), buf("S1")]

        def make_S(dst_i):
            P.op("scalar", [lambda e, h=h: e.activation(out=Sbf[dst_i][:, h, :], in_=Rst[:, h, :], func=AF.Copy,
                                                         scale=float(CD[h])) for h in range(4)],
                 reads=[buf("R")], writes=[Sb[dst_i]], c=1.3)
        make_S(0)

        pA, pB_, X1, X2, O0, O1, pY, pZ = range(8)

        def proj_fm(bi, wbuf, rcol, tok0, n):
            ti = (tok0 - 16) // 128
            rd = list(wbuf) + [buf(f"hnT{t}") for t in range(ti, min(NT, ti + (n + 127) // 128))]
            P.op("tensor", [lambda e, kc=kc: e.matmul(bank[bi][:, 0:n], lhsT=wrest[:, kc, rcol:rcol + 128],
                                                       rhs=hnT[:, kc, tok0:tok0 + n], start=(kc == 0), stop=(kc == 7))
                            for kc in range(8)],
                 reads=rd, writes=[bankb[bi]], c=8 * (0.03 + (MM512 - 0.03) * n / 512.0))

        for cb_i in range(4):
            P.op("tensor", [lambda e, kc=kc, cb_i=cb_i: e.matmul(bank[pA][:, 4 * cb_i:4 * cb_i + 2],
                                                                  lhsT=wrest[:, kc, R_CC + cb_i * 128:R_CC + (cb_i + 1) * 128],
                                                                  rhs=hnT[:, kc, 14:16], start=(kc == 0), stop=(kc == 7))
                            for kc in range(8)] +
                           [lambda e, kc=kc, cb_i=cb_i: e.matmul(bank[pA][:, 4 * cb_i + 2:4 * cb_i + 4],
                                                                  lhsT=wrest[:, kc, R_CX + cb_i * 128:R_CX + (cb_i + 1) * 128],
                                                                  rhs=hnT[:, kc, 14:16], start=(kc == 0), stop=(kc == 7))
                            for kc in range(8)],
                 reads=wb("wcc") + wb("wcx") + [buf("hnT_h")], writes=[bankb[pA]], c=16 * 0.07)
        P.op("scalar", lambda e: e.activation(out=uh_sb[:], in_=bank[pA][:, 0:16], func=AF.Copy),
             reads=[bankb[pA]], writes=[buf("uh")], n=16)
        uh4 = uh_sb[:].rearrange("p (c k t) -> p c k t", c=4, k=2)
        P.op("gpsimd", lambda e: e.tensor_tensor(out=uhalo[:], in0=uh4[:, :, 0, :], in1=uh4[:, :, 1, :], op=ALU.mult),
             reads=[buf("uh")], writes=[buf("uhalo")], n=8)

        ucnt = {"n": 0}

        GROUPS = [(0, 4), (4, 4), (8, 4), (12, 2), (14, 2)]
        TILE_G = {}
        for gi, (t0g, ntg) in enumerate(GROUPS):
            for j in range(ntg):
                TILE_G[t0g + j] = (gi, j)

        def conv_unit(s, cb_i):
            tok0 = 16 + 128 * GROUPS[s][0]
            nt_ = 128 * GROUPS[s][1]
            k = ucnt["n"] % 2
            ucnt["n"] += 1
            u, uB = uw[k], buf(f"uw{k}")
            c1k, c1B = c1[k], buf(f"c1_{k}")
            tbk, tbB = tb[k], buf(f"tb_{k}")
            mx, mxB = mixc[s % 2], buf(f"mixc{s % 2}")
            cw = lambda j: convw[:, 3 * cb_i + j:3 * cb_i + j + 1]
            bA, bB = (pY, pZ) if (s == 0 and cb_i % 2 == 1) else (pA, pB_)
            proj_fm(bA, wb("wcc"), R_CC + cb_i * 128, tok0, nt_)
            P.op("scalar", lambda e: e.activation(out=c1k[:, 0:nt_], in_=bank[bA][:, 0:nt_], func=AF.Copy),
                 reads=[bankb[bA]], writes=[c1B], n=nt_)
            proj_fm(bB, wb("wcx"), R_CX + cb_i * 128, tok0, nt_)
            P.op("gpsimd", lambda e: e.tensor_copy(out=u[:, 0:2], in_=uhalo[:, cb_i, :]),
                 reads=[buf("uhalo"), uB], writes=[uB], c=0.2)
            P.op("vector", lambda e: e.tensor_tensor(out=u[:, 2:2 + nt_], in0=bank[bB][:, 0:nt_], in1=c1k[:, 0:nt_], op=ALU.mult),
                 reads=[bankb[bB], c1B, uB], writes=[uB], n=nt_)
            P.op("gpsimd", lambda e: e.tensor_copy(out=uhalo[:, cb_i, :], in_=u[:, nt_:nt_ + 2]),
                 reads=[uB, buf("uhalo")], writes=[buf("uhalo")], c=0.2)
            P.op("vector", lambda e: e.tensor_scalar(out=tbk[:, 0:nt_], in0=u[:, 0:nt_], scalar1=cw(0), scalar2=None,
                                                     op0=ALU.mult),
                 reads=[uB, buf("convw")], writes=[tbB], n=nt_)
            P.op("vector", lambda e: e.scalar_tensor_tensor(out=tbk[:, 0:nt_], in0=u[:, 1:1 + nt_], scalar=cw(1), in1=tbk[:, 0:nt_],
                                                            op0=ALU.mult, op1=ALU.add),
                 reads=[uB, buf("convw"), tbB], writes=[tbB], n=nt_)
            P.op("vector", lambda e: e.scalar_tensor_tensor(out=tbk[:, 0:nt_], in0=u[:, 2:2 + nt_], scalar=cw(2), in1=tbk[:, 0:nt_],
                                                            op0=ALU.mult, op1=ALU.add),
                 reads=[uB, buf("convw"), tbB], writes=[tbB], n=nt_)
            proj_fm(bA, wb("wcb"), R_CB + cb_i * 128, tok0, nt_)
            P.op("vector", lambda e: e.tensor_tensor(out=tbk[:, 0:nt_], in0=bank[bA][:, 0:nt_], in1=tbk[:, 0:nt_], op=ALU.mult),
                 reads=[bankb[bA], tbB], writes=[tbB], n=nt_)
            proj_fm(bB, wb("wcg"), R_CG + cb_i * 128, tok0, nt_)
            P.op("scalar", lambda e: e.activation(out=c1k[:, 0:nt_], in_=bank[bB][:, 0:nt_], func=AF.Silu),
                 reads=[bankb[bB], c1B], writes=[c1B], n=nt_)
            P.op("vector", lambda e: e.tensor_tensor(out=mx[:, cb_i, 0:nt_], in0=tbk[:, 0:nt_], in1=c1k[:, 0:nt_], op=ALU.mult),
                 reads=[tbB, c1B, mxB], writes=[mxB], n=nt_)

        def rg_unit(s, h):
            tok0 = 16 + 128 * GROUPS[s][0]
            nt_ = 128 * GROUPS[s][1]
            bk = pA if h % 2 == 0 else pB_
            proj_fm(bk, wb("wrg"), R_RG + h * 128, tok0, nt_)
            P.op("scalar", lambda e: e.activation(out=sgr[s % 2][:, h, 0:nt_], in_=bank[bk][:, 0:nt_], func=AF.Silu),
                 reads=[bankb[bk], buf(f"sgr{s % 2}")], writes=[buf(f"sgr{s % 2}")], n=nt_)

        p2 = {"slot": 0, "cs": 0}
        xinfo = {}

        def issue_reload(t):
            sl = p2["slot"] % 3
            p2["slot"] += 1
            xinfo[t] = sl
            P.dma("sync", f"xl{sl}", xt[sl][:], x_d[t * 128:(t + 1) * 128, :],
                  writes=[buf(f"xt{sl}"), buf(f"xt{sl}_h0"), buf(f"xt{sl}_h1")], nbytes=524288)
            csl = p2["cs"] % 4
            p2["cs"] += 1
            xinfo[("cs", t)] = csl
            P.dma("sync", f"csl{csl}", cst[csl][:], cs_v[:, t + 1, :], writes=[buf(f"cs{csl}")])

        out_ops = []

        def ret_A(t, cur):
            csl = xinfo[("cs", t)]
            csb = buf(f"cs{csl}")
            bO = O0 + (t % 2)
            hT = hnT[:, :, 16 + t * 128:16 + (t + 1) * 128]
            P.op("tensor", [lambda e, kc=kc: e.matmul(bank[X1][:], lhsT=hT[:, kc, :], rhs=wrest[:, kc, R_Q:R_Q + 512],
                                                       start=(kc == 0), stop=(kc == 7)) for kc in range(8)],
                 reads=[buf(f"hnT{t}")] + wb("wq"), writes=[bankb[X1]], c=8 * MM512)
            rotary(X1, a_bc, csl, csb, qrot[:], buf("qrot"))
            pT = bf_view(X1)
            P.op("tensor", [lambda e, h=h: e.transpose(out=pT[:, h, :], in_=qrot[:, h * 128:(h + 1) * 128],
                                                        identity=identb[:]) for h in range(4)] +
                           [lambda e, h=h: e.transpose(out=pT[:, 4 + h, :], in_=ktok[:, t, h * 128:(h + 1) * 128],
                                                        identity=identb[:]) for h in range(4)],
                 reads=[buf("qrot"), buf(f"ktok{t}"), buf("identb")], writes=[bankb[X1]], c=8 * TR)
            P.op("scalar", lambda e: e.activation(out=qkT[:], in_=pT, func=AF.Copy),
                 reads=[bankb[X1]], writes=[buf("qkT")], n=1024)
            pSv = h_view(X2)
            P.op("tensor", [lambda e, h=h: e.matmul(pSv[:, h, :], lhsT=qkT[:, 4 + h, :], rhs=qkT[:, h, :],
                                                     start=True, stop=True) for h in range(4)],
                 reads=[buf("qkT")], writes=[bankb[X2]], c=4 * MM128)
            mask_bc = maskT[:].unsqueeze(1).broadcast_to([128, 4, 128])
            P.op("vector", lambda e: e.tensor_tensor(out=sT[:], in0=pSv, in1=mask_bc, op=ALU.mult),
                 reads=[bankb[X2], buf("maskT")], writes=[buf("sT")], n=512)
            pOv = h_view(bO)
            fl = []
            for h in range(4):
                fl.append(lambda e, h=h: e.matmul(pOv[:, h, :], lhsT=sT[:, h, :], rhs=vtok[:, t, h * 128:(h + 1) * 128],
                                                   start=True, stop=False))
                fl.append(lambda e, h=h: e.matmul(pOv[:, h, :], lhsT=qkT[:, h, :], rhs=Sbf[cur][:, h, :],
                                                   start=False, stop=True))
            P.op("tensor", fl, reads=[buf("sT"), buf(f"vtok{t}"), buf("qkT"), Sb[cur]], writes=[bankb[bO]],
                 c=8 * MM128)
            if t < NT - 1:
                pUv = h_view(X2)
                P.op("tensor", [lambda e, h=h: e.matmul(pUv[:, h, :], lhsT=ktok[:, t, h * 128:(h + 1) * 128],
                                                         rhs=vtok[:, t, h * 128:(h + 1) * 128], start=True, stop=True)
                                for h in range(4)],
                     reads=[buf(f"ktok{t}"), buf(f"vtok{t}")], writes=[bankb[X2]], c=4 * MM128)
                P.op("vector", [lambda e, h=h: e.scalar_tensor_tensor(out=Rst[:, h, :], in0=Rst[:, h, :],
                                                                       scalar=float(CD[h]), in1=pUv[:, h, :],
                                                                       op0=ALU.mult, op1=ALU.add) for h in range(4)],
                     reads=[bankb[X2], buf("R")], writes=[buf("R")], c=0.9)
                make_S(1 - cur)

        def ret_B(t):
            s, i = TILE_G[t]
            sl = xinfo[t]
            xb = buf(f"xt{sl}")
            xs = xt[sl]
            par = t % 2
            bO = O0 + par
            pOv = h_view(bO)
            sg = sgr[s % 2]
            sgB = buf(f"sgr{s % 2}")
            mx, mxB = mixc[s % 2], buf(f"mixc{s % 2}")
            stB = buf(f"gn{par}")
            P.op("vector", [lambda e, h=h: e.bn_stats(out=bns[:, par, h, :], in_=pOv[:, h, :]) for h in range(4)],
                 reads=[bankb[bO]], writes=[buf(f"bns{par}")], c=0.9)
            P.op("vector", [lambda e, h=h: e.bn_aggr(out=mv[:, par, h, :], in_=bns[:, par, h, :]) for h in range(4)],
                 reads=[buf(f"bns{par}")], writes=[buf(f"mv{par}")], c=0.4)
            P.op("vector", lambda e: e.tensor_scalar(out=gve[:, par, :], in0=mv[:, par, :, 1], scalar1=EPS,
                                                     scalar2=None, op0=ALU.add),
                 reads=[buf(f"mv{par}")], writes=[stB], n=4)
            P.op("gpsimd", lambda e: e.tensor_tensor(out=grs[:, par, :], in0=gve[:, par, :], in1=mhalf[:], op=ALU.pow),
                 reads=[stB, buf("mhalf")], writes=[buf(f"grs{par}")], c=1.0)
            P.op("vector", lambda e: e.scalar_tensor_tensor(out=gnm[:, par, :], in0=mv[:, par, :, 0], scalar=-1.0,
                                                            in1=grs[:, par, :], op0=ALU.mult, op1=ALU.mult),
                 reads=[buf(f"mv{par}"), buf(f"grs{par}")], writes=[buf(f"gnm{par}")], n=4)
            P.op("scalar", [lambda e, h=h: e.activation(out=ybuf[:, h, :], in_=pOv[:, h, :], func=AF.Identity,
                                                         bias=gnm[:, par, h:h + 1], scale=grs[:, par, h:h + 1])
                            for h in range(4)],
                 reads=[bankb[bO], buf(f"grs{par}"), buf(f"gnm{par}")], writes=[buf("ybuf")], c=1.5)
            for half in range(2):
                bo = pY if half == 0 else pZ
                P.op("tensor", [lambda e, ec=ec, half=half, bo=bo: e.matmul(bank[bo][:], lhsT=mx[:, ec, i * 128:(i + 1) * 128],
                                                                              rhs=wout[:, ec, half * 512:(half + 1) * 512],
                                                                              start=(ec == 0), stop=False)
                                for ec in range(4)],
                     reads=[mxB] + wb("wout0") + wb("wout1"), writes=[bankb[bo]], c=4 * MM512)
            pT = bf_view(bO)
            P.op("tensor", [lambda e, h=h: e.transpose(out=pT[:, h, :], in_=ybuf[:, h, :], identity=identb[:])
                            for h in range(4)],
                 reads=[buf("ybuf"), buf("identb")], writes=[bankb[bO]], c=4 * TR)
            P.op("vector", [lambda e, h=h: e.scalar_tensor_tensor(out=mixr[:, h, i * 128:(i + 1) * 128], in0=pT[:, h, :],
                                                                   scalar=gret[:, h:h + 1],
                                                                   in1=sg[:, h, i * 128:(i + 1) * 128],
                                                                   op0=ALU.mult, op1=ALU.mult) for h in range(4)],
                 reads=[bankb[bO], buf("gret"), sgB, buf("mixr")], writes=[buf("mixr")], c=0.9)
            for half in range(2):
                bo = pY if half == 0 else pZ
                P.op("tensor", [lambda e, ec=ec, half=half, bo=bo: e.matmul(bank[bo][:], lhsT=mixr[:, ec - 4, i * 128:(i + 1) * 128],
                                                                              rhs=wout[:, ec, half * 512:(half + 1) * 512],
                                                                              start=False, stop=(ec == 7))
                                for ec in range(4, 8)],
                     reads=[buf("mixr")] + wb("wout0") + wb("wout1"), writes=[bankb[bo]], c=4 * MM512)
                P.op("vector", lambda e, half=half, bo=bo: e.tensor_tensor(out=xs[:, half * 512:(half + 1) * 512],
                                                                           in0=bank[bo][:],
                                                                           in1=xs[:, half * 512:(half + 1) * 512], op=ALU.add),
                     reads=[bankb[bo], xb], writes=[xb], n=512)
            rs, rsb = rms_rstd(xs[:], xb, 32 + t)
            for half in range(2):
                cs_ = slice(half * 512, (half + 1) * 512)
                hb = buf(f"xt{sl}_h{half}")
                P.op("vector", lambda e, cs_=cs_: e.scalar_tensor_tensor(out=xs[:, cs_], in0=xs[:, cs_], scalar=rs,
                                                                          in1=gbc[:, cs_], op0=ALU.mult, op1=ALU.mult),
                     reads=[xb, rsb, buf("gbc")], writes=[hb], n=512)
                out_ops.append(P.dma("sync", f"st{sl}_{half}", y_d[t * 128:(t + 1) * 128, cs_], xs[:, cs_], reads=[hb],
                                     nbytes=262144))

        issue_reload(0)
        issue_reload(1)
        for cb_i in range(4):
            conv_unit(0, cb_i)
        for h in range(4):
            rg_unit(0, h)
        cur = 0
        for gi, (t0g, ntg) in enumerate(GROUPS):
            nxt = gi + 1 if gi + 1 < len(GROUPS) else None
            units = ([("c", cb) for cb in range(4)] + [("r", h) for h in range(4)]) if nxt is not None else []
            per = (len(units) + 2 * ntg - 1) // (2 * ntg) if units else 0
            for j in range(ntg):
                t = t0g + j
                if t + 2 < NT:
                    issue_reload(t + 2)
                ret_A(t, cur)
                cur = 1 - cur
                for _ in range(per):
                    if units:
                        kind_u, a_u = units.pop(0)
                        (conv_unit if kind_u == "c" else rg_unit)(nxt, a_u)
                ret_B(t)
                for _ in range(per):
                    if units:
                        kind_u, a_u = units.pop(0)
                        (conv_unit if kind_u == "c" else rg_unit)(nxt, a_u)
            assert not units
        P.final = ("sync", out_ops)
        P.run()
    return nc


def _host_constants():
    h = np.arange(NH)
    gam = (1.0 - 2.0 ** (-5.0 - h)).astype(np.float64)
    idx = np.arange(128, dtype=np.float64)
    a = (128.0 ** -0.5) * gam[None, :] ** (idx[:, None] + 1.0)
    b = gam[None, :] ** (-(idx[:, None] + 1.0))
    ab = np.concatenate([a, b], axis=1).astype(np.float32)
    maskT = (idx[None, :] >= idx[:, None]).astype(np.float32)
    ident = np.eye(128, dtype=np.float32)
    half = 64
    freqs = (1.0 / (np.float32(10000.0) ** (np.arange(half, dtype=np.float32) / np.float32(half)))).astype(np.float32)
    cd = gam ** 128
    return gam, cd, ab, maskT, ident, freqs


def _cs_table(freqs, positions):
    ang = (positions.astype(np.float32)[:, None] * freqs[None, :]).astype(np.float32)
    return np.concatenate([np.cos(ang), np.sin(ang)], axis=1).astype(np.float32)


def _cs_table_all():
    L = NMETA + SEQ
    try:
        import jax
        import jax.numpy as jnp
        with jax.default_device(jax.devices("cpu")[0]):
            half = 64
            fr = 1.0 / (10000.0 ** (jnp.arange(half, dtype=jnp.float32) / half))
            ang = jnp.arange(L, dtype=jnp.int32).astype(jnp.float32)[:, None] * fr[None, :]
            tab = np.concatenate([np.asarray(jnp.cos(ang)), np.asarray(jnp.sin(ang))], axis=1)
        return np.ascontiguousarray(tab.astype(np.float32))
    except Exception:
        freqs = (1.0 / (np.float32(10000.0) ** (np.arange(64, dtype=np.float32) / np.float32(64)))).astype(np.float32)
        return _cs_table(freqs, np.arange(L))


_NC_CACHE = {}


def kernel(x, meta, norm1_g, w_in, conv_w, ret_norm_g, w_out, final_g):
    x = np.ascontiguousarray(np.asarray(x, dtype=np.float32))
    meta = np.ascontiguousarray(np.asarray(meta, dtype=np.float32))
    w_in = np.ascontiguousarray(np.asarray(w_in, dtype=np.float32))
    w_out = np.ascontiguousarray(np.asarray(w_out, dtype=np.float32))
    norm1_g = np.asarray(norm1_g, dtype=np.float32).reshape(1, D)
    final_g = np.asarray(final_g, dtype=np.float32).reshape(1, D)
    conv_w = np.asarray(conv_w, dtype=np.float32)
    ret_norm_g = np.asarray(ret_norm_g, dtype=np.float32)

    gam, cd, ab, maskT, ident, freqs = _host_constants()
    convw_l = np.ascontiguousarray(conv_w.reshape(3, 4, 128).transpose(2, 1, 0).reshape(128, 12))
    gret_l = np.ascontiguousarray(ret_norm_g.reshape(4, 128).T)

    if "nc" not in _NC_CACHE:
        _NC_CACHE["nc"] = build_nc()
    nc = _NC_CACHE["nc"]

    cs_all = _cs_table_all()
    in_maps = []
    for core in range(NCORE):
        b, c = divmod(core, 4)
        xs = x[b, c * TOK:(c + 1) * TOK, :]
        xprev = meta if c == 0 else x[b, c * TOK - 16:c * TOK, :]
        cs = np.zeros((17 * 128, 128), np.float32)
        cs[112:128] = cs_all[0:NMETA]
        cs[128:] = cs_all[NMETA + c * TOK:NMETA + (c + 1) * TOK]
        coef = np.zeros((1, 20), np.float64)
        coef[0, 0:4] = cd ** (16 * c)
        for r in range(4):
            if r < c:
                coef[0, 4 + 4 * r:8 + 4 * r] = cd ** (16 * (c - 1 - r))
        in_maps.append({
            "x": np.ascontiguousarray(xs), "xprev": np.ascontiguousarray(xprev), "meta": meta,
            "w_in": w_in, "w_out": w_out, "norm1_g": norm1_g, "final_g": final_g,
            "convw": convw_l, "gret": gret_l, "cs": cs, "ab": ab, "maskT": maskT, "ident": ident,
            "coef": coef.astype(np.float32),
        })
    res = run_bass_kernel_spmd(nc, in_maps, core_ids=list(range(NCORE)))
    out = np.empty((BATCH, SEQ, D), np.float32)
    for core in range(NCORE):
        b, c = divmod(core, 4)
        out[b, c * TOK:(c + 1) * TOK, :] = res.results[core]["y"]
    return out
```

```python
import numpy as np
from contextlib import ExitStack
import concourse.bass as bass
import concourse.mybir as mybir
from concourse.bass_utils import run_bass_kernel_spmd

F32 = mybir.dt.float32
BF16 = mybir.dt.bfloat16
ALU = mybir.AluOpType
AF = mybir.ActivationFunctionType

D = 1024
SEQ = 8192
BATCH = 2
NMETA = 16
NCORE = 8
TOK = 2048
NT = 16
NH = 4
EPS = 1e-6
ENGS = ["sync", "scalar", "vector", "gpsimd", "tensor"]

GAMMA = [1.0 - 2.0 ** (-5.0 - h) for h in range(NH)]
CD = [g ** 128 for g in GAMMA]

C_CX, C_CB, C_CC, C_CG, C_Q, C_K, C_V, C_RG = [i * 512 for i in range(8)]
R_CX, R_CB, R_CC, R_CG, R_Q, R_RG = 0, 512, 1024, 1536, 2048, 2560


class Buf:
    def __init__(self, name, excl=False):
        self.name = name
        self.w = None
        self.r = []
        self.excl = excl


def _cost(eng, n):
    if eng == "vector":
        return 0.12 + n / 900.0
    if eng == "scalar":
        return 0.22 + n / 1200.0
    if eng == "gpsimd":
        return 0.35 + n / 330.0
    return 0.25


SEM_LAT = 0.12
import os
NOSCHED = bool(int(os.environ.get("KERNEL_NOSCHED", "0")))
SCHED_CP = bool(int(os.environ.get("KERNEL_CP", "1")))
SCHED_W = int(os.environ.get("KERNEL_W", "96"))
DMA_BPUS = float(os.environ.get("KERNEL_DMA_BPUS", "150e3"))


class Prog:
    def __init__(self, nc, stack):
        self.nc = nc
        self.stack = stack
        self.ops = []
        self.esem = {e: stack.enter_context(nc.semaphore("es_" + e)) for e in ENGS if e != "sync"}
        self.dsem = {}

    def sb(self, name, shape, dt):
        return self.stack.enter_context(self.nc.sbuf_tensor(name, shape, dt))

    def ps(self, name, shape, dt):
        return self.stack.enter_context(self.nc.psum_tensor(name, shape, dt))

    def _add(self, op, reads, writes, extra):
        idx = len(self.ops)
        deps = {}
        for b in reads:
            if b.w is not None:
                deps[b.w] = "raw"
            if b.excl:
                for r in b.r:
                    deps.setdefault(r, "war")
        for b in writes:
            if b.w is not None:
                deps[b.w] = "raw"
            for r in b.r:
                deps.setdefault(r, "war")
        for x in extra:
            if x is not None:
                deps[x] = "raw"
        deps.pop(idx, None)
        op["deps"] = deps
        op["idx"] = idx
        self.ops.append(op)
        for b in reads:
            if b not in writes:
                b.r.append(idx)
        for b in writes:
            b.w = idx
            b.r = []
        return idx

    def op(self, eng, fns, reads=(), writes=(), extra=(), n=512, c=None):
        if not isinstance(fns, (list, tuple)):
            fns = [fns]
        cost = c if c is not None else _cost(eng, n)
        return self._add({"eng": eng, "fns": list(fns), "kind": "op", "busy": cost, "lat": cost}, reads, writes, extra)

    def dma(self, eng, slot, out, in_, reads=(), writes=(), extra=(), nbytes=65536):
        busy = {"sync": 0.6, "scalar": 0.8}.get(eng, 5.0)
        lat = busy + 2.0 + nbytes / DMA_BPUS
        return self._add({"eng": eng, "kind": "dma", "slot": slot, "out": out, "in_": in_, "busy": busy, "lat": lat},
                         reads, writes, extra)

    def raw(self, eng, fn, sem, reads=(), writes=(), busy=1.0, lat=50.0):
        return self._add({"eng": eng, "kind": "raw", "fn": fn, "sem": sem, "busy": busy, "lat": lat},
                         reads, writes, ())

    def schedule(self):
        ops = self.ops
        N = len(ops)
        if NOSCHED:
            order = {e: [] for e in ENGS}
            for o in ops:
                order[o["eng"]].append(o["idx"])
            self.order = order
            self.makespan = 0.0
            return order
        succ = [[] for _ in range(N)]
        for o in ops:
            for d in o["deps"]:
                succ[d].append(o["idx"])
        blevel = [0.0] * N
        for i in range(N - 1, -1, -1):
            m = 0.0
            for j in succ[i]:
                if blevel[j] > m:
                    m = blevel[j]
            blevel[i] = ops[i]["lat"] + SEM_LAT + m
        pending = {e: [] for e in ENGS}
        for o in ops:
            pending[o["eng"]].append(o["idx"])
        fin = [None] * N
        free = {e: 0.0 for e in ENGS}
        order = {e: [] for e in ENGS}
        head = {e: 0 for e in ENGS}
        done = [False] * N
        W = SCHED_W
        nleft = N
        while nleft:
            best = None
            for e in ENGS:
                lst = pending[e]
                h = head[e]
                while h < len(lst) and done[lst[h]]:
                    h += 1
                head[e] = h
                cand = None
                cnt = 0
                for j in range(h, len(lst)):
                    i = lst[j]
                    if done[i]:
                        continue
                    cnt += 1
                    if cnt > W:
                        break
                    o = ops[i]
                    est = 0.0
                    ok = True
                    for d in o["deps"]:
                        f = fin[d]
                        if f is None:
                            ok = False
                            break
                        if f + SEM_LAT > est:
                            est = f + SEM_LAT
                    if not ok:
                        continue
                    st = est if est > free[e] else free[e]
                    key = (st, -blevel[i] if SCHED_CP else i, i)
                    if cand is None or key < cand:
                        cand = key
                if cand is not None and (best is None or (cand[0], cand[2]) < (best[0], best[1])):
                    best = (cand[0], cand[2], e)
            assert best is not None, "scheduler deadlock"
            st, i, e = best
            o = ops[i]
            fin[i] = st + o["lat"]
            free[e] = st + o["busy"]
            done[i] = True
            order[e].append(i)
            nleft -= 1
        self.order = order
        self.makespan = max(f for f in fin)
        return order

    def run(self):
        order = self.schedule()
        ops = self.ops
        tok = [None] * len(ops)
        dcnt = {}
        for e in ENGS:
            cnt = 0
            for i in order[e]:
                o = ops[i]
                if o["kind"] == "op":
                    cnt += 1
                    tok[i] = (self.esem[e], cnt, "e_" + e)
                elif o["kind"] == "dma":
                    slot = o["slot"]
                    if slot not in self.dsem:
                        self.dsem[slot] = self.stack.enter_context(self.nc.semaphore("ds_" + slot))
                        dcnt[slot] = 0
                    dcnt[slot] += 16
                    tok[i] = (self.dsem[slot], dcnt[slot], "d_" + slot)
                else:
                    tok[i] = (o["sem"], 1, "r_%d" % i)

        def emit_engine(e, eng_obj):
            waited = {}
            for i in order[e]:
                o = ops[i]
                need = {}
                for d, kind in o["deps"].items():
                    od = ops[d]
                    if e == "tensor" and od["eng"] == "tensor" and od["kind"] == "op":
                        continue
                    sem, val, key = tok[d]
                    if waited.get(key, 0) >= val:
                        continue
                    if need.get(key, (None, 0))[1] < val:
                        need[key] = (sem, val)
                for key, (sem, val) in need.items():
                    waited[key] = val
                    eng_obj.wait_ge(sem, val)
                if o["kind"] == "op":
                    for f in o["fns"][:-1]:
                        f(eng_obj)
                    o["fns"][-1](eng_obj).then_inc(tok[i][0], 1)
                elif o["kind"] == "dma":
                    eng_obj.dma_start(out=o["out"], in_=o["in_"]).then_inc(tok[i][0], 16)
                else:
                    o["fn"](eng_obj).then_inc(o["sem"])

        final = getattr(self, "final", None)

        def emit_final(e, eng_obj):
            if final is not None and final[0] == e:
                seen = {}
                for d in final[1]:
                    sem, val, key = tok[d]
                    if seen.get(key, (None, 0))[1] < val:
                        seen[key] = (sem, val)
                for key, (sem, val) in seen.items():
                    eng_obj.wait_ge(sem, val)

        with self.nc.Block() as block:
            @block.sync
            def _(e):
                emit_engine("sync", e)
                emit_final("sync", e)

            @block.scalar
            def _(e):
                emit_engine("scalar", e)

            @block.vector
            def _(e):
                emit_engine("vector", e)

            @block.gpsimd
            def _(e):
                emit_engine("gpsimd", e)

            @block.tensor
            def _(e):
                emit_engine("tensor", e)


def build_nc():
    nc = bass.Bass("TRN2", target_bir_lowering=False)
    dr = lambda name, shape, kind="ExternalInput": nc.dram_tensor(name, shape, F32, kind=kind).ap()
    x_d = dr("x", [TOK, D])
    xprev_d = dr("xprev", [16, D])
    meta_d = dr("meta", [16, D])
    win_d = dr("w_in", [D, 4096])
    wout_d = dr("w_out", [D, D])
    g1_d = dr("norm1_g", [1, D])
    gf_d = dr("final_g", [1, D])
    convw_d = dr("convw", [128, 12])
    gret_d = dr("gret", [128, 4])
    cs_d = dr("cs", [17 * 128, 128])
    ab_d = dr("ab", [128, 8])
    mask_d = dr("maskT", [128, 128])
    ident_d = dr("ident", [128, 128])
    coef_d = dr("coef", [1, 20])
    y_d = dr("y", [TOK, D], kind="ExternalOutput")
    aloc_d = dr("a_loc", [128, 512], kind="Internal")
    aall_d = dr("a_all", [4 * 128, 512], kind="Internal")

    win_v = win_d.rearrange("(k p) n -> p k n", p=128)
    wout_v = wout_d.rearrange("(k p) n -> p k n", p=128)
    cs_v = cs_d.rearrange("(t p) f -> p t f", p=128)

    with ExitStack() as st:
        P = Prog(nc, st)
        wkv = P.sb("wkv", [128, 8, 1024], BF16)
        wrest = P.sb("wrest", [128, 8, 3072], BF16)
        wout = P.sb("wout", [128, 8, 1024], BF16)
        hnT = P.sb("hnT", [128, 8, 16 + TOK], BF16)
        ktok = P.sb("ktok", [128, NT, 512], BF16)
        vtok = P.sb("vtok", [128, NT, 512], BF16)
        xt = [P.sb(f"xt{i}", [128, D], F32) for i in range(3)]
        xn_mixr = P.sb("xn_mixr", [128, 2048], BF16)
        junk = P.sb("junk", [128, D], BF16)
        gbc = P.sb("gbc", [128, D], F32)
        cst = [P.sb(f"cs{i}", [128, 128], F32) for i in range(4)]
        ks = P.sb("ks", [128, 512], F32)
        rA = P.sb("rA", [128, 512], F32)
        rB = P.sb("rB", [128, 512], F32)
        qrot = P.sb("qrot", [128, 512], BF16)
        qkT = P.sb("qkT", [128, 8, 128], BF16)
        sT = P.sb("sT", [128, 4, 128], BF16)
        ybuf = P.sb("ybuf", [128, 4, 128], BF16)
        c1 = [P.sb(f"c1_{i}", [128, 512], F32) for i in range(2)]
        tb = [P.sb(f"tb_{i}", [128, 512], F32) for i in range(2)]
        uw = [P.sb(f"uw{i}", [128, 514], F32) for i in range(2)]
        uhalo = P.sb("uhalo", [128, 4, 2], F32)
        mixc = [P.sb(f"mixc{i}", [128, 4, 512], BF16) for i in range(2)]
        Rst = P.sb("Rst", [128, 4, 128], F32)
        U0 = P.sb("U0", [128, 4, 128], F32)
        Sbf = [P.sb(f"Sbf{i}", [128, 4, 128], BF16) for i in range(2)]
        identb = P.sb("identb", [128, 128], BF16)
        maskT = P.sb("maskT_sb", [128, 128], F32)
        ab = P.sb("ab_sb", [128, 8], F32)
        coef = P.sb("coef_sb", [128, 20], F32)
        convw = P.sb("convw_sb", [128, 12], F32)
        gret = P.sb("gret_sb", [128, 4], F32)
        mhalf = P.sb("mhalf", [128, 4], F32)
        ssq = P.sb("ssq", [128, 64], F32)
        ms = P.sb("ms", [128, 64], F32)
        rstd = P.sb("rstd", [128, 64], F32)
        cdb = P.sb("cdb", [128, 4], F32)
        bns = P.sb("bns", [128, 2, 4, 6], F32)
        mv = P.sb("mv", [128, 2, 4, 2], F32)
        gve = P.sb("gve", [128, 2, 4], F32)
        grs = P.sb("grs", [128, 2, 4], F32)
        gnm = P.sb("gnm", [128, 2, 4], F32)
        uh_sb = P.sb("uh_sb", [128, 16], F32)

        identf = junk[:, 0:256].bitcast(F32)
        xt_p1 = xt + [mixc[1][:].rearrange("p a b -> p (a b)").bitcast(F32)]
        xn = [xn_mixr[:, 0:1024], xn_mixr[:, 1024:2048]]
        mixr = xn_mixr[:].rearrange("p (h t) -> p h t", h=4)
        wkv_flat = wkv[:].rearrange("p a b -> p (a b)")
        AG = wkv_flat[:, 0:4096].bitcast(F32).rearrange("p (r f) -> p r f", r=4)
        sgr = [wkv_flat[:, 4096:6144].rearrange("p (h t) -> p h t", h=4),
               wkv_flat[:, 6144:8192].rearrange("p (h t) -> p h t", h=4)]
        hnT_m = qkT
        ktok_m = qrot
        vtok_m = sT[:].rearrange("p a b -> p (a b)")

        bank = [P.ps(f"bank{i}", [128, 512], F32) for i in range(8)]
        bankb = [Buf(f"bank{i}", excl=True) for i in range(8)]

        def bf_view(i):
            return bank[i][:].bitcast(BF16).rearrange("p (k t) -> p k t", k=8)

        def h_view(i):
            return bank[i][:].rearrange("p (h t) -> p h t", h=4)

        B = {}

        def buf(name):
            if name not in B:
                B[name] = Buf(name)
            return B[name]

        MM512 = 0.27
        MM128 = 0.09
        TR = 0.12

        P.dma("sync", "c_ident", identf, ident_d[:, :], writes=[buf("junk")])
        P.dma("sync", "c_g", gbc[:], g1_d[0:1, :].broadcast_to([128, D]), writes=[buf("gbc")], nbytes=524288)
        P.dma("scalar", "c_ab", ab[:], ab_d[:, :], writes=[buf("ab")])
        P.dma("scalar", "c_mask", maskT[:], mask_d[:, :], writes=[buf("maskT")])
        P.dma("scalar", "c_coef", coef[:], coef_d[0:1, :].broadcast_to([128, 20]), writes=[buf("coef")])
        P.dma("scalar", "c_convw", convw[:], convw_d[:, :], writes=[buf("convw")])
        P.dma("scalar", "c_gret", gret[:], gret_d[:, :], writes=[buf("gret")])
        P.dma("gpsimd", "w_k", wkv[:, :, 0:512], win_v[:, :, C_K:C_K + 512], writes=[buf("wk")], nbytes=2 << 20)
        P.dma("gpsimd", "w_v", wkv[:, :, 512:1024], win_v[:, :, C_V:C_V + 512], writes=[buf("wv")], nbytes=2 << 20)
        P.op("gpsimd", lambda e: e.memset(mhalf[:], -0.5), writes=[buf("mhalf")], n=4)
        P.op("gpsimd", [lambda e, h=h: e.memset(cdb[:, h:h + 1], float(CD[h])) for h in range(4)],
             writes=[buf("cdb")], n=16)
        P.op("vector", lambda e: e.tensor_copy(out=identb[:], in_=identf), reads=[buf("junk")],
             writes=[buf("identb")], n=128)
        wpieces = []
        wparts = {}

        def add_piece(nm, dst, src, nb):
            key = nm + "_%d" % len(wparts.setdefault(nm, []))
            wparts[nm].append(buf(key))
            wpieces.append((key, dst, src, nb))

        for (nm, ro, co) in [("wcc", R_CC, C_CC), ("wcx", R_CX, C_CX), ("wcb", R_CB, C_CB), ("wcg", R_CG, C_CG)]:
            for qk in range(4):
                add_piece(nm, wrest[:, 2 * qk:2 * qk + 2, ro:ro + 512], win_v[:, 2 * qk:2 * qk + 2, co:co + 512], 1 << 19)
        n_paced = len(wpieces)
        for (nm, ro, co) in [("wrg", R_RG, C_RG), ("wq", R_Q, C_Q)]:
            for hk in range(2):
                add_piece(nm, wrest[:, 4 * hk:4 * hk + 4, ro:ro + 512], win_v[:, 4 * hk:4 * hk + 4, co:co + 512], 1 << 20)
        for half in range(2):
            for hk in range(2):
                add_piece(f"wout{half}", wout[:, 4 * hk:4 * hk + 4, half * 512:(half + 1) * 512],
                          wout_v[:, 4 * hk:4 * hk + 4, half * 512:(half + 1) * 512], 1 << 20)
        assert n_paced == NT

        def wb(nm):
            return list(wparts[nm])

        wstate = {"n": 0}

        def load_weight_piece(after_op):
            key, dst, src, nb = wpieces.pop(0)
            P.dma("gpsimd", "w_" + key, dst, src, writes=[buf(key)], extra=[after_op], nbytes=nb)
            wstate["n"] += 1
            if wstate["n"] == NT:
                while wpieces:
                    key, dst, src, nb = wpieces.pop(0)
                    P.dma("gpsimd", "w_" + key, dst, src, writes=[buf(key)], extra=[after_op], nbytes=nb)

        cnt = {"slot": 0, "cs": 0}

        def rms_rstd(src_ap, srcbuf, col, extra_reads=()):
            P.op("scalar", lambda e: e.activation(out=junk[:], in_=src_ap, func=AF.Square,
                                                  accum_out=ssq[:, col:col + 1]),
                 reads=[srcbuf] + list(extra_reads), writes=[buf(f"ssq{col}"), buf("junk")], n=1024)
            P.op("scalar", lambda e: e.activation(out=ms[:, col:col + 1], in_=ssq[:, col:col + 1], func=AF.Identity,
                                                  scale=1.0 / D, bias=EPS),
                 reads=[buf(f"ssq{col}")], writes=[buf(f"ms{col}")], n=1)
            P.op("gpsimd", lambda e: e.tensor_tensor(out=rstd[:, col:col + 1], in0=ms[:, col:col + 1],
                                                     in1=mhalf[:, 0:1], op=ALU.pow),
                 reads=[buf(f"ms{col}"), buf("mhalf")], writes=[buf(f"rstd{col}")], c=1.0)
            return rstd[:, col:col + 1], buf(f"rstd{col}")

        def rotary(psum_i, scale_bc, csl, csbuf, dst_ap, dstbuf):
            pv = h_view(psum_i)
            ks4 = ks[:].rearrange("p (h t f) -> p h t f", h=4, t=2)
            a4 = rA[:].rearrange("p (h t f) -> p h t f", h=4, t=2)
            b4 = rB[:].rearrange("p (h t f) -> p h t f", h=4, t=2)
            d4 = dst_ap.rearrange("p (h t f) -> p h t f", h=4, t=2)
            cosb = cst[csl][:, 0:64].unsqueeze(1).unsqueeze(1).broadcast_to([128, 4, 2, 64])
            sinb = cst[csl][:, 64:128].unsqueeze(1).unsqueeze(1).broadcast_to([128, 4, 2, 64])
            ksh = ks[:].rearrange("p (h t) -> p h t", h=4)
            P.op("vector", lambda e: e.tensor_tensor(out=ksh, in0=pv, in1=scale_bc, op=ALU.mult),
                 reads=[bankb[psum_i], buf("ab")], writes=[buf("ks")], n=512)
            P.op("vector", lambda e: e.tensor_tensor(out=a4, in0=ks4, in1=cosb, op=ALU.mult),
                 reads=[buf("ks"), csbuf], writes=[buf("rA")], n=512)
            P.op("vector", lambda e: e.tensor_tensor(out=b4, in0=ks4[:, :, ::-1, :], in1=sinb, op=ALU.mult),
                 reads=[buf("ks"), csbuf], writes=[buf("rB")], n=512)
            P.op("vector", lambda e: e.tensor_tensor(out=d4[:, :, 0, :], in0=a4[:, :, 0, :], in1=b4[:, :, 0, :],
                                                     op=ALU.subtract),
                 reads=[buf("rA"), buf("rB")], writes=[dstbuf], n=256)
            P.op("gpsimd", lambda e: e.tensor_tensor(out=d4[:, :, 1, :], in0=a4[:, :, 1, :], in1=b4[:, :, 1, :],
                                                     op=ALU.add),
                 reads=[buf("rA"), buf("rB"), dstbuf], writes=[dstbuf], n=256)

        a_bc = ab[:, 0:4].unsqueeze(2).broadcast_to([128, 4, 128])
        b_bc = ab[:, 4:8].unsqueeze(2).broadcast_to([128, 4, 128])

        def p1_A(kind, pset, idx):
            bT = 4 * pset
            sl = cnt["slot"] % 4
            cnt["slot"] += 1
            xb = buf(f"xt{sl}")
            xs_t = xt_p1[sl]
            xs = xs_t if sl == 3 else xs_t[:]
            xsel = cnt["slot"] % 2
            xnb = buf(f"xn{xsel}")
            xna = xn[xsel]
            ctx = {"kind": kind, "pset": pset}
            if kind in ("M", "H"):
                P.op("vector", lambda e: e.memset(xs[:], 0.0), writes=[xb], c=0.6)
                src = meta_d if kind == "M" else xprev_d
                P.dma("sync", f"xl{sl}", xs[112:128, :], src[:, :], reads=[xb], writes=[xb])
            else:
                xop = P.dma("sync", f"xl{sl}", xs[:], x_d[kind * 128:(kind + 1) * 128, :], writes=[xb],
                            extra=([cnt["x0op"]] if kind in (2, 3) else []), nbytes=524288)
                if kind == 0:
                    cnt["x0op"] = xop
                load_weight_piece(xop)
            if kind != "H":
                csl = cnt["cs"] % 4
                cnt["cs"] += 1
                csb = buf(f"cs{csl}")
                ti = 0 if kind == "M" else kind + 1
                P.dma("sync", f"csl{csl}", cst[csl][:], cs_v[:, ti, :], writes=[csb])
                ctx["csl"], ctx["csb"] = csl, csb
            rs, rsb = rms_rstd(xs[:], xb, idx)
            P.op("vector", lambda e: e.scalar_tensor_tensor(out=xna, in0=xs[:], scalar=rs, in1=gbc[:],
                                                            op0=ALU.mult, op1=ALU.mult),
                 reads=[xb, rsb, buf("gbc")], writes=[xnb], n=1024)
            pT = bf_view(bT)
            P.op("tensor", [lambda e, kc=kc: e.transpose(out=pT[:, kc, :], in_=xna[:, kc * 128:(kc + 1) * 128],
                                                          identity=identb[:]) for kc in range(8)],
                 reads=[xnb, buf("identb")], writes=[bankb[bT]], c=8 * TR)
            if kind == "H":
                P.op("scalar", lambda e: e.activation(out=hnT[:, :, 0:16], in_=pT[:, :, 112:128], func=AF.Copy),
                     reads=[bankb[bT]], writes=[buf("hnT_h")], n=128)
                return None
            if kind == "M":
                hsrc, hb = hnT_m[:], buf("qkT")
            else:
                hsrc, hb = hnT[:, :, 16 + kind * 128:16 + (kind + 1) * 128], buf(f"hnT{kind}")
            P.op("scalar", lambda e: e.activation(out=hsrc, in_=pT, func=AF.Copy), reads=[bankb[bT]], writes=[hb],
                 n=1024)
            ctx["hsrc"], ctx["hb"] = hsrc, hb
            return ctx

        def p1_B(ctx):
            kind, pset = ctx["kind"], ctx["pset"]
            bK, bV, bU = 4 * pset + 1, 4 * pset + 2, 4 * pset + 3
            hsrc, hb, csl, csb = ctx["hsrc"], ctx["hb"], ctx["csl"], ctx["csb"]
            if kind == "M":
                kdst, kb = ktok_m[:], buf("qrot")
                vdst, vb = vtok_m, buf("sT")
            else:
                kdst, kb = ktok[:, kind, :], buf(f"ktok{kind}")
                vdst, vb = vtok[:, kind, :], buf(f"vtok{kind}")
            P.op("tensor", [lambda e, kc=kc: e.matmul(bank[bK][:], lhsT=hsrc[:, kc, :], rhs=wkv[:, kc, 0:512],
                                                       start=(kc == 0), stop=(kc == 7)) for kc in range(8)],
                 reads=[hb, buf("wk")], writes=[bankb[bK]], c=8 * MM512)
            P.op("tensor", [lambda e, kc=kc: e.matmul(bank[bV][:], lhsT=hsrc[:, kc, :], rhs=wkv[:, kc, 512:1024],
                                                       start=(kc == 0), stop=(kc == 7)) for kc in range(8)],
                 reads=[hb, buf("wv")], writes=[bankb[bV]], c=8 * MM512)
            rotary(bK, b_bc, csl, csb, kdst, kb)
            P.op("scalar", lambda e: e.activation(out=vdst, in_=bank[bV][:], func=AF.Copy),
                 reads=[bankb[bV]], writes=[vb], n=512)
            pU = h_view(bU)
            P.op("tensor", [lambda e, h=h: e.matmul(pU[:, h, :], lhsT=kdst[:, h * 128:(h + 1) * 128],
                                                     rhs=vdst[:, h * 128:(h + 1) * 128], start=True, stop=True)
                            for h in range(4)],
                 reads=[kb, vb], writes=[bankb[bU]], c=4 * MM128)
            if kind == "M":
                P.op("vector", lambda e: e.tensor_copy(out=U0[:], in_=pU), reads=[bankb[bU]], writes=[buf("U0")], n=512)
            elif kind == 0:
                P.op("vector", lambda e: e.tensor_copy(out=Rst[:], in_=pU), reads=[bankb[bU]], writes=[buf("R")], n=512)
            else:
                P.op("vector", [lambda e, h=h: e.scalar_tensor_tensor(out=Rst[:, h, :], in0=Rst[:, h, :],
                                                                       scalar=float(CD[h]), in1=pU[:, h, :],
                                                                       op0=ALU.mult, op1=ALU.add)
                                for h in range(4)],
                     reads=[bankb[bU], buf("R")], writes=[buf("R")], c=0.9)

        order = ["H"] + list(range(NT)) + ["M"]
        pend = []
        for i, kind in enumerate(order):
            ctx = p1_A(kind, i % 2, i)
            if ctx is not None:
                pend.append(ctx)
            if i >= 2 and pend:
                p1_B(pend.pop(0))
        while pend:
            p1_B(pend.pop(0))
        assert not wpieces

        def alias_after(dst, srcs):
            for sname in srcs:
                sb_ = buf(sname)
                dst.r = dst.r + list(sb_.r) + ([sb_.w] if sb_.w is not None else [])
        for nm in ("AG0", "AG1", "AG2", "AG3", "sgr0", "sgr1"):
            alias_after(buf(nm), ["wk", "wv"])
        alias_after(buf("mixr"), ["xn0", "xn1"])
        alias_after(buf("mixc1"), ["xt3"])
        P.dma("sync", "aloc", aloc_d[:, :], Rst[:].rearrange("p h t -> p (h t)"), reads=[buf("R")],
              writes=[buf("aloc")], nbytes=262144)
        cc_sem = st.enter_context(nc.semaphore("cc_sem"))
        P.raw("gpsimd", lambda e: e.collective_compute("AllGather", ALU.bypass,
                                                         replica_groups=[[0, 1, 2, 3], [4, 5, 6, 7]],
                                                         ins=[aloc_d[:, :]], outs=[aall_d[:, :]]),
              cc_sem, reads=[buf("aloc")], writes=[buf("aall")])
        P.dma("sync", "c_g", gbc[:], gf_d[0:1, :].broadcast_to([128, D]), writes=[buf("gbc")], nbytes=524288)
        for r in range(4):
            P.dma("sync", f"agl{r}", AG[:, r, :], aall_d[r * 128:(r + 1) * 128, :], reads=[buf("aall")],
                  writes=[buf(f"AG{r}")], nbytes=1 << 18)
        P.op("vector", [lambda e, h=h: e.tensor_scalar(out=Rst[:, h, :], in0=U0[:, h, :], scalar1=coef[:, h:h + 1],
                                                        scalar2=None, op0=ALU.mult) for h in range(4)],
             reads=[buf("U0"), buf("coef")], writes=[buf("R")], c=0.9)
        AGh = AG.rearrange("p r (h t) -> p r h t", h=4)
        for r in range(4):
            P.op("vector", [lambda e, r=r, h=h: e.scalar_tensor_tensor(out=Rst[:, h, :], in0=AGh[:, r, h, :],
                                                                        scalar=coef[:, 4 + 4 * r + h:5 + 4 * r + h],
                                                                        in1=Rst[:, h, :], op0=ALU.mult, op1=ALU.add)
                            for h in range(4)],
                 reads=[buf(f"AG{r}"), buf("coef"), buf("R")], writes=[buf("R")], c=0.9)
        Sb = [buf("S0"), buf("S1")]

        def make_S(dst_i):
            P.op("scalar", [lambda e, h=h: e.activation(out=Sbf[dst_i][:, h, :], in_=Rst[:, h, :], func=AF.Copy,
                                                         scale=float(CD[h])) for h in range(4)],
                 reads=[buf("R")], writes=[Sb[dst_i]], c=1.3)
        make_S(0)

        pA, pB_, X1, X2, O0, O1, pY, pZ = range(8)

        def proj_fm(bi, wbuf, rcol, tok0, n):
            ti = (tok0 - 16) // 128
            rd = list(wbuf) + [buf(f"hnT{t}") for t in range(ti, min(NT, ti + (n + 127) // 128))]
            P.op("tensor", [lambda e, kc=kc: e.matmul(bank[bi][:, 0:n], lhsT=wrest[:, kc, rcol:rcol + 128],
                                                       rhs=hnT[:, kc, tok0:tok0 + n], start=(kc == 0), stop=(kc == 7))
                            for kc in range(8)],
                 reads=rd, writes=[bankb[bi]], c=8 * (0.03 + (MM512 - 0.03) * n / 512.0))

        for cb_i in range(4):
            P.op("tensor", [lambda e, kc=kc, cb_i=cb_i: e.matmul(bank[pA][:, 4 * cb_i:4 * cb_i + 2],
                                                                  lhsT=wrest[:, kc, R_CC + cb_i * 128:R_CC + (cb_i + 1) * 128],
                                                                  rhs=hnT[:, kc, 14:16], start=(kc == 0), stop=(kc == 7))
                            for kc in range(8)] +
                           [lambda e, kc=kc, cb_i=cb_i: e.matmul(bank[pA][:, 4 * cb_i + 2:4 * cb_i + 4],
                                                                  lhsT=wrest[:, kc, R_CX + cb_i * 128:R_CX + (cb_i + 1) * 128],
                                                                  rhs=hnT[:, kc, 14:16], start=(kc == 0), stop=(kc == 7))
                            for kc in range(8)],
                 reads=wb("wcc") + wb("wcx") + [buf("hnT_h")], writes=[bankb[pA]], c=16 * 0.07)
        P.op("scalar", lambda e: e.activation(out=uh_sb[:], in_=bank[pA][:, 0:16], func=AF.Copy),
             reads=[bankb[pA]], writes=[buf("uh")], n=16)
        uh4 = uh_sb[:].rearrange("p (c k t) -> p c k t", c=4, k=2)
        P.op("gpsimd", lambda e: e.tensor_tensor(out=uhalo[:], in0=uh4[:, :, 0, :], in1=uh4[:, :, 1, :], op=ALU.mult),
             reads=[buf("uh")], writes=[buf("uhalo")], n=8)

        ucnt = {"n": 0}

        GROUPS = [(0, 4), (4, 4), (8, 4), (12, 2), (14, 2)]
        TILE_G = {}
        for gi, (t0g, ntg) in enumerate(GROUPS):
            for j in range(ntg):
                TILE_G[t0g + j] = (gi, j)

        def conv_unit(s, cb_i):
            tok0 = 16 + 128 * GROUPS[s][0]
            nt_ = 128 * GROUPS[s][1]
            k = ucnt["n"] % 2
            ucnt["n"] += 1
            u, uB = uw[k], buf(f"uw{k}")
            c1k, c1B = c1[k], buf(f"c1_{k}")
            tbk, tbB = tb[k], buf(f"tb_{k}")
            mx, mxB = mixc[s % 2], buf(f"mixc{s % 2}")
            cw = lambda j: convw[:, 3 * cb_i + j:3 * cb_i + j + 1]
            bA, bB = (pY, pZ) if (s == 0 and cb_i % 2 == 1) else (pA, pB_)
            proj_fm(bA, wb("wcc"), R_CC + cb_i * 128, tok0, nt_)
            P.op("scalar", lambda e: e.activation(out=c1k[:, 0:nt_], in_=bank[bA][:, 0:nt_], func=AF.Copy),
                 reads=[bankb[bA]], writes=[c1B], n=nt_)
            proj_fm(bB, wb("wcx"), R_CX + cb_i * 128, tok0, nt_)
            P.op("gpsimd", lambda e: e.tensor_copy(out=u[:, 0:2], in_=uhalo[:, cb_i, :]),
                 reads=[buf("uhalo"), uB], writes=[uB], c=0.2)
            P.op("vector", lambda e: e.tensor_tensor(out=u[:, 2:2 + nt_], in0=bank[bB][:, 0:nt_], in1=c1k[:, 0:nt_], op=ALU.mult),
                 reads=[bankb[bB], c1B, uB], writes=[uB], n=nt_)
            P.op("gpsimd", lambda e: e.tensor_copy(out=uhalo[:, cb_i, :], in_=u[:, nt_:nt_ + 2]),
                 reads=[uB, buf("uhalo")], writes=[buf("uhalo")], c=0.2)
            P.op("vector", lambda e: e.tensor_scalar(out=tbk[:, 0:nt_], in0=u[:, 0:nt_], scalar1=cw(0), scalar2=None,
                                                     op0=ALU.mult),
                 reads=[uB, buf("convw")], writes=[tbB], n=nt_)
            P.op("vector", lambda e: e.scalar_tensor_tensor(out=tbk[:, 0:nt_], in0=u[:, 1:1 + nt_], scalar=cw(1), in1=tbk[:, 0:nt_],
                                                            op0=ALU.mult, op1=ALU.add),
                 reads=[uB, buf("convw"), tbB], writes=[tbB], n=nt_)
            P.op("vector", lambda e: e.scalar_tensor_tensor(out=tbk[:, 0:nt_], in0=u[:, 2:2 + nt_], scalar=cw(2), in1=tbk[:, 0:nt_],
                                                            op0=ALU.mult, op1=ALU.add),
                 reads=[uB, buf("convw"), tbB], writes=[tbB], n=nt_)
            proj_fm(bA, wb("wcb"), R_CB + cb_i * 128, tok0, nt_)
            P.op("vector", lambda e: e.tensor_tensor(out=tbk[:, 0:nt_], in0=bank[bA][:, 0:nt_], in1=tbk[:, 0:nt_], op=ALU.mult),
                 reads=[bankb[bA], tbB], writes=[tbB], n=nt_)
            proj_fm(bB, wb("wcg"), R_CG + cb_i * 128, tok0, nt_)
            P.op("scalar", lambda e: e.activation(out=c1k[:, 0:nt_], in_=bank[bB][:, 0:nt_], func=AF.Silu),
                 reads=[bankb[bB], c1B], writes=[c1B], n=nt_)
            P.op("vector", lambda e: e.tensor_tensor(out=mx[:, cb_i, 0:nt_], in0=tbk[:, 0:nt_], in1=c1k[:, 0:nt_], op=ALU.mult),
                 reads=[tbB, c1B, mxB], writes=[mxB], n=nt_)

        def rg_unit(s, h):
            tok0 = 16 + 128 * GROUPS[s][0]
            nt_ = 128 * GROUPS[s][1]
            bk = pA if h % 2 == 0 else pB_
            proj_fm(bk, wb("wrg"), R_RG + h * 128, tok0, nt_)
            P.op("scalar", lambda e: e.activation(out=sgr[s % 2][:, h, 0:nt_], in_=bank[bk][:, 0:nt_], func=AF.Silu),
                 reads=[bankb[bk], buf(f"sgr{s % 2}")], writes=[buf(f"sgr{s % 2}")], n=nt_)

        p2 = {"slot": 0, "cs": 0}
        xinfo = {}

        def issue_reload(t):
            sl = p2["slot"] % 3
            p2["slot"] += 1
            xinfo[t] = sl
            P.dma("sync", f"xl{sl}", xt[sl][:], x_d[t * 128:(t + 1) * 128, :],
                  writes=[buf(f"xt{sl}"), buf(f"xt{sl}_h0"), buf(f"xt{sl}_h1")], nbytes=524288)
            csl = p2["cs"] % 4
            p2["cs"] += 1
            xinfo[("cs", t)] = csl
            P.dma("sync", f"csl{csl}", cst[csl][:], cs_v[:, t + 1, :], writes=[buf(f"cs{csl}")])

        out_ops = []

        def ret_A(t, cur):
            csl = xinfo[("cs", t)]
            csb = buf(f"cs{csl}")
            bO = O0 + (t % 2)
            hT = hnT[:, :, 16 + t * 128:16 + (t + 1) * 128]
            P.op("tensor", [lambda e, kc=kc: e.matmul(bank[X1][:], lhsT=hT[:, kc, :], rhs=wrest[:, kc, R_Q:R_Q + 512],
                                                       start=(kc == 0), stop=(kc == 7)) for kc in range(8)],
                 reads=[buf(f"hnT{t}")] + wb("wq"), writes=[bankb[X1]], c=8 * MM512)
            rotary(X1, a_bc, csl, csb, qrot[:], buf("qrot"))
            pT = bf_view(X1)
            P.op("tensor", [lambda e, h=h: e.transpose(out=pT[:, h, :], in_=qrot[:, h * 128:(h + 1) * 128],
                                                        identity=identb[:]) for h in range(4)] +
                           [lambda e, h=h: e.transpose(out=pT[:, 4 + h, :], in_=ktok[:, t, h * 128:(h + 1) * 128],
                                                        identity=identb[:]) for h in range(4)],
                 reads=[buf("qrot"), buf(f"ktok{t}"), buf("identb")], writes=[bankb[X1]], c=8 * TR)
            P.op("scalar", lambda e: e.activation(out=qkT[:], in_=pT, func=AF.Copy),
                 reads=[bankb[X1]], writes=[buf("qkT")], n=1024)
            pSv = h_view(X2)
            P.op("tensor", [lambda e, h=h: e.matmul(pSv[:, h, :], lhsT=qkT[:, 4 + h, :], rhs=qkT[:, h, :],
                                                     start=True, stop=True) for h in range(4)],
                 reads=[buf("qkT")], writes=[bankb[X2]], c=4 * MM128)
            mask_bc = maskT[:].unsqueeze(1).broadcast_to([128, 4, 128])
            P.op("vector", lambda e: e.tensor_tensor(out=sT[:], in0=pSv, in1=mask_bc, op=ALU.mult),
                 reads=[bankb[X2], buf("maskT")], writes=[buf("sT")], n=512)
            pOv = h_view(bO)
            fl = []
            for h in range(4):
                fl.append(lambda e, h=h: e.matmul(pOv[:, h, :], lhsT=sT[:, h, :], rhs=vtok[:, t, h * 128:(h + 1) * 128],
                                                   start=True, stop=False))
                fl.append(lambda e, h=h: e.matmul(pOv[:, h, :], lhsT=qkT[:, h, :], rhs=Sbf[cur][:, h, :],
                                                   start=False, stop=True))
            P.op("tensor", fl, reads=[buf("sT"), buf(f"vtok{t}"), buf("qkT"), Sb[cur]], writes=[bankb[bO]],
                 c=8 * MM128)
            if t < NT - 1:
                pUv = h_view(X2)
                P.op("tensor", [lambda e, h=h: e.matmul(pUv[:, h, :], lhsT=ktok[:, t, h * 128:(h + 1) * 128],
                                                         rhs=vtok[:, t, h * 128:(h + 1) * 128], start=True, stop=True)
                                for h in range(4)],
                     reads=[buf(f"ktok{t}"), buf(f"vtok{t}")], writes=[bankb[X2]], c=4 * MM128)
                P.op("vector", [lambda e, h=h: e.scalar_tensor_tensor(out=Rst[:, h, :], in0=Rst[:, h, :],
                                                                       scalar=float(CD[h]), in1=pUv[:, h, :],
                                                                       op0=ALU.mult, op1=ALU.add) for h in range(4)],
                     reads=[bankb[X2], buf("R")], writes=[buf("R")], c=0.9)
                make_S(1 - cur)

        def ret_B(t):
            s, i = TILE_G[t]
            sl = xinfo[t]
            xb = buf(f"xt{sl}")
            xs = xt[sl]
            par = t % 2
            bO = O0 + par
            pOv = h_view(bO)
            sg = sgr[s % 2]
            sgB = buf(f"sgr{s % 2}")
            mx, mxB = mixc[s % 2], buf(f"mixc{s % 2}")
            stB = buf(f"gn{par}")
            P.op("vector", [lambda e, h=h: e.bn_stats(out=bns[:, par, h, :], in_=pOv[:, h, :]) for h in range(4)],
                 reads=[bankb[bO]], writes=[buf(f"bns{par}")], c=0.9)
            P.op("vector", [lambda e, h=h: e.bn_aggr(out=mv[:, par, h, :], in_=bns[:, par, h, :]) for h in range(4)],
                 reads=[buf(f"bns{par}")], writes=[buf(f"mv{par}")], c=0.4)
            P.op("vector", lambda e: e.tensor_scalar(out=gve[:, par, :], in0=mv[:, par, :, 1], scalar1=EPS,
                                                     scalar2=None, op0=ALU.add),
                 reads=[buf(f"mv{par}")], writes=[stB], n=4)
            P.op("gpsimd", lambda e: e.tensor_tensor(out=grs[:, par, :], in0=gve[:, par, :], in1=mhalf[:], op=ALU.pow),
                 reads=[stB, buf("mhalf")], writes=[buf(f"grs{par}")], c=1.0)
            P.op("vector", lambda e: e.scalar_tensor_tensor(out=gnm[:, par, :], in0=mv[:, par, :, 0], scalar=-1.0,
                                                            in1=grs[:, par, :], op0=ALU.mult, op1=ALU.mult),
                 reads=[buf(f"mv{par}"), buf(f"grs{par}")], writes=[buf(f"gnm{par}")], n=4)
            P.op("scalar", [lambda e, h=h: e.activation(out=ybuf[:, h, :], in_=pOv[:, h, :], func=AF.Identity,
                                                         bias=gnm[:, par, h:h + 1], scale=grs[:, par, h:h + 1])
                            for h in range(4)],
                 reads=[bankb[bO], buf(f"grs{par}"), buf(f"gnm{par}")], writes=[buf("ybuf")], c=1.5)
            for half in range(2):
                bo = pY if half == 0 else pZ
                P.op("tensor", [lambda e, ec=ec, half=half, bo=bo: e.matmul(bank[bo][:], lhsT=mx[:, ec, i * 128:(i + 1) * 128],
                                                                              rhs=wout[:, ec, half * 512:(half + 1) * 512],
                                                                              start=(ec == 0), stop=False)
                                for ec in range(4)],
                     reads=[mxB] + wb("wout0") + wb("wout1"), writes=[bankb[bo]], c=4 * MM512)
            pT = bf_view(bO)
            P.op("tensor", [lambda e, h=h: e.transpose(out=pT[:, h, :], in_=ybuf[:, h, :], identity=identb[:])
                            for h in range(4)],
                 reads=[buf("ybuf"), buf("identb")], writes=[bankb[bO]], c=4 * TR)
            P.op("vector", [lambda e, h=h: e.scalar_tensor_tensor(out=mixr[:, h, i * 128:(i + 1) * 128], in0=pT[:, h, :],
                                                                   scalar=gret[:, h:h + 1],
                                                                   in1=sg[:, h, i * 128:(i + 1) * 128],
                                                                   op0=ALU.mult, op1=ALU.mult) for h in range(4)],
                 reads=[bankb[bO], buf("gret"), sgB, buf("mixr")], writes=[buf("mixr")], c=0.9)
            for half in range(2):
                bo = pY if half == 0 else pZ
                P.op("tensor", [lambda e, ec=ec, half=half, bo=bo: e.matmul(bank[bo][:], lhsT=mixr[:, ec - 4, i * 128:(i + 1) * 128],
                                                                              rhs=wout[:, ec, half * 512:(half + 1) * 512],
                                                                              start=False, stop=(ec == 7))
                                for ec in range(4, 8)],
                     reads=[buf("mixr")] + wb("wout0") + wb("wout1"), writes=[bankb[bo]], c=4 * MM512)
                P.op("vector", lambda e, half=half, bo=bo: e.tensor_tensor(out=xs[:, half * 512:(half + 1) * 512],
                                                                           in0=bank[bo][:],
                                                                           in1=xs[:, half * 512:(half + 1) * 512], op=ALU.add),
                     reads=[bankb[bo], xb], writes=[xb], n=512)
            rs, rsb = rms_rstd(xs[:], xb, 32 + t)
            for half in range(2):
                cs_ = slice(half * 512, (half + 1) * 512)
                hb = buf(f"xt{sl}_h{half}")
                P.op("vector", lambda e, cs_=cs_: e.scalar_tensor_tensor(out=xs[:, cs_], in0=xs[:, cs_], scalar=rs,
                                                                          in1=gbc[:, cs_], op0=ALU.mult, op1=ALU.mult),
                     reads=[xb, rsb, buf("gbc")], writes=[hb], n=512)
                out_ops.append(P.dma("sync", f"st{sl}_{half}", y_d[t * 128:(t + 1) * 128, cs_], xs[:, cs_], reads=[hb],
                                     nbytes=262144))

        issue_reload(0)
        issue_reload(1)
        for cb_i in range(4):
            conv_unit(0, cb_i)
        for h in range(4):
            rg_unit(0, h)
        cur = 0
        for gi, (t0g, ntg) in enumerate(GROUPS):
            nxt = gi + 1 if gi + 1 < len(GROUPS) else None
            units = ([("c", cb) for cb in range(4)] + [("r", h) for h in range(4)]) if nxt is not None else []
            per = (len(units) + 2 * ntg - 1) // (2 * ntg) if units else 0
            for j in range(ntg):
                t = t0g + j
                if t + 2 < NT:
                    issue_reload(t + 2)
                ret_A(t, cur)
                cur = 1 - cur
                for _ in range(per):
                    if units:
                        kind_u, a_u = units.pop(0)
                        (conv_unit if kind_u == "c" else rg_unit)(nxt, a_u)
                ret_B(t)
                for _ in range(per):
                    if units:
                        kind_u, a_u = units.pop(0)
                        (conv_unit if kind_u == "c" else rg_unit)(nxt, a_u)
            assert not units
        P.final = ("sync", out_ops)
        P.run()
    return nc


def _host_constants():
    h = np.arange(NH)
    gam = (1.0 - 2.0 ** (-5.0 - h)).astype(np.float64)
    idx = np.arange(128, dtype=np.float64)
    a = (128.0 ** -0.5) * gam[None, :] ** (idx[:, None] + 1.0)
    b = gam[None, :] ** (-(idx[:, None] + 1.0))
    ab = np.concatenate([a, b], axis=1).astype(np.float32)
    maskT = (idx[None, :] >= idx[:, None]).astype(np.float32)
    ident = np.eye(128, dtype=np.float32)
    half = 64
    freqs = (1.0 / (np.float32(10000.0) ** (np.arange(half, dtype=np.float32) / np.float32(half)))).astype(np.float32)
    cd = gam ** 128
    return gam, cd, ab, maskT, ident, freqs


def _cs_table(freqs, positions):
    ang = (positions.astype(np.float32)[:, None] * freqs[None, :]).astype(np.float32)
    return np.concatenate([np.cos(ang), np.sin(ang)], axis=1).astype(np.float32)


def _cs_table_all():
    L = NMETA + SEQ
    try:
        import jax
        import jax.numpy as jnp
        with jax.default_device(jax.devices("cpu")[0]):
            half = 64
            fr = 1.0 / (10000.0 ** (jnp.arange(half, dtype=jnp.float32) / half))
            ang = jnp.arange(L, dtype=jnp.int32).astype(jnp.float32)[:, None] * fr[None, :]
            tab = np.concatenate([np.asarray(jnp.cos(ang)), np.asarray(jnp.sin(ang))], axis=1)
        return np.ascontiguousarray(tab.astype(np.float32))
    except Exception:
        freqs = (1.0 / (np.float32(10000.0) ** (np.arange(64, dtype=np.float32) / np.float32(64)))).astype(np.float32)
        return _cs_table(freqs, np.arange(L))


_NC_CACHE = {}


def kernel(x, meta, norm1_g, w_in, conv_w, ret_norm_g, w_out, final_g):
    x = np.ascontiguousarray(np.asarray(x, dtype=np.float32))
    meta = np.ascontiguousarray(np.asarray(meta, dtype=np.float32))
    w_in = np.ascontiguousarray(np.asarray(w_in, dtype=np.float32))
    w_out = np.ascontiguousarray(np.asarray(w_out, dtype=np.float32))
    norm1_g = np.asarray(norm1_g, dtype=np.float32).reshape(1, D)
    final_g = np.asarray(final_g, dtype=np.float32).reshape(1, D)
    conv_w = np.asarray(conv_w, dtype=np.float32)
    ret_norm_g = np.asarray(ret_norm_g, dtype=np.float32)

    gam, cd, ab, maskT, ident, freqs = _host_constants()
    convw_l = np.ascontiguousarray(conv_w.reshape(3, 4, 128).transpose(2, 1, 0).reshape(128, 12))
    gret_l = np.ascontiguousarray(ret_norm_g.reshape(4, 128).T)

    if "nc" not in _NC_CACHE:
        _NC_CACHE["nc"] = build_nc()
    nc = _NC_CACHE["nc"]

    cs_all = _cs_table_all()
    in_maps = []
    for core in range(NCORE):
        b, c = divmod(core, 4)
        xs = x[b, c * TOK:(c + 1) * TOK, :]
        xprev = meta if c == 0 else x[b, c * TOK - 16:c * TOK, :]
        cs = np.zeros((17 * 128, 128), np.float32)
        cs[112:128] = cs_all[0:NMETA]
        cs[128:] = cs_all[NMETA + c * TOK:NMETA + (c + 1) * TOK]
        coef = np.zeros((1, 20), np.float64)
        coef[0, 0:4] = cd ** (16 * c)
        for r in range(4):
            if r < c:
                coef[0, 4 + 4 * r:8 + 4 * r] = cd ** (16 * (c - 1 - r))
        in_maps.append({
            "x": np.ascontiguousarray(xs), "xprev": np.ascontiguousarray(xprev), "meta": meta,
            "w_in": w_in, "w_out": w_out, "norm1_g": norm1_g, "final_g": final_g,
            "convw": convw_l, "gret": gret_l, "cs": cs, "ab": ab, "maskT": maskT, "ident": ident,
            "coef": coef.astype(np.float32),
        })
    res = run_bass_kernel_spmd(nc, in_maps, core_ids=list(range(NCORE)))
    out = np.empty((BATCH, SEQ, D), np.float32)
    for core in range(NCORE):
        b, c = divmod(core, 4)
        out[b, c * TOK:(c + 1) * TOK, :] = res.results[core]["y"]
    return out
```

```python
import numpy as np
from contextlib import ExitStack
import concourse.bass as bass
import concourse.mybir as mybir
from concourse.bass_utils import run_bass_kernel_spmd

F32 = mybir.dt.float32
BF16 = mybir.dt.bfloat16
ALU = mybir.AluOpType
AF = mybir.ActivationFunctionType

D = 1024
SEQ = 8192
BATCH = 2
NMETA = 16
NCORE = 8
TOK = 2048
NT = 16
NH = 4
EPS = 1e-6
ENGS = ["sync", "scalar", "vector", "gpsimd", "tensor"]

GAMMA = [1.0 - 2.0 ** (-5.0 - h) for h in range(NH)]
CD = [g ** 128 for g in GAMMA]

C_CX, C_CB, C_CC, C_CG, C_Q, C_K, C_V, C_RG = [i * 512 for i in range(8)]
R_CX, R_CB, R_CC, R_CG, R_Q, R_RG = 0, 512, 1024, 1536, 2048, 2560


class Buf:
    def __init__(self, name, excl=False):
        self.name = name
        self.w = None
        self.r = []
        self.excl = excl


def _cost(eng, n):
    if eng == "vector":
        return 0.12 + n / 900.0
    if eng == "scalar":
        return 0.22 + n / 1200.0
    if eng == "gpsimd":
        return 0.35 + n / 330.0
    return 0.25


SEM_LAT = 0.12
import os
NOSCHED = bool(int(os.environ.get("KERNEL_NOSCHED", "0")))
SCHED_CP = bool(int(os.environ.get("KERNEL_CP", "1")))
SCHED_W = int(os.environ.get("KERNEL_W", "96"))
DMA_BPUS = float(os.environ.get("KERNEL_DMA_BPUS", "150e3"))


class Prog:
    def __init__(self, nc, stack):
        self.nc = nc
        self.stack = stack
        self.ops = []
        self.esem = {e: stack.enter_context(nc.semaphore("es_" + e)) for e in ENGS if e != "sync"}
        self.dsem = {}

    def sb(self, name, shape, dt):
        return self.stack.enter_context(self.nc.sbuf_tensor(name, shape, dt))

    def ps(self, name, shape, dt):
        return self.stack.enter_context(self.nc.psum_tensor(name, shape, dt))

    def _add(self, op, reads, writes, extra):
        idx = len(self.ops)
        deps = {}
        for b in reads:
            if b.w is not None:
                deps[b.w] = "raw"
            if b.excl:
                for r in b.r:
                    deps.setdefault(r, "war")
        for b in writes:
            if b.w is not None:
                deps[b.w] = "raw"
            for r in b.r:
                deps.setdefault(r, "war")
        for x in extra:
            if x is not None:
                deps[x] = "raw"
        deps.pop(idx, None)
        op["deps"] = deps
        op["idx"] = idx
        self.ops.append(op)
        for b in reads:
            if b not in writes:
                b.r.append(idx)
        for b in writes:
            b.w = idx
            b.r = []
        return idx

    def op(self, eng, fns, reads=(), writes=(), extra=(), n=512, c=None):
        if not isinstance(fns, (list, tuple)):
            fns = [fns]
        cost = c if c is not None else _cost(eng, n)
        return self._add({"eng": eng, "fns": list(fns), "kind": "op", "busy": cost, "lat": cost}, reads, writes, extra)

    def dma(self, eng, slot, out, in_, reads=(), writes=(), extra=(), nbytes=65536):
        busy = {"sync": 0.4, "scalar": 0.8}.get(eng, 3.0)
        lat = busy + 2.0 + nbytes / DMA_BPUS
        return self._add({"eng": eng, "kind": "dma", "slot": slot, "out": out, "in_": in_, "busy": busy, "lat": lat},
                         reads, writes, extra)

    def raw(self, eng, fn, sem, reads=(), writes=(), busy=1.0, lat=50.0):
        return self._add({"eng": eng, "kind": "raw", "fn": fn, "sem": sem, "busy": busy, "lat": lat},
                         reads, writes, ())

    def schedule(self):
        ops = self.ops
        N = len(ops)
        if NOSCHED:
            order = {e: [] for e in ENGS}
            for o in ops:
                order[o["eng"]].append(o["idx"])
            self.order = order
            self.makespan = 0.0
            return order
        succ = [[] for _ in range(N)]
        for o in ops:
            for d in o["deps"]:
                succ[d].append(o["idx"])
        blevel = [0.0] * N
        for i in range(N - 1, -1, -1):
            m = 0.0
            for j in succ[i]:
                if blevel[j] > m:
                    m = blevel[j]
            blevel[i] = ops[i]["lat"] + SEM_LAT + m
        pending = {e: [] for e in ENGS}
        for o in ops:
            pending[o["eng"]].append(o["idx"])
        fin = [None] * N
        free = {e: 0.0 for e in ENGS}
        order = {e: [] for e in ENGS}
        head = {e: 0 for e in ENGS}
        done = [False] * N
        W = SCHED_W
        nleft = N
        while nleft:
            best = None
            for e in ENGS:
                lst = pending[e]
                h = head[e]
                while h < len(lst) and done[lst[h]]:
                    h += 1
                head[e] = h
                cand = None
                cnt = 0
                for j in range(h, len(lst)):
                    i = lst[j]
                    if done[i]:
                        continue
                    cnt += 1
                    if cnt > W:
                        break
                    o = ops[i]
                    est = 0.0
                    ok = True
                    for d in o["deps"]:
                        f = fin[d]
                        if f is None:
                            ok = False
                            break
                        if f + SEM_LAT > est:
                            est = f + SEM_LAT
                    if not ok:
                        continue
                    st = est if est > free[e] else free[e]
                    key = (st, -blevel[i] if SCHED_CP else i, i)
                    if cand is None or key < cand:
                        cand = key
                if cand is not None and (best is None or (cand[0], cand[2]) < (best[0], best[1])):
                    best = (cand[0], cand[2], e)
            assert best is not None, "scheduler deadlock"
            st, i, e = best
            o = ops[i]
            fin[i] = st + o["lat"]
            free[e] = st + o["busy"]
            done[i] = True
            order[e].append(i)
            nleft -= 1
        self.order = order
        self.makespan = max(f for f in fin)
        return order

    def run(self):
        order = self.schedule()
        ops = self.ops
        tok = [None] * len(ops)
        dcnt = {}
        for e in ENGS:
            cnt = 0
            for i in order[e]:
                o = ops[i]
                if o["kind"] == "op":
                    cnt += 1
                    tok[i] = (self.esem[e], cnt, "e_" + e)
                elif o["kind"] == "dma":
                    slot = o["slot"]
                    if slot not in self.dsem:
                        self.dsem[slot] = self.stack.enter_context(self.nc.semaphore("ds_" + slot))
                        dcnt[slot] = 0
                    dcnt[slot] += 16
                    tok[i] = (self.dsem[slot], dcnt[slot], "d_" + slot)
                else:
                    tok[i] = (o["sem"], 1, "r_%d" % i)

        def emit_engine(e, eng_obj):
            waited = {}
            for i in order[e]:
                o = ops[i]
                need = {}
                for d, kind in o["deps"].items():
                    od = ops[d]
                    if e == "tensor" and od["eng"] == "tensor" and od["kind"] == "op":
                        continue
                    sem, val, key = tok[d]
                    if waited.get(key, 0) >= val:
                        continue
                    if need.get(key, (None, 0))[1] < val:
                        need[key] = (sem, val)
                for key, (sem, val) in need.items():
                    waited[key] = val
                    eng_obj.wait_ge(sem, val)
                if o["kind"] == "op":
                    for f in o["fns"][:-1]:
                        f(eng_obj)
                    o["fns"][-1](eng_obj).then_inc(tok[i][0], 1)
                elif o["kind"] == "dma":
                    eng_obj.dma_start(out=o["out"], in_=o["in_"]).then_inc(tok[i][0], 16)
                else:
                    o["fn"](eng_obj).then_inc(o["sem"])

        final = getattr(self, "final", None)

        def emit_final(e, eng_obj):
            if final is not None and final[0] == e:
                seen = {}
                for d in final[1]:
                    sem, val, key = tok[d]
                    if seen.get(key, (None, 0))[1] < val:
                        seen[key] = (sem, val)
                for key, (sem, val) in seen.items():
                    eng_obj.wait_ge(sem, val)

        with self.nc.Block() as block:
            @block.sync
            def _(e):
                emit_engine("sync", e)
                emit_final("sync", e)

            @block.scalar
            def _(e):
                emit_engine("scalar", e)

            @block.vector
            def _(e):
                emit_engine("vector", e)

            @block.gpsimd
            def _(e):
                emit_engine("gpsimd", e)

            @block.tensor
            def _(e):
                emit_engine("tensor", e)


def build_nc():
    nc = bass.Bass("TRN2", target_bir_lowering=False)
    dr = lambda name, shape, kind="ExternalInput": nc.dram_tensor(name, shape, F32, kind=kind).ap()
    x_d = dr("x", [TOK, D])
    xprev_d = dr("xprev", [16, D])
    meta_d = dr("meta", [16, D])
    win_d = dr("w_in", [D, 4096])
    wout_d = dr("w_out", [D, D])
    g1_d = dr("norm1_g", [1, D])
    gf_d = dr("final_g", [1, D])
    convw_d = dr("convw", [128, 12])
    gret_d = dr("gret", [128, 4])
    cs_d = dr("cs", [17 * 128, 128])
    ab_d = dr("ab", [128, 8])
    mask_d = dr("maskT", [128, 128])
    ident_d = dr("ident", [128, 128])
    coef_d = dr("coef", [1, 20])
    y_d = dr("y", [TOK, D], kind="ExternalOutput")
    aloc_d = dr("a_loc", [128, 512], kind="Internal")
    aall_d = dr("a_all", [4 * 128, 512], kind="Internal")

    win_v = win_d.rearrange("(k p) n -> p k n", p=128)
    wout_v = wout_d.rearrange("(k p) n -> p k n", p=128)
    cs_v = cs_d.rearrange("(t p) f -> p t f", p=128)

    with ExitStack() as st:
        P = Prog(nc, st)
        wkv = P.sb("wkv", [128, 8, 1024], BF16)
        wrest = P.sb("wrest", [128, 8, 3072], BF16)
        wout = P.sb("wout", [128, 8, 1024], BF16)
        hnT = P.sb("hnT", [128, 8, 16 + TOK], BF16)
        ktok = P.sb("ktok", [128, NT, 512], BF16)
        vtok = P.sb("vtok", [128, NT, 512], BF16)
        xt = [P.sb(f"xt{i}", [128, D], F32) for i in range(3)]
        xn_mixr = P.sb("xn_mixr", [128, 2048], BF16)
        junk = P.sb("junk", [128, D], BF16)
        gbc = P.sb("gbc", [128, D], F32)
        cst = [P.sb(f"cs{i}", [128, 128], F32) for i in range(4)]
        ks = P.sb("ks", [128, 512], F32)
        rA = P.sb("rA", [128, 512], F32)
        rB = P.sb("rB", [128, 512], F32)
        qrot = P.sb("qrot", [128, 512], BF16)
        qkT = P.sb("qkT", [128, 8, 128], BF16)
        sT = P.sb("sT", [128, 4, 128], BF16)
        ybuf = P.sb("ybuf", [128, 4, 128], BF16)
        c1 = [P.sb(f"c1_{i}", [128, 512], F32) for i in range(2)]
        tb = [P.sb(f"tb_{i}", [128, 512], F32) for i in range(2)]
        uw = [P.sb(f"uw{i}", [128, 514], F32) for i in range(2)]
        uhalo = P.sb("uhalo", [128, 4, 2], F32)
        mixc = [P.sb(f"mixc{i}", [128, 4, 512], BF16) for i in range(2)]
        Rst = P.sb("Rst", [128, 4, 128], F32)
        U0 = P.sb("U0", [128, 4, 128], F32)
        Sbf = [P.sb(f"Sbf{i}", [128, 4, 128], BF16) for i in range(2)]
        identb = P.sb("identb", [128, 128], BF16)
        maskT = P.sb("maskT_sb", [128, 128], F32)
        ab = P.sb("ab_sb", [128, 8], F32)
        coef = P.sb("coef_sb", [128, 20], F32)
        convw = P.sb("convw_sb", [128, 12], F32)
        gret = P.sb("gret_sb", [128, 4], F32)
        mhalf = P.sb("mhalf", [128, 4], F32)
        ssq = P.sb("ssq", [128, 64], F32)
        ms = P.sb("ms", [128, 64], F32)
        rstd = P.sb("rstd", [128, 64], F32)
        cdb = P.sb("cdb", [128, 4], F32)
        bns = P.sb("bns", [128, 2, 4, 6], F32)
        mv = P.sb("mv", [128, 2, 4, 2], F32)
        gve = P.sb("gve", [128, 2, 4], F32)
        grs = P.sb("grs", [128, 2, 4], F32)
        gnm = P.sb("gnm", [128, 2, 4], F32)
        uh_sb = P.sb("uh_sb", [128, 16], F32)

        identf = junk[:, 0:256].bitcast(F32)
        xt_p1 = xt + [mixc[1][:].rearrange("p a b -> p (a b)").bitcast(F32)]
        xn = [xn_mixr[:, 0:1024], xn_mixr[:, 1024:2048]]
        mixr = xn_mixr[:].rearrange("p (h t) -> p h t", h=4)
        wkv_flat = wkv[:].rearrange("p a b -> p (a b)")
        AG = wkv_flat[:, 0:4096].bitcast(F32).rearrange("p (r f) -> p r f", r=4)
        sgr = [wkv_flat[:, 4096:6144].rearrange("p (h t) -> p h t", h=4),
               wkv_flat[:, 6144:8192].rearrange("p (h t) -> p h t", h=4)]
        hnT_m = qkT
        ktok_m = qrot
        vtok_m = sT[:].rearrange("p a b -> p (a b)")

        bank = [P.ps(f"bank{i}", [128, 512], F32) for i in range(8)]
        bankb = [Buf(f"bank{i}", excl=True) for i in range(8)]

        def bf_view(i):
            return bank[i][:].bitcast(BF16).rearrange("p (k t) -> p k t", k=8)

        def h_view(i):
            return bank[i][:].rearrange("p (h t) -> p h t", h=4)

        B = {}

        def buf(name):
            if name not in B:
                B[name] = Buf(name)
            return B[name]

        MM512 = 0.27
        MM128 = 0.09
        TR = 0.12

        P.dma("sync", "c_ident", identf, ident_d[:, :], writes=[buf("junk")])
        P.dma("sync", "c_g", gbc[:], g1_d[0:1, :].broadcast_to([128, D]), writes=[buf("gbc")], nbytes=524288)
        P.dma("scalar", "c_ab", ab[:], ab_d[:, :], writes=[buf("ab")])
        P.dma("scalar", "c_mask", maskT[:], mask_d[:, :], writes=[buf("maskT")])
        P.dma("scalar", "c_coef", coef[:], coef_d[0:1, :].broadcast_to([128, 20]), writes=[buf("coef")])
        P.dma("scalar", "c_convw", convw[:], convw_d[:, :], writes=[buf("convw")])
        P.dma("scalar", "c_gret", gret[:], gret_d[:, :], writes=[buf("gret")])
        P.dma("gpsimd", "w_k", wkv[:, :, 0:512], win_v[:, :, C_K:C_K + 512], writes=[buf("wk")], nbytes=2 << 20)
        P.dma("gpsimd", "w_v", wkv[:, :, 512:1024], win_v[:, :, C_V:C_V + 512], writes=[buf("wv")], nbytes=2 << 20)
        P.op("gpsimd", lambda e: e.memset(mhalf[:], -0.5), writes=[buf("mhalf")], n=4)
        P.op("gpsimd", [lambda e, h=h: e.memset(cdb[:, h:h + 1], float(CD[h])) for h in range(4)],
             writes=[buf("cdb")], n=16)
        P.op("vector", lambda e: e.tensor_copy(out=identb[:], in_=identf), reads=[buf("junk")],
             writes=[buf("identb")], n=128)
        wpieces = []
        wparts = {}

        def add_piece(nm, dst, src, nb):
            key = nm + "_%d" % len(wparts.setdefault(nm, []))
            wparts[nm].append(buf(key))
            wpieces.append((key, dst, src, nb))

        for (nm, ro, co) in [("wcc", R_CC, C_CC), ("wcx", R_CX, C_CX), ("wcb", R_CB, C_CB), ("wcg", R_CG, C_CG)]:
            for qk in range(4):
                add_piece(nm, wrest[:, 2 * qk:2 * qk + 2, ro:ro + 512], win_v[:, 2 * qk:2 * qk + 2, co:co + 512], 1 << 19)
        n_paced = len(wpieces)
        for (nm, ro, co) in [("wrg", R_RG, C_RG), ("wq", R_Q, C_Q)]:
            for hk in range(2):
                add_piece(nm, wrest[:, 4 * hk:4 * hk + 4, ro:ro + 512], win_v[:, 4 * hk:4 * hk + 4, co:co + 512], 1 << 20)
        for half in range(2):
            for hk in range(2):
                add_piece(f"wout{half}", wout[:, 4 * hk:4 * hk + 4, half * 512:(half + 1) * 512],
                          wout_v[:, 4 * hk:4 * hk + 4, half * 512:(half + 1) * 512], 1 << 20)
        assert n_paced == NT

        def wb(nm):
            return list(wparts[nm])

        wstate = {"n": 0}

        def load_weight_piece(after_op):
            key, dst, src, nb = wpieces.pop(0)
            P.dma("gpsimd", "w_" + key, dst, src, writes=[buf(key)], extra=[after_op], nbytes=nb)
            wstate["n"] += 1
            if wstate["n"] == NT:
                while wpieces:
                    key, dst, src, nb = wpieces.pop(0)
                    P.dma("gpsimd", "w_" + key, dst, src, writes=[buf(key)], extra=[after_op], nbytes=nb)

        cnt = {"slot": 0, "cs": 0}

        def rms_rstd(src_ap, srcbuf, col, extra_reads=()):
            P.op("scalar", lambda e: e.activation(out=junk[:], in_=src_ap, func=AF.Square,
                                                  accum_out=ssq[:, col:col + 1]),
                 reads=[srcbuf] + list(extra_reads), writes=[buf(f"ssq{col}"), buf("junk")], n=1024)
            P.op("scalar", lambda e: e.activation(out=ms[:, col:col + 1], in_=ssq[:, col:col + 1], func=AF.Identity,
                                                  scale=1.0 / D, bias=EPS),
                 reads=[buf(f"ssq{col}")], writes=[buf(f"ms{col}")], n=1)
            P.op("gpsimd", lambda e: e.tensor_tensor(out=rstd[:, col:col + 1], in0=ms[:, col:col + 1],
                                                     in1=mhalf[:, 0:1], op=ALU.pow),
                 reads=[buf(f"ms{col}"), buf("mhalf")], writes=[buf(f"rstd{col}")], c=1.0)
            return rstd[:, col:col + 1], buf(f"rstd{col}")

        def rotary(psum_i, scale_bc, csl, csbuf, dst_ap, dstbuf):
            pv = h_view(psum_i)
            ks4 = ks[:].rearrange("p (h t f) -> p h t f", h=4, t=2)
            a4 = rA[:].rearrange("p (h t f) -> p h t f", h=4, t=2)
            b4 = rB[:].rearrange("p (h t f) -> p h t f", h=4, t=2)
            d4 = dst_ap.rearrange("p (h t f) -> p h t f", h=4, t=2)
            cosb = cst[csl][:, 0:64].unsqueeze(1).unsqueeze(1).broadcast_to([128, 4, 2, 64])
            sinb = cst[csl][:, 64:128].unsqueeze(1).unsqueeze(1).broadcast_to([128, 4, 2, 64])
            ksh = ks[:].rearrange("p (h t) -> p h t", h=4)
            P.op("vector", lambda e: e.tensor_tensor(out=ksh, in0=pv, in1=scale_bc, op=ALU.mult),
                 reads=[bankb[psum_i], buf("ab")], writes=[buf("ks")], n=512)
            P.op("vector", lambda e: e.tensor_tensor(out=a4, in0=ks4, in1=cosb, op=ALU.mult),
                 reads=[buf("ks"), csbuf], writes=[buf("rA")], n=512)
            P.op("vector", lambda e: e.tensor_tensor(out=b4, in0=ks4[:, :, ::-1, :], in1=sinb, op=ALU.mult),
                 reads=[buf("ks"), csbuf], writes=[buf("rB")], n=512)
            P.op("vector", lambda e: e.tensor_tensor(out=d4[:, :, 0, :], in0=a4[:, :, 0, :], in1=b4[:, :, 0, :],
                                                     op=ALU.subtract),
                 reads=[buf("rA"), buf("rB")], writes=[dstbuf], n=256)
            P.op("gpsimd", lambda e: e.tensor_tensor(out=d4[:, :, 1, :], in0=a4[:, :, 1, :], in1=b4[:, :, 1, :],
                                                     op=ALU.add),
                 reads=[buf("rA"), buf("rB"), dstbuf], writes=[dstbuf], n=256)

        a_bc = ab[:, 0:4].unsqueeze(2).broadcast_to([128, 4, 128])
        b_bc = ab[:, 4:8].unsqueeze(2).broadcast_to([128, 4, 128])

        def p1_A(kind, pset, idx):
            bT = 4 * pset
            sl = cnt["slot"] % 4
            cnt["slot"] += 1
            xb = buf(f"xt{sl}")
            xs_t = xt_p1[sl]
            xs = xs_t if sl == 3 else xs_t[:]
            xsel = cnt["slot"] % 2
            xnb = buf(f"xn{xsel}")
            xna = xn[xsel]
            ctx = {"kind": kind, "pset": pset}
            if kind in ("M", "H"):
                P.op("vector", lambda e: e.memset(xs[:], 0.0), writes=[xb], c=0.6)
                src = meta_d if kind == "M" else xprev_d
                P.dma("sync", f"xl{sl}", xs[112:128, :], src[:, :], reads=[xb], writes=[xb])
            else:
                xop = P.dma("sync", f"xl{sl}", xs[:], x_d[kind * 128:(kind + 1) * 128, :], writes=[xb],
                            extra=([cnt["x0op"]] if kind in (2, 3) else []), nbytes=524288)
                if kind == 0:
                    cnt["x0op"] = xop
                load_weight_piece(xop)
            if kind != "H":
                csl = cnt["cs"] % 4
                cnt["cs"] += 1
                csb = buf(f"cs{csl}")
                ti = 0 if kind == "M" else kind + 1
                P.dma("sync", f"csl{csl}", cst[csl][:], cs_v[:, ti, :], writes=[csb])
                ctx["csl"], ctx["csb"] = csl, csb
            rs, rsb = rms_rstd(xs[:], xb, idx)
            P.op("vector", lambda e: e.scalar_tensor_tensor(out=xna, in0=xs[:], scalar=rs, in1=gbc[:],
                                                            op0=ALU.mult, op1=ALU.mult),
                 reads=[xb, rsb, buf("gbc")], writes=[xnb], n=1024)
            pT = bf_view(bT)
            P.op("tensor", [lambda e, kc=kc: e.transpose(out=pT[:, kc, :], in_=xna[:, kc * 128:(kc + 1) * 128],
                                                          identity=identb[:]) for kc in range(8)],
                 reads=[xnb, buf("identb")], writes=[bankb[bT]], c=8 * TR)
            if kind == "H":
                P.op("scalar", lambda e: e.activation(out=hnT[:, :, 0:16], in_=pT[:, :, 112:128], func=AF.Copy),
                     reads=[bankb[bT]], writes=[buf("hnT_h")], n=128)
                return None
            if kind == "M":
                hsrc, hb = hnT_m[:], buf("qkT")
            else:
                hsrc, hb = hnT[:, :, 16 + kind * 128:16 + (kind + 1) * 128], buf(f"hnT{kind}")
            P.op("scalar", lambda e: e.activation(out=hsrc, in_=pT, func=AF.Copy), reads=[bankb[bT]], writes=[hb],
                 n=1024)
            ctx["hsrc"], ctx["hb"] = hsrc, hb
            return ctx

        def p1_B(ctx):
            kind, pset = ctx["kind"], ctx["pset"]
            bK, bV, bU = 4 * pset + 1, 4 * pset + 2, 4 * pset + 3
            hsrc, hb, csl, csb = ctx["hsrc"], ctx["hb"], ctx["csl"], ctx["csb"]
            if kind == "M":
                kdst, kb = ktok_m[:], buf("qrot")
                vdst, vb = vtok_m, buf("sT")
            else:
                kdst, kb = ktok[:, kind, :], buf(f"ktok{kind}")
                vdst, vb = vtok[:, kind, :], buf(f"vtok{kind}")
            P.op("tensor", [lambda e, kc=kc: e.matmul(bank[bK][:], lhsT=hsrc[:, kc, :], rhs=wkv[:, kc, 0:512],
                                                       start=(kc == 0), stop=(kc == 7)) for kc in range(8)],
                 reads=[hb, buf("wk")], writes=[bankb[bK]], c=8 * MM512)
            P.op("tensor", [lambda e, kc=kc: e.matmul(bank[bV][:], lhsT=hsrc[:, kc, :], rhs=wkv[:, kc, 512:1024],
                                                       start=(kc == 0), stop=(kc == 7)) for kc in range(8)],
                 reads=[hb, buf("wv")], writes=[bankb[bV]], c=8 * MM512)
            rotary(bK, b_bc, csl, csb, kdst, kb)
            P.op("scalar", lambda e: e.activation(out=vdst, in_=bank[bV][:], func=AF.Copy),
                 reads=[bankb[bV]], writes=[vb], n=512)
            pU = h_view(bU)
            P.op("tensor", [lambda e, h=h: e.matmul(pU[:, h, :], lhsT=kdst[:, h * 128:(h + 1) * 128],
                                                     rhs=vdst[:, h * 128:(h + 1) * 128], start=True, stop=True)
                            for h in range(4)],
                 reads=[kb, vb], writes=[bankb[bU]], c=4 * MM128)
            if kind == "M":
                P.op("vector", lambda e: e.tensor_copy(out=U0[:], in_=pU), reads=[bankb[bU]], writes=[buf("U0")], n=512)
            elif kind == 0:
                P.op("vector", lambda e: e.tensor_copy(out=Rst[:], in_=pU), reads=[bankb[bU]], writes=[buf("R")], n=512)
            else:
                P.op("vector", [lambda e, h=h: e.scalar_tensor_tensor(out=Rst[:, h, :], in0=Rst[:, h, :],
                                                                       scalar=float(CD[h]), in1=pU[:, h, :],
                                                                       op0=ALU.mult, op1=ALU.add)
                                for h in range(4)],
                     reads=[bankb[bU], buf("R")], writes=[buf("R")], c=0.9)

        order = ["H"] + list(range(NT)) + ["M"]
        pend = []
        for i, kind in enumerate(order):
            ctx = p1_A(kind, i % 2, i)
            if ctx is not None:
                pend.append(ctx)
            if i >= 2 and pend:
                p1_B(pend.pop(0))
        while pend:
            p1_B(pend.pop(0))
        assert not wpieces

        def alias_after(dst, srcs):
            for sname in srcs:
                sb_ = buf(sname)
                dst.r = dst.r + list(sb_.r) + ([sb_.w] if sb_.w is not None else [])
        for nm in ("AG0", "AG1", "AG2", "AG3", "sgr0", "sgr1"):
            alias_after(buf(nm), ["wk", "wv"])
        alias_after(buf("mixr"), ["xn0", "xn1"])
        alias_after(buf("mixc1"), ["xt3"])
        P.dma("sync", "aloc", aloc_d[:, :], Rst[:].rearrange("p h t -> p (h t)"), reads=[buf("R")],
              writes=[buf("aloc")], nbytes=262144)
        cc_sem = st.enter_context(nc.semaphore("cc_sem"))
        P.raw("gpsimd", lambda e: e.collective_compute("AllGather", ALU.bypass,
                                                         replica_groups=[[0, 1, 2, 3], [4, 5, 6, 7]],
                                                         ins=[aloc_d[:, :]], outs=[aall_d[:, :]]),
              cc_sem, reads=[buf("aloc")], writes=[buf("aall")])
        P.dma("sync", "c_g", gbc[:], gf_d[0:1, :].broadcast_to([128, D]), writes=[buf("gbc")], nbytes=524288)
        for r in range(4):
            P.dma("sync", f"agl{r}", AG[:, r, :], aall_d[r * 128:(r + 1) * 128, :], reads=[buf("aall")],
                  writes=[buf(f"AG{r}")], nbytes=1 << 18)
        P.op("vector", [lambda e, h=h: e.tensor_scalar(out=Rst[:, h, :], in0=U0[:, h, :], scalar1=coef[:, h:h + 1],
                                                        scalar2=None, op0=ALU.mult) for h in range(4)],
             reads=[buf("U0"), buf("coef")], writes=[buf("R")], c=0.9)
        AGh = AG.rearrange("p r (h t) -> p r h t", h=4)
        for r in range(4):
            P.op("vector", [lambda e, r=r, h=h: e.scalar_tensor_tensor(out=Rst[:, h, :], in0=AGh[:, r, h, :],
                                                                        scalar=coef[:, 4 + 4 * r + h:5 + 4 * r + h],
                                                                        in1=Rst[:, h, :], op0=ALU.mult, op1=ALU.add)
                            for h in range(4)],
                 reads=[buf(f"AG{r}"), buf("coef"), buf("R")], writes=[buf("R")], c=0.9)
        Sb = [buf("S0"), buf("S1")]

        def make_S(dst_i):
            P.op("scalar", [lambda e, h=h: e.activation(out=Sbf[dst_i][:, h, :], in_=Rst[:, h, :], func=AF.Copy,
                                                         scale=float(CD[h])) for h in range(4)],
                 reads=[buf("R")], writes=[Sb[dst_i]], c=1.3)
        make_S(0)

        pA, pB_, X1, X2, O0, O1, pY, pZ = range(8)

        def proj_fm(bi, wbuf, rcol, tok0, n):
            ti = (tok0 - 16) // 128
            rd = list(wbuf) + [buf(f"hnT{t}") for t in range(ti, min(NT, ti + (n + 127) // 128))]
            P.op("tensor", [lambda e, kc=kc: e.matmul(bank[bi][:, 0:n], lhsT=wrest[:, kc, rcol:rcol + 128],
                                                       rhs=hnT[:, kc, tok0:tok0 + n], start=(kc == 0), stop=(kc == 7))
                            for kc in range(8)],
                 reads=rd, writes=[bankb[bi]], c=8 * (0.03 + (MM512 - 0.03) * n / 512.0))

        for cb_i in range(4):
            P.op("tensor", [lambda e, kc=kc, cb_i=cb_i: e.matmul(bank[pA][:, 4 * cb_i:4 * cb_i + 2],
                                                                  lhsT=wrest[:, kc, R_CC + cb_i * 128:R_CC + (cb_i + 1) * 128],
                                                                  rhs=hnT[:, kc, 14:16], start=(kc == 0), stop=(kc == 7))
                            for kc in range(8)] +
                           [lambda e, kc=kc, cb_i=cb_i: e.matmul(bank[pA][:, 4 * cb_i + 2:4 * cb_i + 4],
                                                                  lhsT=wrest[:, kc, R_CX + cb_i * 128:R_CX + (cb_i + 1) * 128],
                                                                  rhs=hnT[:, kc, 14:16], start=(kc == 0), stop=(kc == 7))
                            for kc in range(8)],
                 reads=wb("wcc") + wb("wcx") + [buf("hnT_h")], writes=[bankb[pA]], c=16 * 0.07)
        P.op("scalar", lambda e: e.activation(out=uh_sb[:], in_=bank[pA][:, 0:16], func=AF.Copy),
             reads=[bankb[pA]], writes=[buf("uh")], n=16)
        uh4 = uh_sb[:].rearrange("p (c k t) -> p c k t", c=4, k=2)
        P.op("gpsimd", lambda e: e.tensor_tensor(out=uhalo[:], in0=uh4[:, :, 0, :], in1=uh4[:, :, 1, :], op=ALU.mult),
             reads=[buf("uh")], writes=[buf("uhalo")], n=8)

        ucnt = {"n": 0}

        GROUPS = [(0, 4), (4, 4), (8, 4), (12, 2), (14, 2)]
        TILE_G = {}
        for gi, (t0g, ntg) in enumerate(GROUPS):
            for j in range(ntg):
                TILE_G[t0g + j] = (gi, j)

        def conv_unit(s, cb_i):
            tok0 = 16 + 128 * GROUPS[s][0]
            nt_ = 128 * GROUPS[s][1]
            k = ucnt["n"] % 2
            ucnt["n"] += 1
            u, uB = uw[k], buf(f"uw{k}")
            c1k, c1B = c1[k], buf(f"c1_{k}")
            tbk, tbB = tb[k], buf(f"tb_{k}")
            mx, mxB = mixc[s % 2], buf(f"mixc{s % 2}")
            cw = lambda j: convw[:, 3 * cb_i + j:3 * cb_i + j + 1]
            bA, bB = (pY, pZ) if (s == 0 and cb_i % 2 == 1) else (pA, pB_)
            proj_fm(bA, wb("wcc"), R_CC + cb_i * 128, tok0, nt_)
            P.op("scalar", lambda e: e.activation(out=c1k[:, 0:nt_], in_=bank[bA][:, 0:nt_], func=AF.Copy),
                 reads=[bankb[bA]], writes=[c1B], n=nt_)
            proj_fm(bB, wb("wcx"), R_CX + cb_i * 128, tok0, nt_)
            P.op("gpsimd", lambda e: e.tensor_copy(out=u[:, 0:2], in_=uhalo[:, cb_i, :]),
                 reads=[buf("uhalo"), uB], writes=[uB], c=0.2)
            P.op("vector", lambda e: e.tensor_tensor(out=u[:, 2:2 + nt_], in0=bank[bB][:, 0:nt_], in1=c1k[:, 0:nt_], op=ALU.mult),
                 reads=[bankb[bB], c1B, uB], writes=[uB], n=nt_)
            P.op("gpsimd", lambda e: e.tensor_copy(out=uhalo[:, cb_i, :], in_=u[:, nt_:nt_ + 2]),
                 reads=[uB, buf("uhalo")], writes=[buf("uhalo")], c=0.2)
            P.op("vector", lambda e: e.tensor_scalar(out=tbk[:, 0:nt_], in0=u[:, 0:nt_], scalar1=cw(0), scalar2=None,
                                                     op0=ALU.mult),
                 reads=[uB, buf("convw")], writes=[tbB], n=nt_)
            P.op("vector", lambda e: e.scalar_tensor_tensor(out=tbk[:, 0:nt_], in0=u[:, 1:1 + nt_], scalar=cw(1), in1=tbk[:, 0:nt_],
                                                            op0=ALU.mult, op1=ALU.add),
                 reads=[uB, buf("convw"), tbB], writes=[tbB], n=nt_)
            P.op("vector", lambda e: e.scalar_tensor_tensor(out=tbk[:, 0:nt_], in0=u[:, 2:2 + nt_], scalar=cw(2), in1=tbk[:, 0:nt_],
                                                            op0=ALU.mult, op1=ALU.add),
                 reads=[uB, buf("convw"), tbB], writes=[tbB], n=nt_)
            proj_fm(bA, wb("wcb"), R_CB + cb_i * 128, tok0, nt_)
            P.op("vector", lambda e: e.tensor_tensor(out=tbk[:, 0:nt_], in0=bank[bA][:, 0:nt_], in1=tbk[:, 0:nt_], op=ALU.mult),
                 reads=[bankb[bA], tbB], writes=[tbB], n=nt_)
            proj_fm(bB, wb("wcg"), R_CG + cb_i * 128, tok0, nt_)
            P.op("scalar", lambda e: e.activation(out=c1k[:, 0:nt_], in_=bank[bB][:, 0:nt_], func=AF.Silu),
                 reads=[bankb[bB], c1B], writes=[c1B], n=nt_)
            P.op("vector", lambda e: e.tensor_tensor(out=mx[:, cb_i, 0:nt_], in0=tbk[:, 0:nt_], in1=c1k[:, 0:nt_], op=ALU.mult),
                 reads=[tbB, c1B, mxB], writes=[mxB], n=nt_)

        def rg_unit(s, h):
            tok0 = 16 + 128 * GROUPS[s][0]
            nt_ = 128 * GROUPS[s][1]
            bk = pA if h % 2 == 0 else pB_
            proj_fm(bk, wb("wrg"), R_RG + h * 128, tok0, nt_)
            P.op("scalar", lambda e: e.activation(out=sgr[s % 2][:, h, 0:nt_], in_=bank[bk][:, 0:nt_], func=AF.Silu),
                 reads=[bankb[bk], buf(f"sgr{s % 2}")], writes=[buf(f"sgr{s % 2}")], n=nt_)

        p2 = {"slot": 0, "cs": 0}
        xinfo = {}

        def issue_reload(t):
            sl = p2["slot"] % 3
            p2["slot"] += 1
            xinfo[t] = sl
            P.dma("sync", f"xl{sl}", xt[sl][:], x_d[t * 128:(t + 1) * 128, :],
                  writes=[buf(f"xt{sl}"), buf(f"xt{sl}_h0"), buf(f"xt{sl}_h1")], nbytes=524288)
            csl = p2["cs"] % 4
            p2["cs"] += 1
            xinfo[("cs", t)] = csl
            P.dma("sync", f"csl{csl}", cst[csl][:], cs_v[:, t + 1, :], writes=[buf(f"cs{csl}")])

        out_ops = []

        def ret_A(t, cur):
            csl = xinfo[("cs", t)]
            csb = buf(f"cs{csl}")
            bO = O0 + (t % 2)
            hT = hnT[:, :, 16 + t * 128:16 + (t + 1) * 128]
            P.op("tensor", [lambda e, kc=kc: e.matmul(bank[X1][:], lhsT=hT[:, kc, :], rhs=wrest[:, kc, R_Q:R_Q + 512],
                                                       start=(kc == 0), stop=(kc == 7)) for kc in range(8)],
                 reads=[buf(f"hnT{t}")] + wb("wq"), writes=[bankb[X1]], c=8 * MM512)
            rotary(X1, a_bc, csl, csb, qrot[:], buf("qrot"))
            pT = bf_view(X1)
            P.op("tensor", [lambda e, h=h: e.transpose(out=pT[:, h, :], in_=qrot[:, h * 128:(h + 1) * 128],
                                                        identity=identb[:]) for h in range(4)] +
                           [lambda e, h=h: e.transpose(out=pT[:, 4 + h, :], in_=ktok[:, t, h * 128:(h + 1) * 128],
                                                        identity=identb[:]) for h in range(4)],
                 reads=[buf("qrot"), buf(f"ktok{t}"), buf("identb")], writes=[bankb[X1]], c=8 * TR)
            P.op("scalar", lambda e: e.activation(out=qkT[:], in_=pT, func=AF.Copy),
                 reads=[bankb[X1]], writes=[buf("qkT")], n=1024)
            pSv = h_view(X2)
            P.op("tensor", [lambda e, h=h: e.matmul(pSv[:, h, :], lhsT=qkT[:, 4 + h, :], rhs=qkT[:, h, :],
                                                     start=True, stop=True) for h in range(4)],
                 reads=[buf("qkT")], writes=[bankb[X2]], c=4 * MM128)
            mask_bc = maskT[:].unsqueeze(1).broadcast_to([128, 4, 128])
            P.op("vector", lambda e: e.tensor_tensor(out=sT[:], in0=pSv, in1=mask_bc, op=ALU.mult),
                 reads=[bankb[X2], buf("maskT")], writes=[buf("sT")], n=512)
            pOv = h_view(bO)
            fl = []
            for h in range(4):
                fl.append(lambda e, h=h: e.matmul(pOv[:, h, :], lhsT=sT[:, h, :], rhs=vtok[:, t, h * 128:(h + 1) * 128],
                                                   start=True, stop=False))
                fl.append(lambda e, h=h: e.matmul(pOv[:, h, :], lhsT=qkT[:, h, :], rhs=Sbf[cur][:, h, :],
                                                   start=False, stop=True))
            P.op("tensor", fl, reads=[buf("sT"), buf(f"vtok{t}"), buf("qkT"), Sb[cur]], writes=[bankb[bO]],
                 c=8 * MM128)
            if t < NT - 1:
                pUv = h_view(X2)
                P.op("tensor", [lambda e, h=h: e.matmul(pUv[:, h, :], lhsT=ktok[:, t, h * 128:(h + 1) * 128],
                                                         rhs=vtok[:, t, h * 128:(h + 1) * 128], start=True, stop=True)
                                for h in range(4)],
                     reads=[buf(f"ktok{t}"), buf(f"vtok{t}")], writes=[bankb[X2]], c=4 * MM128)
                P.op("vector", [lambda e, h=h: e.scalar_tensor_tensor(out=Rst[:, h, :], in0=Rst[:, h, :],
                                                                       scalar=float(CD[h]), in1=pUv[:, h, :],
                                                                       op0=ALU.mult, op1=ALU.add) for h in range(4)],
                     reads=[bankb[X2], buf("R")], writes=[buf("R")], c=0.9)
                make_S(1 - cur)

        def ret_B(t):
            s, i = TILE_G[t]
            sl = xinfo[t]
            xb = buf(f"xt{sl}")
            xs = xt[sl]
            par = t % 2
            bO = O0 + par
            pOv = h_view(bO)
            sg = sgr[s % 2]
            sgB = buf(f"sgr{s % 2}")
            mx, mxB = mixc[s % 2], buf(f"mixc{s % 2}")
            stB = buf(f"gn{par}")
            P.op("vector", [lambda e, h=h: e.bn_stats(out=bns[:, par, h, :], in_=pOv[:, h, :]) for h in range(4)],
                 reads=[bankb[bO]], writes=[buf(f"bns{par}")], c=0.9)
            P.op("vector", [lambda e, h=h: e.bn_aggr(out=mv[:, par, h, :], in_=bns[:, par, h, :]) for h in range(4)],
                 reads=[buf(f"bns{par}")], writes=[buf(f"mv{par}")], c=0.4)
            P.op("vector", lambda e: e.tensor_scalar(out=gve[:, par, :], in0=mv[:, par, :, 1], scalar1=EPS,
                                                     scalar2=None, op0=ALU.add),
                 reads=[buf(f"mv{par}")], writes=[stB], n=4)
            P.op("gpsimd", lambda e: e.tensor_tensor(out=grs[:, par, :], in0=gve[:, par, :], in1=mhalf[:], op=ALU.pow),
                 reads=[stB, buf("mhalf")], writes=[buf(f"grs{par}")], c=1.0)
            P.op("vector", lambda e: e.scalar_tensor_tensor(out=gnm[:, par, :], in0=mv[:, par, :, 0], scalar=-1.0,
                                                            in1=grs[:, par, :], op0=ALU.mult, op1=ALU.mult),
                 reads=[buf(f"mv{par}"), buf(f"grs{par}")], writes=[buf(f"gnm{par}")], n=4)
            P.op("scalar", [lambda e, h=h: e.activation(out=ybuf[:, h, :], in_=pOv[:, h, :], func=AF.Identity,
                                                         bias=gnm[:, par, h:h + 1], scale=grs[:, par, h:h + 1])
                            for h in range(4)],
                 reads=[bankb[bO], buf(f"grs{par}"), buf(f"gnm{par}")], writes=[buf("ybuf")], c=1.5)
            for half in range(2):
                bo = pY if half == 0 else pZ
                P.op("tensor", [lambda e, ec=ec, half=half, bo=bo: e.matmul(bank[bo][:], lhsT=mx[:, ec, i * 128:(i + 1) * 128],
                                                                              rhs=wout[:, ec, half * 512:(half + 1) * 512],
                                                                              start=(ec == 0), stop=False)
                                for ec in range(4)],
                     reads=[mxB] + wb("wout0") + wb("wout1"), writes=[bankb[bo]], c=4 * MM512)
            pT = bf_view(bO)
            P.op("tensor", [lambda e, h=h: e.transpose(out=pT[:, h, :], in_=ybuf[:, h, :], identity=identb[:])
                            for h in range(4)],
                 reads=[buf("ybuf"), buf("identb")], writes=[bankb[bO]], c=4 * TR)
            P.op("vector", [lambda e, h=h: e.scalar_tensor_tensor(out=mixr[:, h, i * 128:(i + 1) * 128], in0=pT[:, h, :],
                                                                   scalar=gret[:, h:h + 1],
                                                                   in1=sg[:, h, i * 128:(i + 1) * 128],
                                                                   op0=ALU.mult, op1=ALU.mult) for h in range(4)],
                 reads=[bankb[bO], buf("gret"), sgB, buf("mixr")], writes=[buf("mixr")], c=0.9)
            for half in range(2):
                bo = pY if half == 0 else pZ
                P.op("tensor", [lambda e, ec=ec, half=half, bo=bo: e.matmul(bank[bo][:], lhsT=mixr[:, ec - 4, i * 128:(i + 1) * 128],
                                                                              rhs=wout[:, ec, half * 512:(half + 1) * 512],
                                                                              start=False, stop=(ec == 7))
                                for ec in range(4, 8)],
                     reads=[buf("mixr")] + wb("wout0") + wb("wout1"), writes=[bankb[bo]], c=4 * MM512)
                P.op("vector", lambda e, half=half, bo=bo: e.tensor_tensor(out=xs[:, half * 512:(half + 1) * 512],
                                                                           in0=bank[bo][:],
                                                                           in1=xs[:, half * 512:(half + 1) * 512], op=ALU.add),
                     reads=[bankb[bo], xb], writes=[xb], n=512)
            rs, rsb = rms_rstd(xs[:], xb, 32 + t)
            for half in range(2):
                cs_ = slice(half * 512, (half + 1) * 512)
                hb = buf(f"xt{sl}_h{half}")
                P.op("vector", lambda e, cs_=cs_: e.scalar_tensor_tensor(out=xs[:, cs_], in0=xs[:, cs_], scalar=rs,
                                                                          in1=gbc[:, cs_], op0=ALU.mult, op1=ALU.mult),
                     reads=[xb, rsb, buf("gbc")], writes=[hb], n=512)
                out_ops.append(P.dma("sync", f"st{sl}_{half}", y_d[t * 128:(t + 1) * 128, cs_], xs[:, cs_], reads=[hb],
                                     nbytes=262144))

        issue_reload(0)
        issue_reload(1)
        for cb_i in range(4):
            conv_unit(0, cb_i)
        for h in range(4):
            rg_unit(0, h)
        cur = 0
        for gi, (t0g, ntg) in enumerate(GROUPS):
            nxt = gi + 1 if gi + 1 < len(GROUPS) else None
            units = ([("c", cb) for cb in range(4)] + [("r", h) for h in range(4)]) if nxt is not None else []
            per = (len(units) + 2 * ntg - 1) // (2 * ntg) if units else 0
            for j in range(ntg):
                t = t0g + j
                if t + 2 < NT:
                    issue_reload(t + 2)
                ret_A(t, cur)
                cur = 1 - cur
                for _ in range(per):
                    if units:
                        kind_u, a_u = units.pop(0)
                        (conv_unit if kind_u == "c" else rg_unit)(nxt, a_u)
                ret_B(t)
                for _ in range(per):
                    if units:
                        kind_u, a_u = units.pop(0)
                        (conv_unit if kind_u == "c" else rg_unit)(nxt, a_u)
            assert not units
        P.final = ("sync", out_ops)
        P.run()
    return nc


def _host_constants():
    h = np.arange(NH)
    gam = (1.0 - 2.0 ** (-5.0 - h)).astype(np.float64)
    idx = np.arange(128, dtype=np.float64)
    a = (128.0 ** -0.5) * gam[None, :] ** (idx[:, None] + 1.0)
    b = gam[None, :] ** (-(idx[:, None] + 1.0))
    ab = np.concatenate([a, b], axis=1).astype(np.float32)
    maskT = (idx[None, :] >= idx[:, None]).astype(np.float32)
    ident = np.eye(128, dtype=np.float32)
    half = 64
    freqs = (1.0 / (np.float32(10000.0) ** (np.arange(half, dtype=np.float32) / np.float32(half)))).astype(np.float32)
    cd = gam ** 128
    return gam, cd, ab, maskT, ident, freqs


def _cs_table(freqs, positions):
    ang = (positions.astype(np.float32)[:, None] * freqs[None, :]).astype(np.float32)
    return np.concatenate([np.cos(ang), np.sin(ang)], axis=1).astype(np.float32)


def _cs_table_all():
    L = NMETA + SEQ
    try:
        import jax
        import jax.numpy as jnp
        with jax.default_device(jax.devices("cpu")[0]):
            half = 64
            fr = 1.0 / (10000.0 ** (jnp.arange(half, dtype=jnp.float32) / half))
            ang = jnp.arange(L, dtype=jnp.int32).astype(jnp.float32)[:, None] * fr[None, :]
            tab = np.concatenate([np.asarray(jnp.cos(ang)), np.asarray(jnp.sin(ang))], axis=1)
        return np.ascontiguousarray(tab.astype(np.float32))
    except Exception:
        freqs = (1.0 / (np.float32(10000.0) ** (np.arange(64, dtype=np.float32) / np.float32(64)))).astype(np.float32)
        return _cs_table(freqs, np.arange(L))


_NC_CACHE = {}


def kernel(x, meta, norm1_g, w_in, conv_w, ret_norm_g, w_out, final_g):
    x = np.ascontiguousarray(np.asarray(x, dtype=np.float32))
    meta = np.ascontiguousarray(np.asarray(meta, dtype=np.float32))
    w_in = np.ascontiguousarray(np.asarray(w_in, dtype=np.float32))
    w_out = np.ascontiguousarray(np.asarray(w_out, dtype=np.float32))
    norm1_g = np.asarray(norm1_g, dtype=np.float32).reshape(1, D)
    final_g = np.asarray(final_g, dtype=np.float32).reshape(1, D)
    conv_w = np.asarray(conv_w, dtype=np.float32)
    ret_norm_g = np.asarray(ret_norm_g, dtype=np.float32)

    gam, cd, ab, maskT, ident, freqs = _host_constants()
    convw_l = np.ascontiguousarray(conv_w.reshape(3, 4, 128).transpose(2, 1, 0).reshape(128, 12))
    gret_l = np.ascontiguousarray(ret_norm_g.reshape(4, 128).T)

    if "nc" not in _NC_CACHE:
        _NC_CACHE["nc"] = build_nc()
    nc = _NC_CACHE["nc"]

    cs_all = _cs_table_all()
    in_maps = []
    for core in range(NCORE):
        b, c = divmod(core, 4)
        xs = x[b, c * TOK:(c + 1) * TOK, :]
        xprev = meta if c == 0 else x[b, c * TOK - 16:c * TOK, :]
        cs = np.zeros((17 * 128, 128), np.float32)
        cs[112:128] = cs_all[0:NMETA]
        cs[128:] = cs_all[NMETA + c * TOK:NMETA + (c + 1) * TOK]
        coef = np.zeros((1, 20), np.float64)
        coef[0, 0:4] = cd ** (16 * c)
        for r in range(4):
            if r < c:
                coef[0, 4 + 4 * r:8 + 4 * r] = cd ** (16 * (c - 1 - r))
        in_maps.append({
            "x": np.ascontiguousarray(xs), "xprev": np.ascontiguousarray(xprev), "meta": meta,
            "w_in": w_in, "w_out": w_out, "norm1_g": norm1_g, "final_g": final_g,
            "convw": convw_l, "gret": gret_l, "cs": cs, "ab": ab, "maskT": maskT, "ident": ident,
            "coef": coef.astype(np.float32),
        })
    res = run_bass_kernel_spmd(nc, in_maps, core_ids=list(range(NCORE)))
    out = np.empty((BATCH, SEQ, D), np.float32)
    for core in range(NCORE):
        b, c = divmod(core, 4)
        out[b, c * TOK:(c + 1) * TOK, :] = res.results[core]["y"]
    return out
```

```python
import numpy as np
from contextlib import ExitStack
import concourse.bass as bass
import concourse.mybir as mybir
from concourse.bass_utils import run_bass_kernel_spmd

F32 = mybir.dt.float32
BF16 = mybir.dt.bfloat16
ALU = mybir.AluOpType
AF = mybir.ActivationFunctionType

D = 1024
SEQ = 8192
BATCH = 2
NMETA = 16
NCORE = 8
TOK = 2048
NT = 16
NH = 4
EPS = 1e-6
ENGS = ["sync", "scalar", "vector", "gpsimd", "tensor"]

GAMMA = [1.0 - 2.0 ** (-5.0 - h) for h in range(NH)]
CD = [g ** 128 for g in GAMMA]

C_CX, C_CB, C_CC, C_CG, C_Q, C_K, C_V, C_RG = [i * 512 for i in range(8)]
R_CX, R_CB, R_CC, R_CG, R_Q, R_RG = 0, 512, 1024, 1536, 2048, 2560


class Buf:
    def __init__(self, name, excl=False):
        self.name = name
        self.w = None
        self.r = []
        self.excl = excl


def _cost(eng, n):
    if eng == "vector":
        return 0.12 + n / 900.0
    if eng == "scalar":
        return 0.22 + n / 1200.0
    if eng == "gpsimd":
        return 0.35 + n / 330.0
    return 0.25


SEM_LAT = 0.12
import os
NOSCHED = bool(int(os.environ.get("KERNEL_NOSCHED", "0")))
SCHED_CP = bool(int(os.environ.get("KERNEL_CP", "1")))
SCHED_W = int(os.environ.get("KERNEL_W", "96"))
DMA_BPUS = float(os.environ.get("KERNEL_DMA_BPUS", "150e3"))


class Prog:
    def __init__(self, nc, stack):
        self.nc = nc
        self.stack = stack
        self.ops = []
        self.esem = {e: stack.enter_context(nc.semaphore("es_" + e)) for e in ENGS if e != "sync"}
        self.dsem = {}

    def sb(self, name, shape, dt):
        return self.stack.enter_context(self.nc.sbuf_tensor(name, shape, dt))

    def ps(self, name, shape, dt):
        return self.stack.enter_context(self.nc.psum_tensor(name, shape, dt))

    def _add(self, op, reads, writes, extra):
        idx = len(self.ops)
        deps = {}
        for b in reads:
            if b.w is not None:
                deps[b.w] = "raw"
            if b.excl:
                for r in b.r:
                    deps.setdefault(r, "war")
        for b in writes:
            if b.w is not None:
                deps[b.w] = "raw"
            for r in b.r:
                deps.setdefault(r, "war")
        for x in extra:
            if x is not None:
                deps[x] = "raw"
        deps.pop(idx, None)
        op["deps"] = deps
        op["idx"] = idx
        self.ops.append(op)
        for b in reads:
            if b not in writes:
                b.r.append(idx)
        for b in writes:
            b.w = idx
            b.r = []
        return idx

    def op(self, eng, fns, reads=(), writes=(), extra=(), n=512, c=None):
        if not isinstance(fns, (list, tuple)):
            fns = [fns]
        cost = c if c is not None else _cost(eng, n)
        return self._add({"eng": eng, "fns": list(fns), "kind": "op", "busy": cost, "lat": cost}, reads, writes, extra)

    def dma(self, eng, slot, out, in_, reads=(), writes=(), extra=(), nbytes=65536):
        busy = {"sync": 0.3, "scalar": 0.8}.get(eng, 3.0)
        lat = busy + 2.0 + nbytes / DMA_BPUS
        return self._add({"eng": eng, "kind": "dma", "slot": slot, "out": out, "in_": in_, "busy": busy, "lat": lat},
                         reads, writes, extra)

    def raw(self, eng, fn, sem, reads=(), writes=(), busy=1.0, lat=50.0):
        return self._add({"eng": eng, "kind": "raw", "fn": fn, "sem": sem, "busy": busy, "lat": lat},
                         reads, writes, ())

    def schedule(self):
        ops = self.ops
        N = len(ops)
        if NOSCHED:
            order = {e: [] for e in ENGS}
            for o in ops:
                order[o["eng"]].append(o["idx"])
            self.order = order
            self.makespan = 0.0
            return order
        succ = [[] for _ in range(N)]
        for o in ops:
            for d in o["deps"]:
                succ[d].append(o["idx"])
        blevel = [0.0] * N
        for i in range(N - 1, -1, -1):
            m = 0.0
            for j in succ[i]:
                if blevel[j] > m:
                    m = blevel[j]
            blevel[i] = ops[i]["lat"] + SEM_LAT + m
        pending = {e: [] for e in ENGS}
        for o in ops:
            pending[o["eng"]].append(o["idx"])
        fin = [None] * N
        free = {e: 0.0 for e in ENGS}
        order = {e: [] for e in ENGS}
        head = {e: 0 for e in ENGS}
        done = [False] * N
        W = SCHED_W
        nleft = N
        while nleft:
            best = None
            for e in ENGS:
                lst = pending[e]
                h = head[e]
                while h < len(lst) and done[lst[h]]:
                    h += 1
                head[e] = h
                cand = None
                cnt = 0
                for j in range(h, len(lst)):
                    i = lst[j]
                    if done[i]:
                        continue
                    cnt += 1
                    if cnt > W:
                        break
                    o = ops[i]
                    est = 0.0
                    ok = True
                    for d in o["deps"]:
                        f = fin[d]
                        if f is None:
                            ok = False
                            break
                        if f + SEM_LAT > est:
                            est = f + SEM_LAT
                    if not ok:
                        continue
                    st = est if est > free[e] else free[e]
                    key = (st, -blevel[i] if SCHED_CP else i, i)
                    if cand is None or key < cand:
                        cand = key
                if cand is not None and (best is None or (cand[0], cand[2]) < (best[0], best[1])):
                    best = (cand[0], cand[2], e)
            assert best is not None, "scheduler deadlock"
            st, i, e = best
            o = ops[i]
            fin[i] = st + o["lat"]
            free[e] = st + o["busy"]
            done[i] = True
            order[e].append(i)
            nleft -= 1
        self.order = order
        self.makespan = max(f for f in fin)
        return order

    def run(self):
        order = self.schedule()
        ops = self.ops
        tok = [None] * len(ops)
        dcnt = {}
        for e in ENGS:
            cnt = 0
            for i in order[e]:
                o = ops[i]
                if o["kind"] == "op":
                    cnt += 1
                    tok[i] = (self.esem[e], cnt, "e_" + e)
                elif o["kind"] == "dma":
                    slot = o["slot"]
                    if slot not in self.dsem:
                        self.dsem[slot] = self.stack.enter_context(self.nc.semaphore("ds_" + slot))
                        dcnt[slot] = 0
                    dcnt[slot] += 16
                    tok[i] = (self.dsem[slot], dcnt[slot], "d_" + slot)
                else:
                    tok[i] = (o["sem"], 1, "r_%d" % i)

        def emit_engine(e, eng_obj):
            waited = {}
            for i in order[e]:
                o = ops[i]
                need = {}
                for d, kind in o["deps"].items():
                    od = ops[d]
                    if e == "tensor" and od["eng"] == "tensor" and od["kind"] == "op":
                        continue
                    sem, val, key = tok[d]
                    if waited.get(key, 0) >= val:
                        continue
                    if need.get(key, (None, 0))[1] < val:
                        need[key] = (sem, val)
                for key, (sem, val) in need.items():
                    waited[key] = val
                    eng_obj.wait_ge(sem, val)
                if o["kind"] == "op":
                    for f in o["fns"][:-1]:
                        f(eng_obj)
                    o["fns"][-1](eng_obj).then_inc(tok[i][0], 1)
                elif o["kind"] == "dma":
                    eng_obj.dma_start(out=o["out"], in_=o["in_"]).then_inc(tok[i][0], 16)
                else:
                    o["fn"](eng_obj).then_inc(o["sem"])

        final = getattr(self, "final", None)

        def emit_final(e, eng_obj):
            if final is not None and final[0] == e:
                seen = {}
                for d in final[1]:
                    sem, val, key = tok[d]
                    if seen.get(key, (None, 0))[1] < val:
                        seen[key] = (sem, val)
                for key, (sem, val) in seen.items():
                    eng_obj.wait_ge(sem, val)

        with self.nc.Block() as block:
            @block.sync
            def _(e):
                emit_engine("sync", e)
                emit_final("sync", e)

            @block.scalar
            def _(e):
                emit_engine("scalar", e)

            @block.vector
            def _(e):
                emit_engine("vector", e)

            @block.gpsimd
            def _(e):
                emit_engine("gpsimd", e)

            @block.tensor
            def _(e):
                emit_engine("tensor", e)


def build_nc():
    nc = bass.Bass("TRN2", target_bir_lowering=False)
    dr = lambda name, shape, kind="ExternalInput": nc.dram_tensor(name, shape, F32, kind=kind).ap()
    x_d = dr("x", [TOK, D])
    xprev_d = dr("xprev", [16, D])
    meta_d = dr("meta", [16, D])
    win_d = dr("w_in", [D, 4096])
    wout_d = dr("w_out", [D, D])
    g1_d = dr("norm1_g", [1, D])
    gf_d = dr("final_g", [1, D])
    convw_d = dr("convw", [128, 12])
    gret_d = dr("gret", [128, 4])
    cs_d = dr("cs", [17 * 128, 128])
    ab_d = dr("ab", [128, 8])
    mask_d = dr("maskT", [128, 128])
    ident_d = dr("ident", [128, 128])
    coef_d = dr("coef", [1, 20])
    y_d = dr("y", [TOK, D], kind="ExternalOutput")
    aloc_d = dr("a_loc", [128, 512], kind="Internal")
    aall_d = dr("a_all", [4 * 128, 512], kind="Internal")

    win_v = win_d.rearrange("(k p) n -> p k n", p=128)
    wout_v = wout_d.rearrange("(k p) n -> p k n", p=128)
    cs_v = cs_d.rearrange("(t p) f -> p t f", p=128)

    with ExitStack() as st:
        P = Prog(nc, st)
        wkv = P.sb("wkv", [128, 8, 1024], BF16)
        wrest = P.sb("wrest", [128, 8, 3072], BF16)
        wout = P.sb("wout", [128, 8, 1024], BF16)
        hnT = P.sb("hnT", [128, 8, 16 + TOK], BF16)
        ktok = P.sb("ktok", [128, NT, 512], BF16)
        vtok = P.sb("vtok", [128, NT, 512], BF16)
        xt = [P.sb(f"xt{i}", [128, D], F32) for i in range(3)]
        xn_mixr = P.sb("xn_mixr", [128, 2048], BF16)
        junk = P.sb("junk", [128, D], BF16)
        gbc = P.sb("gbc", [128, D], F32)
        cst = [P.sb(f"cs{i}", [128, 128], F32) for i in range(4)]
        ks = P.sb("ks", [128, 512], F32)
        rA = P.sb("rA", [128, 512], F32)
        rB = P.sb("rB", [128, 512], F32)
        qrot = P.sb("qrot", [128, 512], BF16)
        qkT = P.sb("qkT", [128, 8, 128], BF16)
        sT = P.sb("sT", [128, 4, 128], BF16)
        ybuf = P.sb("ybuf", [128, 4, 128], BF16)
        c1 = [P.sb(f"c1_{i}", [128, 512], F32) for i in range(2)]
        tb = [P.sb(f"tb_{i}", [128, 512], F32) for i in range(2)]
        uw = [P.sb(f"uw{i}", [128, 514], F32) for i in range(2)]
        uhalo = P.sb("uhalo", [128, 4, 2], F32)
        mixc = [P.sb(f"mixc{i}", [128, 4, 512], BF16) for i in range(2)]
        Rst = P.sb("Rst", [128, 4, 128], F32)
        U0 = P.sb("U0", [128, 4, 128], F32)
        Sbf = [P.sb(f"Sbf{i}", [128, 4, 128], BF16) for i in range(2)]
        identb = P.sb("identb", [128, 128], BF16)
        maskT = P.sb("maskT_sb", [128, 128], F32)
        ab = P.sb("ab_sb", [128, 8], F32)
        coef = P.sb("coef_sb", [128, 20], F32)
        convw = P.sb("convw_sb", [128, 12], F32)
        gret = P.sb("gret_sb", [128, 4], F32)
        mhalf = P.sb("mhalf", [128, 4], F32)
        ssq = P.sb("ssq", [128, 64], F32)
        ms = P.sb("ms", [128, 64], F32)
        rstd = P.sb("rstd", [128, 64], F32)
        cdb = P.sb("cdb", [128, 4], F32)
        bns = P.sb("bns", [128, 2, 4, 6], F32)
        mv = P.sb("mv", [128, 2, 4, 2], F32)
        gve = P.sb("gve", [128, 2, 4], F32)
        grs = P.sb("grs", [128, 2, 4], F32)
        gnm = P.sb("gnm", [128, 2, 4], F32)
        uh_sb = P.sb("uh_sb", [128, 16], F32)

        identf = junk[:, 0:256].bitcast(F32)
        xt_p1 = xt + [mixc[1][:].rearrange("p a b -> p (a b)").bitcast(F32)]
        xn = [xn_mixr[:, 0:1024], xn_mixr[:, 1024:2048]]
        mixr = xn_mixr[:].rearrange("p (h t) -> p h t", h=4)
        wkv_flat = wkv[:].rearrange("p a b -> p (a b)")
        AG = wkv_flat[:, 0:4096].bitcast(F32).rearrange("p (r f) -> p r f", r=4)
        sgr = [wkv_flat[:, 4096:6144].rearrange("p (h t) -> p h t", h=4),
               wkv_flat[:, 6144:8192].rearrange("p (h t) -> p h t", h=4)]
        hnT_m = qkT
        ktok_m = qrot
        vtok_m = sT[:].rearrange("p a b -> p (a b)")

        bank = [P.ps(f"bank{i}", [128, 512], F32) for i in range(8)]
        bankb = [Buf(f"bank{i}", excl=True) for i in range(8)]

        def bf_view(i):
            return bank[i][:].bitcast(BF16).rearrange("p (k t) -> p k t", k=8)

        def h_view(i):
            return bank[i][:].rearrange("p (h t) -> p h t", h=4)

        B = {}

        def buf(name):
            if name not in B:
                B[name] = Buf(name)
            return B[name]

        MM512 = 0.27
        MM128 = 0.09
        TR = 0.12

        P.dma("sync", "c_ident", identf, ident_d[:, :], writes=[buf("junk")])
        P.dma("sync", "c_g", gbc[:], g1_d[0:1, :].broadcast_to([128, D]), writes=[buf("gbc")], nbytes=524288)
        P.dma("scalar", "c_ab", ab[:], ab_d[:, :], writes=[buf("ab")])
        P.dma("scalar", "c_mask", maskT[:], mask_d[:, :], writes=[buf("maskT")])
        P.dma("scalar", "c_coef", coef[:], coef_d[0:1, :].broadcast_to([128, 20]), writes=[buf("coef")])
        P.dma("scalar", "c_convw", convw[:], convw_d[:, :], writes=[buf("convw")])
        P.dma("scalar", "c_gret", gret[:], gret_d[:, :], writes=[buf("gret")])
        P.dma("gpsimd", "w_k", wkv[:, :, 0:512], win_v[:, :, C_K:C_K + 512], writes=[buf("wk")], nbytes=2 << 20)
        P.dma("gpsimd", "w_v", wkv[:, :, 512:1024], win_v[:, :, C_V:C_V + 512], writes=[buf("wv")], nbytes=2 << 20)
        P.op("gpsimd", lambda e: e.memset(mhalf[:], -0.5), writes=[buf("mhalf")], n=4)
        P.op("gpsimd", [lambda e, h=h: e.memset(cdb[:, h:h + 1], float(CD[h])) for h in range(4)],
             writes=[buf("cdb")], n=16)
        P.op("vector", lambda e: e.tensor_copy(out=identb[:], in_=identf), reads=[buf("junk")],
             writes=[buf("identb")], n=128)
        wpieces = []
        wparts = {}

        def add_piece(nm, dst, src, nb):
            key = nm + "_%d" % len(wparts.setdefault(nm, []))
            wparts[nm].append(buf(key))
            wpieces.append((key, dst, src, nb))

        for (nm, ro, co) in [("wcc", R_CC, C_CC), ("wcx", R_CX, C_CX), ("wcb", R_CB, C_CB), ("wcg", R_CG, C_CG)]:
            for qk in range(4):
                add_piece(nm, wrest[:, 2 * qk:2 * qk + 2, ro:ro + 512], win_v[:, 2 * qk:2 * qk + 2, co:co + 512], 1 << 19)
        n_paced = len(wpieces)
        for (nm, ro, co) in [("wrg", R_RG, C_RG), ("wq", R_Q, C_Q)]:
            for hk in range(2):
                add_piece(nm, wrest[:, 4 * hk:4 * hk + 4, ro:ro + 512], win_v[:, 4 * hk:4 * hk + 4, co:co + 512], 1 << 20)
        for half in range(2):
            for hk in range(2):
                add_piece(f"wout{half}", wout[:, 4 * hk:4 * hk + 4, half * 512:(half + 1) * 512],
                          wout_v[:, 4 * hk:4 * hk + 4, half * 512:(half + 1) * 512], 1 << 20)
        assert n_paced == NT

        def wb(nm):
            return list(wparts[nm])

        wstate = {"n": 0}

        def load_weight_piece(after_op):
            key, dst, src, nb = wpieces.pop(0)
            P.dma("gpsimd", "w_" + key, dst, src, writes=[buf(key)], extra=[after_op], nbytes=nb)
            wstate["n"] += 1
            if wstate["n"] == NT:
                while wpieces:
                    key, dst, src, nb = wpieces.pop(0)
                    P.dma("gpsimd", "w_" + key, dst, src, writes=[buf(key)], extra=[after_op], nbytes=nb)

        cnt = {"slot": 0, "cs": 0}

        def rms_rstd(src_ap, srcbuf, col, extra_reads=()):
            P.op("scalar", lambda e: e.activation(out=junk[:], in_=src_ap, func=AF.Square,
                                                  accum_out=ssq[:, col:col + 1]),
                 reads=[srcbuf] + list(extra_reads), writes=[buf(f"ssq{col}"), buf("junk")], n=1024)
            P.op("scalar", lambda e: e.activation(out=ms[:, col:col + 1], in_=ssq[:, col:col + 1], func=AF.Identity,
                                                  scale=1.0 / D, bias=EPS),
                 reads=[buf(f"ssq{col}")], writes=[buf(f"ms{col}")], n=1)
            P.op("gpsimd", lambda e: e.tensor_tensor(out=rstd[:, col:col + 1], in0=ms[:, col:col + 1],
                                                     in1=mhalf[:, 0:1], op=ALU.pow),
                 reads=[buf(f"ms{col}"), buf("mhalf")], writes=[buf(f"rstd{col}")], c=1.0)
            return rstd[:, col:col + 1], buf(f"rstd{col}")

        def rotary(psum_i, scale_bc, csl, csbuf, dst_ap, dstbuf):
            pv = h_view(psum_i)
            ks4 = ks[:].rearrange("p (h t f) -> p h t f", h=4, t=2)
            a4 = rA[:].rearrange("p (h t f) -> p h t f", h=4, t=2)
            b4 = rB[:].rearrange("p (h t f) -> p h t f", h=4, t=2)
            d4 = dst_ap.rearrange("p (h t f) -> p h t f", h=4, t=2)
            cosb = cst[csl][:, 0:64].unsqueeze(1).unsqueeze(1).broadcast_to([128, 4, 2, 64])
            sinb = cst[csl][:, 64:128].unsqueeze(1).unsqueeze(1).broadcast_to([128, 4, 2, 64])
            ksh = ks[:].rearrange("p (h t) -> p h t", h=4)
            P.op("vector", lambda e: e.tensor_tensor(out=ksh, in0=pv, in1=scale_bc, op=ALU.mult),
                 reads=[bankb[psum_i], buf("ab")], writes=[buf("ks")], n=512)
            P.op("vector", lambda e: e.tensor_tensor(out=a4, in0=ks4, in1=cosb, op=ALU.mult),
                 reads=[buf("ks"), csbuf], writes=[buf("rA")], n=512)
            P.op("vector", lambda e: e.tensor_tensor(out=b4, in0=ks4[:, :, ::-1, :], in1=sinb, op=ALU.mult),
                 reads=[buf("ks"), csbuf], writes=[buf("rB")], n=512)
            P.op("vector", lambda e: e.tensor_tensor(out=d4[:, :, 0, :], in0=a4[:, :, 0, :], in1=b4[:, :, 0, :],
                                                     op=ALU.subtract),
                 reads=[buf("rA"), buf("rB")], writes=[dstbuf], n=256)
            P.op("gpsimd", lambda e: e.tensor_tensor(out=d4[:, :, 1, :], in0=a4[:, :, 1, :], in1=b4[:, :, 1, :],
                                                     op=ALU.add),
                 reads=[buf("rA"), buf("rB"), dstbuf], writes=[dstbuf], n=256)

        a_bc = ab[:, 0:4].unsqueeze(2).broadcast_to([128, 4, 128])
        b_bc = ab[:, 4:8].unsqueeze(2).broadcast_to([128, 4, 128])

        def p1_A(kind, pset, idx):
            bT = 4 * pset
            sl = cnt["slot"] % 4
            cnt["slot"] += 1
            xb = buf(f"xt{sl}")
            xs_t = xt_p1[sl]
            xs = xs_t if sl == 3 else xs_t[:]
            xsel = cnt["slot"] % 2
            xnb = buf(f"xn{xsel}")
            xna = xn[xsel]
            ctx = {"kind": kind, "pset": pset}
            if kind in ("M", "H"):
                P.op("vector", lambda e: e.memset(xs[:], 0.0), writes=[xb], c=0.6)
                src = meta_d if kind == "M" else xprev_d
                P.dma("sync", f"xl{sl}", xs[112:128, :], src[:, :], reads=[xb], writes=[xb])
            else:
                xop = P.dma("sync", f"xl{sl}", xs[:], x_d[kind * 128:(kind + 1) * 128, :], writes=[xb],
                            extra=([cnt["x0op"]] if kind in (2, 3) else []), nbytes=524288)
                if kind == 0:
                    cnt["x0op"] = xop
                load_weight_piece(xop)
            if kind != "H":
                csl = cnt["cs"] % 4
                cnt["cs"] += 1
                csb = buf(f"cs{csl}")
                ti = 0 if kind == "M" else kind + 1
                P.dma("sync", f"csl{csl}", cst[csl][:], cs_v[:, ti, :], writes=[csb])
                ctx["csl"], ctx["csb"] = csl, csb
            rs, rsb = rms_rstd(xs[:], xb, idx)
            P.op("vector", lambda e: e.scalar_tensor_tensor(out=xna, in0=xs[:], scalar=rs, in1=gbc[:],
                                                            op0=ALU.mult, op1=ALU.mult),
                 reads=[xb, rsb, buf("gbc")], writes=[xnb], n=1024)
            pT = bf_view(bT)
            P.op("tensor", [lambda e, kc=kc: e.transpose(out=pT[:, kc, :], in_=xna[:, kc * 128:(kc + 1) * 128],
                                                          identity=identb[:]) for kc in range(8)],
                 reads=[xnb, buf("identb")], writes=[bankb[bT]], c=8 * TR)
            if kind == "H":
                P.op("scalar", lambda e: e.activation(out=hnT[:, :, 0:16], in_=pT[:, :, 112:128], func=AF.Copy),
                     reads=[bankb[bT]], writes=[buf("hnT_h")], n=128)
                return None
            if kind == "M":
                hsrc, hb = hnT_m[:], buf("qkT")
            else:
                hsrc, hb = hnT[:, :, 16 + kind * 128:16 + (kind + 1) * 128], buf(f"hnT{kind}")
            P.op("scalar", lambda e: e.activation(out=hsrc, in_=pT, func=AF.Copy), reads=[bankb[bT]], writes=[hb],
                 n=1024)
            ctx["hsrc"], ctx["hb"] = hsrc, hb
            return ctx

        def p1_B(ctx):
            kind, pset = ctx["kind"], ctx["pset"]
            bK, bV, bU = 4 * pset + 1, 4 * pset + 2, 4 * pset + 3
            hsrc, hb, csl, csb = ctx["hsrc"], ctx["hb"], ctx["csl"], ctx["csb"]
            if kind == "M":
                kdst, kb = ktok_m[:], buf("qrot")
                vdst, vb = vtok_m, buf("sT")
            else:
                kdst, kb = ktok[:, kind, :], buf(f"ktok{kind}")
                vdst, vb = vtok[:, kind, :], buf(f"vtok{kind}")
            P.op("tensor", [lambda e, kc=kc: e.matmul(bank[bK][:], lhsT=hsrc[:, kc, :], rhs=wkv[:, kc, 0:512],
                                                       start=(kc == 0), stop=(kc == 7)) for kc in range(8)],
                 reads=[hb, buf("wk")], writes=[bankb[bK]], c=8 * MM512)
            P.op("tensor", [lambda e, kc=kc: e.matmul(bank[bV][:], lhsT=hsrc[:, kc, :], rhs=wkv[:, kc, 512:1024],
                                                       start=(kc == 0), stop=(kc == 7)) for kc in range(8)],
                 reads=[hb, buf("wv")], writes=[bankb[bV]], c=8 * MM512)
            rotary(bK, b_bc, csl, csb, kdst, kb)
            P.op("scalar", lambda e: e.activation(out=vdst, in_=bank[bV][:], func=AF.Copy),
                 reads=[bankb[bV]], writes=[vb], n=512)
            pU = h_view(bU)
            P.op("tensor", [lambda e, h=h: e.matmul(pU[:, h, :], lhsT=kdst[:, h * 128:(h + 1) * 128],
                                                     rhs=vdst[:, h * 128:(h + 1) * 128], start=True, stop=True)
                            for h in range(4)],
                 reads=[kb, vb], writes=[bankb[bU]], c=4 * MM128)
            if kind == "M":
                P.op("vector", lambda e: e.tensor_copy(out=U0[:], in_=pU), reads=[bankb[bU]], writes=[buf("U0")], n=512)
            elif kind == 0:
                P.op("vector", lambda e: e.tensor_copy(out=Rst[:], in_=pU), reads=[bankb[bU]], writes=[buf("R")], n=512)
            else:
                P.op("vector", [lambda e, h=h: e.scalar_tensor_tensor(out=Rst[:, h, :], in0=Rst[:, h, :],
                                                                       scalar=float(CD[h]), in1=pU[:, h, :],
                                                                       op0=ALU.mult, op1=ALU.add)
                                for h in range(4)],
                     reads=[bankb[bU], buf("R")], writes=[buf("R")], c=0.9)

        order = ["H"] + list(range(NT)) + ["M"]
        pend = []
        for i, kind in enumerate(order):
            ctx = p1_A(kind, i % 2, i)
            if ctx is not None:
                pend.append(ctx)
            if i >= 2 and pend:
                p1_B(pend.pop(0))
        while pend:
            p1_B(pend.pop(0))
        assert not wpieces

        def alias_after(dst, srcs):
            for sname in srcs:
                sb_ = buf(sname)
                dst.r = dst.r + list(sb_.r) + ([sb_.w] if sb_.w is not None else [])
        for nm in ("AG0", "AG1", "AG2", "AG3", "sgr0", "sgr1"):
            alias_after(buf(nm), ["wk", "wv"])
        alias_after(buf("mixr"), ["xn0", "xn1"])
        alias_after(buf("mixc1"), ["xt3"])
        P.dma("sync", "aloc", aloc_d[:, :], Rst[:].rearrange("p h t -> p (h t)"), reads=[buf("R")],
              writes=[buf("aloc")], nbytes=262144)
        cc_sem = st.enter_context(nc.semaphore("cc_sem"))
        P.raw("gpsimd", lambda e: e.collective_compute("AllGather", ALU.bypass,
                                                         replica_groups=[[0, 1, 2, 3], [4, 5, 6, 7]],
                                                         ins=[aloc_d[:, :]], outs=[aall_d[:, :]]),
              cc_sem, reads=[buf("aloc")], writes=[buf("aall")])
        P.dma("sync", "c_g", gbc[:], gf_d[0:1, :].broadcast_to([128, D]), writes=[buf("gbc")], nbytes=524288)
        for r in range(4):
            P.dma("sync", f"agl{r}", AG[:, r, :], aall_d[r * 128:(r + 1) * 128, :], reads=[buf("aall")],
                  writes=[buf(f"AG{r}")], nbytes=1 << 18)
        P.op("vector", [lambda e, h=h: e.tensor_scalar(out=Rst[:, h, :], in0=U0[:, h, :], scalar1=coef[:, h:h + 1],
                                                        scalar2=None, op0=ALU.mult) for h in range(4)],
             reads=[buf("U0"), buf("coef")], writes=[buf("R")], c=0.9)
        AGh = AG.rearrange("p r (h t) -> p r h t", h=4)
        for r in range(4):
            P.op("vector", [lambda e, r=r, h=h: e.scalar_tensor_tensor(out=Rst[:, h, :], in0=AGh[:, r, h, :],
                                                                        scalar=coef[:, 4 + 4 * r + h:5 + 4 * r + h],
                                                                        in1=Rst[:, h, :], op0=ALU.mult, op1=ALU.add)
                            for h in range(4)],
                 reads=[buf(f"AG{r}"), buf("coef"), buf("R")], writes=[buf("R")], c=0.9)
        Sb = [buf("S0"), buf("S1")]

        def make_S(dst_i):
            P.op("scalar", [lambda e, h=h: e.activation(out=Sbf[dst_i][:, h, :], in_=Rst[:, h, :], func=AF.Copy,
                                                         scale=float(CD[h])) for h in range(4)],
                 reads=[buf("R")], writes=[Sb[dst_i]], c=1.3)
        make_S(0)

        pA, pB_, X1, X2, O0, O1, pY, pZ = range(8)

        def proj_fm(bi, wbuf, rcol, tok0, n):
            ti = (tok0 - 16) // 128
            rd = list(wbuf) + [buf(f"hnT{t}") for t in range(ti, min(NT, ti + (n + 127) // 128))]
            P.op("tensor", [lambda e, kc=kc: e.matmul(bank[bi][:, 0:n], lhsT=wrest[:, kc, rcol:rcol + 128],
                                                       rhs=hnT[:, kc, tok0:tok0 + n], start=(kc == 0), stop=(kc == 7))
                            for kc in range(8)],
                 reads=rd, writes=[bankb[bi]], c=8 * (0.03 + (MM512 - 0.03) * n / 512.0))

        for cb_i in range(4):
            P.op("tensor", [lambda e, kc=kc, cb_i=cb_i: e.matmul(bank[pA][:, 4 * cb_i:4 * cb_i + 2],
                                                                  lhsT=wrest[:, kc, R_CC + cb_i * 128:R_CC + (cb_i + 1) * 128],
                                                                  rhs=hnT[:, kc, 14:16], start=(kc == 0), stop=(kc == 7))
                            for kc in range(8)] +
                           [lambda e, kc=kc, cb_i=cb_i: e.matmul(bank[pA][:, 4 * cb_i + 2:4 * cb_i + 4],
                                                                  lhsT=wrest[:, kc, R_CX + cb_i * 128:R_CX + (cb_i + 1) * 128],
                                                                  rhs=hnT[:, kc, 14:16], start=(kc == 0), stop=(kc == 7))
                            for kc in range(8)],
                 reads=wb("wcc") + wb("wcx") + [buf("hnT_h")], writes=[bankb[pA]], c=16 * 0.07)
        P.op("scalar", lambda e: e.activation(out=uh_sb[:], in_=bank[pA][:, 0:16], func=AF.Copy),
             reads=[bankb[pA]], writes=[buf("uh")], n=16)
        uh4 = uh_sb[:].rearrange("p (c k t) -> p c k t", c=4, k=2)
        P.op("gpsimd", lambda e: e.tensor_tensor(out=uhalo[:], in0=uh4[:, :, 0, :], in1=uh4[:, :, 1, :], op=ALU.mult),
             reads=[buf("uh")], writes=[buf("uhalo")], n=8)

        ucnt = {"n": 0}

        GROUPS = [(0, 4), (4, 4), (8, 4), (12, 2), (14, 2)]
        TILE_G = {}
        for gi, (t0g, ntg) in enumerate(GROUPS):
            for j in range(ntg):
                TILE_G[t0g + j] = (gi, j)

        def conv_unit(s, cb_i):
            tok0 = 16 + 128 * GROUPS[s][0]
            nt_ = 128 * GROUPS[s][1]
            k = ucnt["n"] % 2
            ucnt["n"] += 1
            u, uB = uw[k], buf(f"uw{k}")
            c1k, c1B = c1[k], buf(f"c1_{k}")
            tbk, tbB = tb[k], buf(f"tb_{k}")
            mx, mxB = mixc[s % 2], buf(f"mixc{s % 2}")
            cw = lambda j: convw[:, 3 * cb_i + j:3 * cb_i + j + 1]
            bA, bB = (pY, pZ) if (s == 0 and cb_i % 2 == 1) else (pA, pB_)
            proj_fm(bA, wb("wcc"), R_CC + cb_i * 128, tok0, nt_)
            P.op("scalar", lambda e: e.activation(out=c1k[:, 0:nt_], in_=bank[bA][:, 0:nt_], func=AF.Copy),
                 reads=[bankb[bA]], writes=[c1B], n=nt_)
            proj_fm(bB, wb("wcx"), R_CX + cb_i * 128, tok0, nt_)
            P.op("gpsimd", lambda e: e.tensor_copy(out=u[:, 0:2], in_=uhalo[:, cb_i, :]),
                 reads=[buf("uhalo"), uB], writes=[uB], c=0.2)
            P.op("vector", lambda e: e.tensor_tensor(out=u[:, 2:2 + nt_], in0=bank[bB][:, 0:nt_], in1=c1k[:, 0:nt_], op=ALU.mult),
                 reads=[bankb[bB], c1B, uB], writes=[uB], n=nt_)
            P.op("gpsimd", lambda e: e.tensor_copy(out=uhalo[:, cb_i, :], in_=u[:, nt_:nt_ + 2]),
                 reads=[uB, buf("uhalo")], writes=[buf("uhalo")], c=0.2)
            P.op("vector", lambda e: e.tensor_scalar(out=tbk[:, 0:nt_], in0=u[:, 0:nt_], scalar1=cw(0), scalar2=None,
                                                     op0=ALU.mult),
                 reads=[uB, buf("convw")], writes=[tbB], n=nt_)
            P.op("vector", lambda e: e.scalar_tensor_tensor(out=tbk[:, 0:nt_], in0=u[:, 1:1 + nt_], scalar=cw(1), in1=tbk[:, 0:nt_],
                                                            op0=ALU.mult, op1=ALU.add),
                 reads=[uB, buf("convw"), tbB], writes=[tbB], n=nt_)
            P.op("vector", lambda e: e.scalar_tensor_tensor(out=tbk[:, 0:nt_], in0=u[:, 2:2 + nt_], scalar=cw(2), in1=tbk[:, 0:nt_],
                                                            op0=ALU.mult, op1=ALU.add),
                 reads=[uB, buf("convw"), tbB], writes=[tbB], n=nt_)
            proj_fm(bA, wb("wcb"), R_CB + cb_i * 128, tok0, nt_)
            P.op("vector", lambda e: e.tensor_tensor(out=tbk[:, 0:nt_], in0=bank[bA][:, 0:nt_], in1=tbk[:, 0:nt_], op=ALU.mult),
                 reads=[bankb[bA], tbB], writes=[tbB], n=nt_)
            proj_fm(bB, wb("wcg"), R_CG + cb_i * 128, tok0, nt_)
            P.op("scalar", lambda e: e.activation(out=c1k[:, 0:nt_], in_=bank[bB][:, 0:nt_], func=AF.Silu),
                 reads=[bankb[bB], c1B], writes=[c1B], n=nt_)
            P.op("vector", lambda e: e.tensor_tensor(out=mx[:, cb_i, 0:nt_], in0=tbk[:, 0:nt_], in1=c1k[:, 0:nt_], op=ALU.mult),
                 reads=[tbB, c1B, mxB], writes=[mxB], n=nt_)

        def rg_unit(s, h):
            tok0 = 16 + 128 * GROUPS[s][0]
            nt_ = 128 * GROUPS[s][1]
            bk = pA if h % 2 == 0 else pB_
            proj_fm(bk, wb("wrg"), R_RG + h * 128, tok0, nt_)
            P.op("scalar", lambda e: e.activation(out=sgr[s % 2][:, h, 0:nt_], in_=bank[bk][:, 0:nt_], func=AF.Silu),
                 reads=[bankb[bk], buf(f"sgr{s % 2}")], writes=[buf(f"sgr{s % 2}")], n=nt_)

        p2 = {"slot": 0, "cs": 0}
        xinfo = {}

        def issue_reload(t):
            sl = p2["slot"] % 3
            p2["slot"] += 1
            xinfo[t] = sl
            P.dma("sync", f"xl{sl}", xt[sl][:], x_d[t * 128:(t + 1) * 128, :],
                  writes=[buf(f"xt{sl}"), buf(f"xt{sl}_h0"), buf(f"xt{sl}_h1")], nbytes=524288)
            csl = p2["cs"] % 4
            p2["cs"] += 1
            xinfo[("cs", t)] = csl
            P.dma("sync", f"csl{csl}", cst[csl][:], cs_v[:, t + 1, :], writes=[buf(f"cs{csl}")])

        out_ops = []

        def ret_A(t, cur):
            csl = xinfo[("cs", t)]
            csb = buf(f"cs{csl}")
            bO = O0 + (t % 2)
            hT = hnT[:, :, 16 + t * 128:16 + (t + 1) * 128]
            P.op("tensor", [lambda e, kc=kc: e.matmul(bank[X1][:], lhsT=hT[:, kc, :], rhs=wrest[:, kc, R_Q:R_Q + 512],
                                                       start=(kc == 0), stop=(kc == 7)) for kc in range(8)],
                 reads=[buf(f"hnT{t}")] + wb("wq"), writes=[bankb[X1]], c=8 * MM512)
            rotary(X1, a_bc, csl, csb, qrot[:], buf("qrot"))
            pT = bf_view(X1)
            P.op("tensor", [lambda e, h=h: e.transpose(out=pT[:, h, :], in_=qrot[:, h * 128:(h + 1) * 128],
                                                        identity=identb[:]) for h in range(4)] +
                           [lambda e, h=h: e.transpose(out=pT[:, 4 + h, :], in_=ktok[:, t, h * 128:(h + 1) * 128],
                                                        identity=identb[:]) for h in range(4)],
                 reads=[buf("qrot"), buf(f"ktok{t}"), buf("identb")], writes=[bankb[X1]], c=8 * TR)
            P.op("scalar", lambda e: e.activation(out=qkT[:], in_=pT, func=AF.Copy),
                 reads=[bankb[X1]], writes=[buf("qkT")], n=1024)
            pSv = h_view(X2)
            P.op("tensor", [lambda e, h=h: e.matmul(pSv[:, h, :], lhsT=qkT[:, 4 + h, :], rhs=qkT[:, h, :],
                                                     start=True, stop=True) for h in range(4)],
                 reads=[buf("qkT")], writes=[bankb[X2]], c=4 * MM128)
            mask_bc = maskT[:].unsqueeze(1).broadcast_to([128, 4, 128])
            P.op("vector", lambda e: e.tensor_tensor(out=sT[:], in0=pSv, in1=mask_bc, op=ALU.mult),
                 reads=[bankb[X2], buf("maskT")], writes=[buf("sT")], n=512)
            pOv = h_view(bO)
            fl = []
            for h in range(4):
                fl.append(lambda e, h=h: e.matmul(pOv[:, h, :], lhsT=sT[:, h, :], rhs=vtok[:, t, h * 128:(h + 1) * 128],
                                                   start=True, stop=False))
                fl.append(lambda e, h=h: e.matmul(pOv[:, h, :], lhsT=qkT[:, h, :], rhs=Sbf[cur][:, h, :],
                                                   start=False, stop=True))
            P.op("tensor", fl, reads=[buf("sT"), buf(f"vtok{t}"), buf("qkT"), Sb[cur]], writes=[bankb[bO]],
                 c=8 * MM128)
            if t < NT - 1:
                pUv = h_view(X2)
                P.op("tensor", [lambda e, h=h: e.matmul(pUv[:, h, :], lhsT=ktok[:, t, h * 128:(h + 1) * 128],
                                                         rhs=vtok[:, t, h * 128:(h + 1) * 128], start=True, stop=True)
                                for h in range(4)],
                     reads=[buf(f"ktok{t}"), buf(f"vtok{t}")], writes=[bankb[X2]], c=4 * MM128)
                P.op("vector", [lambda e, h=h: e.scalar_tensor_tensor(out=Rst[:, h, :], in0=Rst[:, h, :],
                                                                       scalar=float(CD[h]), in1=pUv[:, h, :],
                                                                       op0=ALU.mult, op1=ALU.add) for h in range(4)],
                     reads=[bankb[X2], buf("R")], writes=[buf("R")], c=0.9)
                make_S(1 - cur)

        def ret_B(t):
            s, i = TILE_G[t]
            sl = xinfo[t]
            xb = buf(f"xt{sl}")
            xs = xt[sl]
            par = t % 2
            bO = O0 + par
            pOv = h_view(bO)
            sg = sgr[s % 2]
            sgB = buf(f"sgr{s % 2}")
            mx, mxB = mixc[s % 2], buf(f"mixc{s % 2}")
            stB = buf(f"gn{par}")
            P.op("vector", [lambda e, h=h: e.bn_stats(out=bns[:, par, h, :], in_=pOv[:, h, :]) for h in range(4)],
                 reads=[bankb[bO]], writes=[buf(f"bns{par}")], c=0.9)
            P.op("vector", [lambda e, h=h: e.bn_aggr(out=mv[:, par, h, :], in_=bns[:, par, h, :]) for h in range(4)],
                 reads=[buf(f"bns{par}")], writes=[buf(f"mv{par}")], c=0.4)
            P.op("vector", lambda e: e.tensor_scalar(out=gve[:, par, :], in0=mv[:, par, :, 1], scalar1=EPS,
                                                     scalar2=None, op0=ALU.add),
                 reads=[buf(f"mv{par}")], writes=[stB], n=4)
            P.op("gpsimd", lambda e: e.tensor_tensor(out=grs[:, par, :], in0=gve[:, par, :], in1=mhalf[:], op=ALU.pow),
                 reads=[stB, buf("mhalf")], writes=[buf(f"grs{par}")], c=1.0)
            P.op("vector", lambda e: e.scalar_tensor_tensor(out=gnm[:, par, :], in0=mv[:, par, :, 0], scalar=-1.0,
                                                            in1=grs[:, par, :], op0=ALU.mult, op1=ALU.mult),
                 reads=[buf(f"mv{par}"), buf(f"grs{par}")], writes=[buf(f"gnm{par}")], n=4)
            P.op("scalar", [lambda e, h=h: e.activation(out=ybuf[:, h, :], in_=pOv[:, h, :], func=AF.Identity,
                                                         bias=gnm[:, par, h:h + 1], scale=grs[:, par, h:h + 1])
                            for h in range(4)],
                 reads=[bankb[bO], buf(f"grs{par}"), buf(f"gnm{par}")], writes=[buf("ybuf")], c=1.5)
            for half in range(2):
                bo = pY if half == 0 else pZ
                P.op("tensor", [lambda e, ec=ec, half=half, bo=bo: e.matmul(bank[bo][:], lhsT=mx[:, ec, i * 128:(i + 1) * 128],
                                                                              rhs=wout[:, ec, half * 512:(half + 1) * 512],
                                                                              start=(ec == 0), stop=False)
                                for ec in range(4)],
                     reads=[mxB] + wb("wout0") + wb("wout1"), writes=[bankb[bo]], c=4 * MM512)
            pT = bf_view(bO)
            P.op("tensor", [lambda e, h=h: e.transpose(out=pT[:, h, :], in_=ybuf[:, h, :], identity=identb[:])
                            for h in range(4)],
                 reads=[buf("ybuf"), buf("identb")], writes=[bankb[bO]], c=4 * TR)
            P.op("vector", [lambda e, h=h: e.scalar_tensor_tensor(out=mixr[:, h, i * 128:(i + 1) * 128], in0=pT[:, h, :],
                                                                   scalar=gret[:, h:h + 1],
                                                                   in1=sg[:, h, i * 128:(i + 1) * 128],
                                                                   op0=ALU.mult, op1=ALU.mult) for h in range(4)],
                 reads=[bankb[bO], buf("gret"), sgB, buf("mixr")], writes=[buf("mixr")], c=0.9)
            for half in range(2):
                bo = pY if half == 0 else pZ
                P.op("tensor", [lambda e, ec=ec, half=half, bo=bo: e.matmul(bank[bo][:], lhsT=mixr[:, ec - 4, i * 128:(i + 1) * 128],
                                                                              rhs=wout[:, ec, half * 512:(half + 1) * 512],
                                                                              start=False, stop=(ec == 7))
                                for ec in range(4, 8)],
                     reads=[buf("mixr")] + wb("wout0") + wb("wout1"), writes=[bankb[bo]], c=4 * MM512)
                P.op("vector", lambda e, half=half, bo=bo: e.tensor_tensor(out=xs[:, half * 512:(half + 1) * 512],
                                                                           in0=bank[bo][:],
                                                                           in1=xs[:, half * 512:(half + 1) * 512], op=ALU.add),
                     reads=[bankb[bo], xb], writes=[xb], n=512)
            rs, rsb = rms_rstd(xs[:], xb, 32 + t)
            for half in range(2):
                cs_ = slice(half * 512, (half + 1) * 512)
                hb = buf(f"xt{sl}_h{half}")
                P.op("vector", lambda e, cs_=cs_: e.scalar_tensor_tensor(out=xs[:, cs_], in0=xs[:, cs_], scalar=rs,
                                                                          in1=gbc[:, cs_], op0=ALU.mult, op1=ALU.mult),
                     reads=[xb, rsb, buf("gbc")], writes=[hb], n=512)
                out_ops.append(P.dma("sync", f"st{sl}_{half}", y_d[t * 128:(t + 1) * 128, cs_], xs[:, cs_], reads=[hb],
                                     nbytes=262144))

        issue_reload(0)
        issue_reload(1)
        for cb_i in range(4):
            conv_unit(0, cb_i)
        for h in range(4):
            rg_unit(0, h)
        cur = 0
        for gi, (t0g, ntg) in enumerate(GROUPS):
            nxt = gi + 1 if gi + 1 < len(GROUPS) else None
            units = ([("c", cb) for cb in range(4)] + [("r", h) for h in range(4)]) if nxt is not None else []
            per = (len(units) + 2 * ntg - 1) // (2 * ntg) if units else 0
            for j in range(ntg):
                t = t0g + j
                if t + 2 < NT:
                    issue_reload(t + 2)
                ret_A(t, cur)
                cur = 1 - cur
                for _ in range(per):
                    if units:
                        kind_u, a_u = units.pop(0)
                        (conv_unit if kind_u == "c" else rg_unit)(nxt, a_u)
                ret_B(t)
                for _ in range(per):
                    if units:
                        kind_u, a_u = units.pop(0)
                        (conv_unit if kind_u == "c" else rg_unit)(nxt, a_u)
            assert not units
        P.final = ("sync", out_ops)
        P.run()
    return nc


def _host_constants():
    h = np.arange(NH)
    gam = (1.0 - 2.0 ** (-5.0 - h)).astype(np.float64)
    idx = np.arange(128, dtype=np.float64)
    a = (128.0 ** -0.5) * gam[None, :] ** (idx[:, None] + 1.0)
    b = gam[None, :] ** (-(idx[:, None] + 1.0))
    ab = np.concatenate([a, b], axis=1).astype(np.float32)
    maskT = (idx[None, :] >= idx[:, None]).astype(np.float32)
    ident = np.eye(128, dtype=np.float32)
    half = 64
    freqs = (1.0 / (np.float32(10000.0) ** (np.arange(half, dtype=np.float32) / np.float32(half)))).astype(np.float32)
    cd = gam ** 128
    return gam, cd, ab, maskT, ident, freqs


def _cs_table(freqs, positions):
    ang = (positions.astype(np.float32)[:, None] * freqs[None, :]).astype(np.float32)
    return np.concatenate([np.cos(ang), np.sin(ang)], axis=1).astype(np.float32)


def _cs_table_all():
    L = NMETA + SEQ
    try:
        import jax
        import jax.numpy as jnp
        with jax.default_device(jax.devices("cpu")[0]):
            half = 64
            fr = 1.0 / (10000.0 ** (jnp.arange(half, dtype=jnp.float32) / half))
            ang = jnp.arange(L, dtype=jnp.int32).astype(jnp.float32)[:, None] * fr[None, :]
            tab = np.concatenate([np.asarray(jnp.cos(ang)), np.asarray(jnp.sin(ang))], axis=1)
        return np.ascontiguousarray(tab.astype(np.float32))
    except Exception:
        freqs = (1.0 / (np.float32(10000.0) ** (np.arange(64, dtype=np.float32) / np.float32(64)))).astype(np.float32)
        return _cs_table(freqs, np.arange(L))


_NC_CACHE = {}


def kernel(x, meta, norm1_g, w_in, conv_w, ret_norm_g, w_out, final_g):
    x = np.ascontiguousarray(np.asarray(x, dtype=np.float32))
    meta = np.ascontiguousarray(np.asarray(meta, dtype=np.float32))
    w_in = np.ascontiguousarray(np.asarray(w_in, dtype=np.float32))
    w_out = np.ascontiguousarray(np.asarray(w_out, dtype=np.float32))
    norm1_g = np.asarray(norm1_g, dtype=np.float32).reshape(1, D)
    final_g = np.asarray(final_g, dtype=np.float32).reshape(1, D)
    conv_w = np.asarray(conv_w, dtype=np.float32)
    ret_norm_g = np.asarray(ret_norm_g, dtype=np.float32)

    gam, cd, ab, maskT, ident, freqs = _host_constants()
    convw_l = np.ascontiguousarray(conv_w.reshape(3, 4, 128).transpose(2, 1, 0).reshape(128, 12))
    gret_l = np.ascontiguousarray(ret_norm_g.reshape(4, 128).T)

    if "nc" not in _NC_CACHE:
        _NC_CACHE["nc"] = build_nc()
    nc = _NC_CACHE["nc"]

    cs_all = _cs_table_all()
    in_maps = []
    for core in range(NCORE):
        b, c = divmod(core, 4)
        xs = x[b, c * TOK:(c + 1) * TOK, :]
        xprev = meta if c == 0 else x[b, c * TOK - 16:c * TOK, :]
        cs = np.zeros((17 * 128, 128), np.float32)
        cs[112:128] = cs_all[0:NMETA]
        cs[128:] = cs_all[NMETA + c * TOK:NMETA + (c + 1) * TOK]
        coef = np.zeros((1, 20), np.float64)
        coef[0, 0:4] = cd ** (16 * c)
        for r in range(4):
            if r < c:
                coef[0, 4 + 4 * r:8 + 4 * r] = cd ** (16 * (c - 1 - r))
        in_maps.append({
            "x": np.ascontiguousarray(xs), "xprev": np.ascontiguousarray(xprev), "meta": meta,
            "w_in": w_in, "w_out": w_out, "norm1_g": norm1_g, "final_g": final_g,
            "convw": convw_l, "gret": gret_l, "cs": cs, "ab": ab, "maskT": maskT, "ident": ident,
            "coef": coef.astype(np.float32),
        })
    res = run_bass_kernel_spmd(nc, in_maps, core_ids=list(range(NCORE)))
    out = np.empty((BATCH, SEQ, D), np.float32)
    for core in range(NCORE):
        b, c = divmod(core, 4)
        out[b, c * TOK:(c + 1) * TOK, :] = res.results[core]["y"]
    return out
```

```python
import numpy as np
from contextlib import ExitStack
import concourse.bass as bass
import concourse.mybir as mybir
from concourse.bass_utils import run_bass_kernel_spmd

F32 = mybir.dt.float32
BF16 = mybir.dt.bfloat16
ALU = mybir.AluOpType
AF = mybir.ActivationFunctionType

D = 1024
SEQ = 8192
BATCH = 2
NMETA = 16
NCORE = 8
TOK = 2048
NT = 16
NH = 4
EPS = 1e-6
ENGS = ["sync", "scalar", "vector", "gpsimd", "tensor"]

GAMMA = [1.0 - 2.0 ** (-5.0 - h) for h in range(NH)]
CD = [g ** 128 for g in GAMMA]

C_CX, C_CB, C_CC, C_CG, C_Q, C_K, C_V, C_RG = [i * 512 for i in range(8)]
R_CX, R_CB, R_CC, R_CG, R_Q, R_RG = 0, 512, 1024, 1536, 2048, 2560


class Buf:
    def __init__(self, name, excl=False):
        self.name = name
        self.w = None
        self.r = []
        self.excl = excl


def _cost(eng, n):
    if eng == "vector":
        return 0.12 + n / 900.0
    if eng == "scalar":
        return 0.22 + n / 1200.0
    if eng == "gpsimd":
        return 0.35 + n / 330.0
    return 0.25


SEM_LAT = 0.12
import os
NOSCHED = bool(int(os.environ.get("KERNEL_NOSCHED", "0")))
SCHED_CP = bool(int(os.environ.get("KERNEL_CP", "1")))
SCHED_W = int(os.environ.get("KERNEL_W", "96"))
DMA_BPUS = float(os.environ.get("KERNEL_DMA_BPUS", "150e3"))


class Prog:
    def __init__(self, nc, stack):
        self.nc = nc
        self.stack = stack
        self.ops = []
        self.esem = {e: stack.enter_context(nc.semaphore("es_" + e)) for e in ENGS if e != "sync"}
        self.dsem = {}

    def sb(self, name, shape, dt):
        return self.stack.enter_context(self.nc.sbuf_tensor(name, shape, dt))

    def ps(self, name, shape, dt):
        return self.stack.enter_context(self.nc.psum_tensor(name, shape, dt))

    def _add(self, op, reads, writes, extra):
        idx = len(self.ops)
        deps = {}
        for b in reads:
            if b.w is not None:
                deps[b.w] = "raw"
            if b.excl:
                for r in b.r:
                    deps.setdefault(r, "war")
        for b in writes:
            if b.w is not None:
                deps[b.w] = "raw"
            for r in b.r:
                deps.setdefault(r, "war")
        for x in extra:
            if x is not None:
                deps[x] = "raw"
        deps.pop(idx, None)
        op["deps"] = deps
        op["idx"] = idx
        self.ops.append(op)
        for b in reads:
            if b not in writes:
                b.r.append(idx)
        for b in writes:
            b.w = idx
            b.r = []
        return idx

    def op(self, eng, fns, reads=(), writes=(), extra=(), n=512, c=None):
        if not isinstance(fns, (list, tuple)):
            fns = [fns]
        cost = c if c is not None else _cost(eng, n)
        return self._add({"eng": eng, "fns": list(fns), "kind": "op", "busy": cost, "lat": cost}, reads, writes, extra)

    def dma(self, eng, slot, out, in_, reads=(), writes=(), extra=(), nbytes=65536):
        busy = {"sync": 0.2, "scalar": 0.8}.get(eng, 3.0)
        lat = busy + 2.0 + nbytes / DMA_BPUS
        return self._add({"eng": eng, "kind": "dma", "slot": slot, "out": out, "in_": in_, "busy": busy, "lat": lat},
                         reads, writes, extra)

    def raw(self, eng, fn, sem, reads=(), writes=(), busy=1.0, lat=50.0):
        return self._add({"eng": eng, "kind": "raw", "fn": fn, "sem": sem, "busy": busy, "lat": lat},
                         reads, writes, ())

    def schedule(self):
        ops = self.ops
        N = len(ops)
        if NOSCHED:
            order = {e: [] for e in ENGS}
            for o in ops:
                order[o["eng"]].append(o["idx"])
            self.order = order
            self.makespan = 0.0
            return order
        succ = [[] for _ in range(N)]
        for o in ops:
            for d in o["deps"]:
                succ[d].append(o["idx"])
        blevel = [0.0] * N
        for i in range(N - 1, -1, -1):
            m = 0.0
            for j in succ[i]:
                if blevel[j] > m:
                    m = blevel[j]
            blevel[i] = ops[i]["lat"] + SEM_LAT + m
        pending = {e: [] for e in ENGS}
        for o in ops:
            pending[o["eng"]].append(o["idx"])
        fin = [None] * N
        free = {e: 0.0 for e in ENGS}
        order = {e: [] for e in ENGS}
        head = {e: 0 for e in ENGS}
        done = [False] * N
        W = SCHED_W
        nleft = N
        while nleft:
            best = None
            for e in ENGS:
                lst = pending[e]
                h = head[e]
                while h < len(lst) and done[lst[h]]:
                    h += 1
                head[e] = h
                cand = None
                cnt = 0
                for j in range(h, len(lst)):
                    i = lst[j]
                    if done[i]:
                        continue
                    cnt += 1
                    if cnt > W:
                        break
                    o = ops[i]
                    est = 0.0
                    ok = True
                    for d in o["deps"]:
                        f = fin[d]
                        if f is None:
                            ok = False
                            break
                        if f + SEM_LAT > est:
                            est = f + SEM_LAT
                    if not ok:
                        continue
                    st = est if est > free[e] else free[e]
                    key = (st, -blevel[i] if SCHED_CP else i, i)
                    if cand is None or key < cand:
                        cand = key
                if cand is not None and (best is None or (cand[0], cand[2]) < (best[0], best[1])):
                    best = (cand[0], cand[2], e)
            assert best is not None, "scheduler deadlock"
            st, i, e = best
            o = ops[i]
            fin[i] = st + o["lat"]
            free[e] = st + o["busy"]
            done[i] = True
            order[e].append(i)
            nleft -= 1
        self.order = order
        self.makespan = max(f for f in fin)
        return order

    def run(self):
        order = self.schedule()
        ops = self.ops
        tok = [None] * len(ops)
        dcnt = {}
        for e in ENGS:
            cnt = 0
            for i in order[e]:
                o = ops[i]
                if o["kind"] == "op":
                    cnt += 1
                    tok[i] = (self.esem[e], cnt, "e_" + e)
                elif o["kind"] == "dma":
                    slot = o["slot"]
                    if slot not in self.dsem:
                        self.dsem[slot] = self.stack.enter_context(self.nc.semaphore("ds_" + slot))
                        dcnt[slot] = 0
                    dcnt[slot] += 16
                    tok[i] = (self.dsem[slot], dcnt[slot], "d_" + slot)
                else:
                    tok[i] = (o["sem"], 1, "r_%d" % i)

        def emit_engine(e, eng_obj):
            waited = {}
            for i in order[e]:
                o = ops[i]
                need = {}
                for d, kind in o["deps"].items():
                    od = ops[d]
                    if e == "tensor" and od["eng"] == "tensor" and od["kind"] == "op":
                        continue
                    sem, val, key = tok[d]
                    if waited.get(key, 0) >= val:
                        continue
                    if need.get(key, (None, 0))[1] < val:
                        need[key] = (sem, val)
                for key, (sem, val) in need.items():
                    waited[key] = val
                    eng_obj.wait_ge(sem, val)
                if o["kind"] == "op":
                    for f in o["fns"][:-1]:
                        f(eng_obj)
                    o["fns"][-1](eng_obj).then_inc(tok[i][0], 1)
                elif o["kind"] == "dma":
                    eng_obj.dma_start(out=o["out"], in_=o["in_"]).then_inc(tok[i][0], 16)
                else:
                    o["fn"](eng_obj).then_inc(o["sem"])

        final = getattr(self, "final", None)

        def emit_final(e, eng_obj):
            if final is not None and final[0] == e:
                seen = {}
                for d in final[1]:
                    sem, val, key = tok[d]
                    if seen.get(key, (None, 0))[1] < val:
                        seen[key] = (sem, val)
                for key, (sem, val) in seen.items():
                    eng_obj.wait_ge(sem, val)

        with self.nc.Block() as block:
            @block.sync
            def _(e):
                emit_engine("sync", e)
                emit_final("sync", e)

            @block.scalar
            def _(e):
                emit_engine("scalar", e)

            @block.vector
            def _(e):
                emit_engine("vector", e)

            @block.gpsimd
            def _(e):
                emit_engine("gpsimd", e)

            @block.tensor
            def _(e):
                emit_engine("tensor", e)


def build_nc():
    nc = bass.Bass("TRN2", target_bir_lowering=False)
    dr = lambda name, shape, kind="ExternalInput": nc.dram_tensor(name, shape, F32, kind=kind).ap()
    x_d = dr("x", [TOK, D])
    xprev_d = dr("xprev", [16, D])
    meta_d = dr("meta", [16, D])
    win_d = dr("w_in", [D, 4096])
    wout_d = dr("w_out", [D, D])
    g1_d = dr("norm1_g", [1, D])
    gf_d = dr("final_g", [1, D])
    convw_d = dr("convw", [128, 12])
    gret_d = dr("gret", [128, 4])
    cs_d = dr("cs", [17 * 128, 128])
    ab_d = dr("ab", [128, 8])
    mask_d = dr("maskT", [128, 128])
    ident_d = dr("ident", [128, 128])
    coef_d = dr("coef", [1, 20])
    y_d = dr("y", [TOK, D], kind="ExternalOutput")
    aloc_d = dr("a_loc", [128, 512], kind="Internal")
    aall_d = dr("a_all", [4 * 128, 512], kind="Internal")

    win_v = win_d.rearrange("(k p) n -> p k n", p=128)
    wout_v = wout_d.rearrange("(k p) n -> p k n", p=128)
    cs_v = cs_d.rearrange("(t p) f -> p t f", p=128)

    with ExitStack() as st:
        P = Prog(nc, st)
        wkv = P.sb("wkv", [128, 8, 1024], BF16)
        wrest = P.sb("wrest", [128, 8, 3072], BF16)
        wout = P.sb("wout", [128, 8, 1024], BF16)
        hnT = P.sb("hnT", [128, 8, 16 + TOK], BF16)
        ktok = P.sb("ktok", [128, NT, 512], BF16)
        vtok = P.sb("vtok", [128, NT, 512], BF16)
        xt = [P.sb(f"xt{i}", [128, D], F32) for i in range(3)]
        xn_mixr = P.sb("xn_mixr", [128, 2048], BF16)
        junk = P.sb("junk", [128, D], BF16)
        gbc = P.sb("gbc", [128, D], F32)
        cst = [P.sb(f"cs{i}", [128, 128], F32) for i in range(4)]
        ks = P.sb("ks", [128, 512], F32)
        rA = P.sb("rA", [128, 512], F32)
        rB = P.sb("rB", [128, 512], F32)
        qrot = P.sb("qrot", [128, 512], BF16)
        qkT = P.sb("qkT", [128, 8, 128], BF16)
        sT = P.sb("sT", [128, 4, 128], BF16)
        ybuf = P.sb("ybuf", [128, 4, 128], BF16)
        c1 = [P.sb(f"c1_{i}", [128, 512], F32) for i in range(2)]
        tb = [P.sb(f"tb_{i}", [128, 512], F32) for i in range(2)]
        uw = [P.sb(f"uw{i}", [128, 514], F32) for i in range(2)]
        uhalo = P.sb("uhalo", [128, 4, 2], F32)
        mixc = [P.sb(f"mixc{i}", [128, 4, 512], BF16) for i in range(2)]
        Rst = P.sb("Rst", [128, 4, 128], F32)
        U0 = P.sb("U0", [128, 4, 128], F32)
        Sbf = [P.sb(f"Sbf{i}", [128, 4, 128], BF16) for i in range(2)]
        identb = P.sb("identb", [128, 128], BF16)
        maskT = P.sb("maskT_sb", [128, 128], F32)
        ab = P.sb("ab_sb", [128, 8], F32)
        coef = P.sb("coef_sb", [128, 20], F32)
        convw = P.sb("convw_sb", [128, 12], F32)
        gret = P.sb("gret_sb", [128, 4], F32)
        mhalf = P.sb("mhalf", [128, 4], F32)
        ssq = P.sb("ssq", [128, 64], F32)
        ms = P.sb("ms", [128, 64], F32)
        rstd = P.sb("rstd", [128, 64], F32)
        cdb = P.sb("cdb", [128, 4], F32)
        bns = P.sb("bns", [128, 2, 4, 6], F32)
        mv = P.sb("mv", [128, 2, 4, 2], F32)
        gve = P.sb("gve", [128, 2, 4], F32)
        grs = P.sb("grs", [128, 2, 4], F32)
        gnm = P.sb("gnm", [128, 2, 4], F32)
        uh_sb = P.sb("uh_sb", [128, 16], F32)

        identf = junk[:, 0:256].bitcast(F32)
        xt_p1 = xt + [mixc[1][:].rearrange("p a b -> p (a b)").bitcast(F32)]
        xn = [xn_mixr[:, 0:1024], xn_mixr[:, 1024:2048]]
        mixr = xn_mixr[:].rearrange("p (h t) -> p h t", h=4)
        wkv_flat = wkv[:].rearrange("p a b -> p (a b)")
        AG = wkv_flat[:, 0:4096].bitcast(F32).rearrange("p (r f) -> p r f", r=4)
        sgr = [wkv_flat[:, 4096:6144].rearrange("p (h t) -> p h t", h=4),
               wkv_flat[:, 6144:8192].rearrange("p (h t) -> p h t", h=4)]
        hnT_m = qkT
        ktok_m = qrot
        vtok_m = sT[:].rearrange("p a b -> p (a b)")

        bank = [P.ps(f"bank{i}", [128, 512], F32) for i in range(8)]
        bankb = [Buf(f"bank{i}", excl=True) for i in range(8)]

        def bf_view(i):
            return bank[i][:].bitcast(BF16).rearrange("p (k t) -> p k t", k=8)

        def h_view(i):
            return bank[i][:].rearrange("p (h t) -> p h t", h=4)

        B = {}

        def buf(name):
            if name not in B:
                B[name] = Buf(name)
            return B[name]

        MM512 = 0.27
        MM128 = 0.09
        TR = 0.12

        P.dma("sync", "c_ident", identf, ident_d[:, :], writes=[buf("junk")])
        P.dma("sync", "c_g", gbc[:], g1_d[0:1, :].broadcast_to([128, D]), writes=[buf("gbc")], nbytes=524288)
        P.dma("scalar", "c_ab", ab[:], ab_d[:, :], writes=[buf("ab")])
        P.dma("scalar", "c_mask", maskT[:], mask_d[:, :], writes=[buf("maskT")])
        P.dma("scalar", "c_coef", coef[:], coef_d[0:1, :].broadcast_to([128, 20]), writes=[buf("coef")])
        P.dma("scalar", "c_convw", convw[:], convw_d[:, :], writes=[buf("convw")])
        P.dma("scalar", "c_gret", gret[:], gret_d[:, :], writes=[buf("gret")])
        P.dma("gpsimd", "w_k", wkv[:, :, 0:512], win_v[:, :, C_K:C_K + 512], writes=[buf("wk")], nbytes=2 << 20)
        P.dma("gpsimd", "w_v", wkv[:, :, 512:1024], win_v[:, :, C_V:C_V + 512], writes=[buf("wv")], nbytes=2 << 20)
        P.op("gpsimd", lambda e: e.memset(mhalf[:], -0.5), writes=[buf("mhalf")], n=4)
        P.op("gpsimd", [lambda e, h=h: e.memset(cdb[:, h:h + 1], float(CD[h])) for h in range(4)],
             writes=[buf("cdb")], n=16)
        P.op("vector", lambda e: e.tensor_copy(out=identb[:], in_=identf), reads=[buf("junk")],
             writes=[buf("identb")], n=128)
        wpieces = []
        wparts = {}

        def add_piece(nm, dst, src, nb):
            key = nm + "_%d" % len(wparts.setdefault(nm, []))
            wparts[nm].append(buf(key))
            wpieces.append((key, dst, src, nb))

        for (nm, ro, co) in [("wcc", R_CC, C_CC), ("wcx", R_CX, C_CX), ("wcb", R_CB, C_CB), ("wcg", R_CG, C_CG)]:
            for qk in range(4):
                add_piece(nm, wrest[:, 2 * qk:2 * qk + 2, ro:ro + 512], win_v[:, 2 * qk:2 * qk + 2, co:co + 512], 1 << 19)
        n_paced = len(wpieces)
        for (nm, ro, co) in [("wrg", R_RG, C_RG), ("wq", R_Q, C_Q)]:
            for hk in range(2):
                add_piece(nm, wrest[:, 4 * hk:4 * hk + 4, ro:ro + 512], win_v[:, 4 * hk:4 * hk + 4, co:co + 512], 1 << 20)
        for half in range(2):
            for hk in range(2):
                add_piece(f"wout{half}", wout[:, 4 * hk:4 * hk + 4, half * 512:(half + 1) * 512],
                          wout_v[:, 4 * hk:4 * hk + 4, half * 512:(half + 1) * 512], 1 << 20)
        assert n_paced == NT

        def wb(nm):
            return list(wparts[nm])

        wstate = {"n": 0}

        def load_weight_piece(after_op):
            key, dst, src, nb = wpieces.pop(0)
            P.dma("gpsimd", "w_" + key, dst, src, writes=[buf(key)], extra=[after_op], nbytes=nb)
            wstate["n"] += 1
            if wstate["n"] == NT:
                while wpieces:
                    key, dst, src, nb = wpieces.pop(0)
                    P.dma("gpsimd", "w_" + key, dst, src, writes=[buf(key)], extra=[after_op], nbytes=nb)

        cnt = {"slot": 0, "cs": 0}

        def rms_rstd(src_ap, srcbuf, col, extra_reads=()):
            P.op("scalar", lambda e: e.activation(out=junk[:], in_=src_ap, func=AF.Square,
                                                  accum_out=ssq[:, col:col + 1]),
                 reads=[srcbuf] + list(extra_reads), writes=[buf(f"ssq{col}"), buf("junk")], n=1024)
            P.op("scalar", lambda e: e.activation(out=ms[:, col:col + 1], in_=ssq[:, col:col + 1], func=AF.Identity,
                                                  scale=1.0 / D, bias=EPS),
                 reads=[buf(f"ssq{col}")], writes=[buf(f"ms{col}")], n=1)
            P.op("gpsimd", lambda e: e.tensor_tensor(out=rstd[:, col:col + 1], in0=ms[:, col:col + 1],
                                                     in1=mhalf[:, 0:1], op=ALU.pow),
                 reads=[buf(f"ms{col}"), buf("mhalf")], writes=[buf(f"rstd{col}")], c=1.0)
            return rstd[:, col:col + 1], buf(f"rstd{col}")

        def rotary(psum_i, scale_bc, csl, csbuf, dst_ap, dstbuf):
            pv = h_view(psum_i)
            ks4 = ks[:].rearrange("p (h t f) -> p h t f", h=4, t=2)
            a4 = rA[:].rearrange("p (h t f) -> p h t f", h=4, t=2)
            b4 = rB[:].rearrange("p (h t f) -> p h t f", h=4, t=2)
            d4 = dst_ap.rearrange("p (h t f) -> p h t f", h=4, t=2)
            cosb = cst[csl][:, 0:64].unsqueeze(1).unsqueeze(1).broadcast_to([128, 4, 2, 64])
            sinb = cst[csl][:, 64:128].unsqueeze(1).unsqueeze(1).broadcast_to([128, 4, 2, 64])
            ksh = ks[:].rearrange("p (h t) -> p h t", h=4)
            P.op("vector", lambda e: e.tensor_tensor(out=ksh, in0=pv, in1=scale_bc, op=ALU.mult),
                 reads=[bankb[psum_i], buf("ab")], writes=[buf("ks")], n=512)
            P.op("vector", lambda e: e.tensor_tensor(out=a4, in0=ks4, in1=cosb, op=ALU.mult),
                 reads=[buf("ks"), csbuf], writes=[buf("rA")], n=512)
            P.op("vector", lambda e: e.tensor_tensor(out=b4, in0=ks4[:, :, ::-1, :], in1=sinb, op=ALU.mult),
                 reads=[buf("ks"), csbuf], writes=[buf("rB")], n=512)
            P.op("vector", lambda e: e.tensor_tensor(out=d4[:, :, 0, :], in0=a4[:, :, 0, :], in1=b4[:, :, 0, :],
                                                     op=ALU.subtract),
                 reads=[buf("rA"), buf("rB")], writes=[dstbuf], n=256)
            P.op("gpsimd", lambda e: e.tensor_tensor(out=d4[:, :, 1, :], in0=a4[:, :, 1, :], in1=b4[:, :, 1, :],
                                                     op=ALU.add),
                 reads=[buf("rA"), buf("rB"), dstbuf], writes=[dstbuf], n=256)

        a_bc = ab[:, 0:4].unsqueeze(2).broadcast_to([128, 4, 128])
        b_bc = ab[:, 4:8].unsqueeze(2).broadcast_to([128, 4, 128])

        def p1_A(kind, pset, idx):
            bT = 4 * pset
            sl = cnt["slot"] % 4
            cnt["slot"] += 1
            xb = buf(f"xt{sl}")
            xs_t = xt_p1[sl]
            xs = xs_t if sl == 3 else xs_t[:]
            xsel = cnt["slot"] % 2
            xnb = buf(f"xn{xsel}")
            xna = xn[xsel]
            ctx = {"kind": kind, "pset": pset}
            if kind in ("M", "H"):
                P.op("vector", lambda e: e.memset(xs[:], 0.0), writes=[xb], c=0.6)
                src = meta_d if kind == "M" else xprev_d
                P.dma("sync", f"xl{sl}", xs[112:128, :], src[:, :], reads=[xb], writes=[xb])
            else:
                xop = P.dma("sync", f"xl{sl}", xs[:], x_d[kind * 128:(kind + 1) * 128, :], writes=[xb],
                            extra=([cnt["x0op"]] if kind in (2, 3) else []), nbytes=524288)
                if kind == 0:
                    cnt["x0op"] = xop
                load_weight_piece(xop)
            if kind != "H":
                csl = cnt["cs"] % 4
                cnt["cs"] += 1
                csb = buf(f"cs{csl}")
                ti = 0 if kind == "M" else kind + 1
                P.dma("sync", f"csl{csl}", cst[csl][:], cs_v[:, ti, :], writes=[csb])
                ctx["csl"], ctx["csb"] = csl, csb
            rs, rsb = rms_rstd(xs[:], xb, idx)
            P.op("vector", lambda e: e.scalar_tensor_tensor(out=xna, in0=xs[:], scalar=rs, in1=gbc[:],
                                                            op0=ALU.mult, op1=ALU.mult),
                 reads=[xb, rsb, buf("gbc")], writes=[xnb], n=1024)
            pT = bf_view(bT)
            P.op("tensor", [lambda e, kc=kc: e.transpose(out=pT[:, kc, :], in_=xna[:, kc * 128:(kc + 1) * 128],
                                                          identity=identb[:]) for kc in range(8)],
                 reads=[xnb, buf("identb")], writes=[bankb[bT]], c=8 * TR)
            if kind == "H":
                P.op("scalar", lambda e: e.activation(out=hnT[:, :, 0:16], in_=pT[:, :, 112:128], func=AF.Copy),
                     reads=[bankb[bT]], writes=[buf("hnT_h")], n=128)
                return None
            if kind == "M":
                hsrc, hb = hnT_m[:], buf("qkT")
            else:
                hsrc, hb = hnT[:, :, 16 + kind * 128:16 + (kind + 1) * 128], buf(f"hnT{kind}")
            P.op("scalar", lambda e: e.activation(out=hsrc, in_=pT, func=AF.Copy), reads=[bankb[bT]], writes=[hb],
                 n=1024)
            ctx["hsrc"], ctx["hb"] = hsrc, hb
            return ctx

        def p1_B(ctx):
            kind, pset = ctx["kind"], ctx["pset"]
            bK, bV, bU = 4 * pset + 1, 4 * pset + 2, 4 * pset + 3
            hsrc, hb, csl, csb = ctx["hsrc"], ctx["hb"], ctx["csl"], ctx["csb"]
            if kind == "M":
                kdst, kb = ktok_m[:], buf("qrot")
                vdst, vb = vtok_m, buf("sT")
            else:
                kdst, kb = ktok[:, kind, :], buf(f"ktok{kind}")
                vdst, vb = vtok[:, kind, :], buf(f"vtok{kind}")
            P.op("tensor", [lambda e, kc=kc: e.matmul(bank[bK][:], lhsT=hsrc[:, kc, :], rhs=wkv[:, kc, 0:512],
                                                       start=(kc == 0), stop=(kc == 7)) for kc in range(8)],
                 reads=[hb, buf("wk")], writes=[bankb[bK]], c=8 * MM512)
            P.op("tensor", [lambda e, kc=kc: e.matmul(bank[bV][:], lhsT=hsrc[:, kc, :], rhs=wkv[:, kc, 512:1024],
                                                       start=(kc == 0), stop=(kc == 7)) for kc in range(8)],
                 reads=[hb, buf("wv")], writes=[bankb[bV]], c=8 * MM512)
            rotary(bK, b_bc, csl, csb, kdst, kb)
            P.op("scalar", lambda e: e.activation(out=vdst, in_=bank[bV][:], func=AF.Copy),
                 reads=[bankb[bV]], writes=[vb], n=512)
            pU = h_view(bU)
            P.op("tensor", [lambda e, h=h: e.matmul(pU[:, h, :], lhsT=kdst[:, h * 128:(h + 1) * 128],
                                                     rhs=vdst[:, h * 128:(h + 1) * 128], start=True, stop=True)
                            for h in range(4)],
                 reads=[kb, vb], writes=[bankb[bU]], c=4 * MM128)
            if kind == "M":
                P.op("vector", lambda e: e.tensor_copy(out=U0[:], in_=pU), reads=[bankb[bU]], writes=[buf("U0")], n=512)
            elif kind == 0:
                P.op("vector", lambda e: e.tensor_copy(out=Rst[:], in_=pU), reads=[bankb[bU]], writes=[buf("R")], n=512)
            else:
                P.op("vector", [lambda e, h=h: e.scalar_tensor_tensor(out=Rst[:, h, :], in0=Rst[:, h, :],
                                                                       scalar=float(CD[h]), in1=pU[:, h, :],
                                                                       op0=ALU.mult, op1=ALU.add)
                                for h in range(4)],
                     reads=[bankb[bU], buf("R")], writes=[buf("R")], c=0.9)

        order = ["H"] + list(range(NT)) + ["M"]
        pend = []
        for i, kind in enumerate(order):
            ctx = p1_A(kind, i % 2, i)
            if ctx is not None:
                pend.append(ctx)
            if i >= 2 and pend:
                p1_B(pend.pop(0))
        while pend:
            p1_B(pend.pop(0))
        assert not wpieces

        def alias_after(dst, srcs):
            for sname in srcs:
                sb_ = buf(sname)
                dst.r = dst.r + list(sb_.r) + ([sb_.w] if sb_.w is not None else [])
        for nm in ("AG0", "AG1", "AG2", "AG3", "sgr0", "sgr1"):
            alias_after(buf(nm), ["wk", "wv"])
        alias_after(buf("mixr"), ["xn0", "xn1"])
        alias_after(buf("mixc1"), ["xt3"])
        P.dma("sync", "aloc", aloc_d[:, :], Rst[:].rearrange("p h t -> p (h t)"), reads=[buf("R")],
              writes=[buf("aloc")], nbytes=262144)
        cc_sem = st.enter_context(nc.semaphore("cc_sem"))
        P.raw("gpsimd", lambda e: e.collective_compute("AllGather", ALU.bypass,
                                                         replica_groups=[[0, 1, 2, 3], [4, 5, 6, 7]],
                                                         ins=[aloc_d[:, :]], outs=[aall_d[:, :]]),
              cc_sem, reads=[buf("aloc")], writes=[buf("aall")])
        P.dma("sync", "c_g", gbc[:], gf_d[0:1, :].broadcast_to([128, D]), writes=[buf("gbc")], nbytes=524288)
        for r in range(4):
            P.dma("sync", f"agl{r}", AG[:, r, :], aall_d[r * 128:(r + 1) * 128, :], reads=[buf("aall")],
                  writes=[buf(f"AG{r}")], nbytes=1 << 18)
        P.op("vector", [lambda e, h=h: e.tensor_scalar(out=Rst[:, h, :], in0=U0[:, h, :], scalar1=coef[:, h:h + 1],
                                                        scalar2=None, op0=ALU.mult) for h in range(4)],
             reads=[buf("U0"), buf("coef")], writes=[buf("R")], c=0.9)
        AGh = AG.rearrange("p r (h t) -> p r h t", h=4)
        for r in range(4):
            P.op("vector", [lambda e, r=r, h=h: e.scalar_tensor_tensor(out=Rst[:, h, :], in0=AGh[:, r, h, :],
                                                                        scalar=coef[:, 4 + 4 * r + h:5 + 4 * r + h],
                                                                        in1=Rst[:, h, :], op0=ALU.mult, op1=ALU.add)
                            for h in range(4)],
                 reads=[buf(f"AG{r}"), buf("coef"), buf("R")], writes=[buf("R")], c=0.9)
        Sb = [buf("S0"), buf("S1")]

        def make_S(dst_i):
            P.op("scalar", [lambda e, h=h: e.activation(out=Sbf[dst_i][:, h, :], in_=Rst[:, h, :], func=AF.Copy,
                                                         scale=float(CD[h])) for h in range(4)],
                 reads=[buf("R")], writes=[Sb[dst_i]], c=1.3)
        make_S(0)

        pA, pB_, X1, X2, O0, O1, pY, pZ = range(8)

        def proj_fm(bi, wbuf, rcol, tok0, n):
            ti = (tok0 - 16) // 128
            rd = list(wbuf) + [buf(f"hnT{t}") for t in range(ti, min(NT, ti + (n + 127) // 128))]
            P.op("tensor", [lambda e, kc=kc: e.matmul(bank[bi][:, 0:n], lhsT=wrest[:, kc, rcol:rcol + 128],
                                                       rhs=hnT[:, kc, tok0:tok0 + n], start=(kc == 0), stop=(kc == 7))
                            for kc in range(8)],
                 reads=rd, writes=[bankb[bi]], c=8 * (0.03 + (MM512 - 0.03) * n / 512.0))

        for cb_i in range(4):
            P.op("tensor", [lambda e, kc=kc, cb_i=cb_i: e.matmul(bank[pA][:, 4 * cb_i:4 * cb_i + 2],
                                                                  lhsT=wrest[:, kc, R_CC + cb_i * 128:R_CC + (cb_i + 1) * 128],
                                                                  rhs=hnT[:, kc, 14:16], start=(kc == 0), stop=(kc == 7))
                            for kc in range(8)] +
                           [lambda e, kc=kc, cb_i=cb_i: e.matmul(bank[pA][:, 4 * cb_i + 2:4 * cb_i + 4],
                                                                  lhsT=wrest[:, kc, R_CX + cb_i * 128:R_CX + (cb_i + 1) * 128],
                                                                  rhs=hnT[:, kc, 14:16], start=(kc == 0), stop=(kc == 7))
                            for kc in range(8)],
                 reads=wb("wcc") + wb("wcx") + [buf("hnT_h")], writes=[bankb[pA]], c=16 * 0.07)
        P.op("scalar", lambda e: e.activation(out=uh_sb[:], in_=bank[pA][:, 0:16], func=AF.Copy),
             reads=[bankb[pA]], writes=[buf("uh")], n=16)
        uh4 = uh_sb[:].rearrange("p (c k t) -> p c k t", c=4, k=2)
        P.op("gpsimd", lambda e: e.tensor_tensor(out=uhalo[:], in0=uh4[:, :, 0, :], in1=uh4[:, :, 1, :], op=ALU.mult),
             reads=[buf("uh")], writes=[buf("uhalo")], n=8)

        ucnt = {"n": 0}

        GROUPS = [(0, 4), (4, 4), (8, 4), (12, 2), (14, 2)]
        TILE_G = {}
        for gi, (t0g, ntg) in enumerate(GROUPS):
            for j in range(ntg):
                TILE_G[t0g + j] = (gi, j)

        def conv_unit(s, cb_i):
            tok0 = 16 + 128 * GROUPS[s][0]
            nt_ = 128 * GROUPS[s][1]
            k = ucnt["n"] % 2
            ucnt["n"] += 1
            u, uB = uw[k], buf(f"uw{k}")
            c1k, c1B = c1[k], buf(f"c1_{k}")
            tbk, tbB = tb[k], buf(f"tb_{k}")
            mx, mxB = mixc[s % 2], buf(f"mixc{s % 2}")
            cw = lambda j: convw[:, 3 * cb_i + j:3 * cb_i + j + 1]
            bA, bB = (pY, pZ) if (s == 0 and cb_i % 2 == 1) else (pA, pB_)
            proj_fm(bA, wb("wcc"), R_CC + cb_i * 128, tok0, nt_)
            P.op("scalar", lambda e: e.activation(out=c1k[:, 0:nt_], in_=bank[bA][:, 0:nt_], func=AF.Copy),
                 reads=[bankb[bA]], writes=[c1B], n=nt_)
            proj_fm(bB, wb("wcx"), R_CX + cb_i * 128, tok0, nt_)
            P.op("gpsimd", lambda e: e.tensor_copy(out=u[:, 0:2], in_=uhalo[:, cb_i, :]),
                 reads=[buf("uhalo"), uB], writes=[uB], c=0.2)
            P.op("vector", lambda e: e.tensor_tensor(out=u[:, 2:2 + nt_], in0=bank[bB][:, 0:nt_], in1=c1k[:, 0:nt_], op=ALU.mult),
                 reads=[bankb[bB], c1B, uB], writes=[uB], n=nt_)
            P.op("gpsimd", lambda e: e.tensor_copy(out=uhalo[:, cb_i, :], in_=u[:, nt_:nt_ + 2]),
                 reads=[uB, buf("uhalo")], writes=[buf("uhalo")], c=0.2)
            P.op("vector", lambda e: e.tensor_scalar(out=tbk[:, 0:nt_], in0=u[:, 0:nt_], scalar1=cw(0), scalar2=None,
                                                     op0=ALU.mult),
                 reads=[uB, buf("convw")], writes=[tbB], n=nt_)
            P.op("vector", lambda e: e.scalar_tensor_tensor(out=tbk[:, 0:nt_], in0=u[:, 1:1 + nt_], scalar=cw(1), in1=tbk[:, 0:nt_],
                                                            op0=ALU.mult, op1=ALU.add),
                 reads=[uB, buf("convw"), tbB], writes=[tbB], n=nt_)
            P.op("vector", lambda e: e.scalar_tensor_tensor(out=tbk[:, 0:nt_], in0=u[:, 2:2 + nt_], scalar=cw(2), in1=tbk[:, 0:nt_],
                                                            op0=ALU.mult, op1=ALU.add),
                 reads=[uB, buf("convw"), tbB], writes=[tbB], n=nt_)
            proj_fm(bA, wb("wcb"), R_CB + cb_i * 128, tok0, nt_)
            P.op("vector", lambda e: e.tensor_tensor(out=tbk[:, 0:nt_], in0=bank[bA][:, 0:nt_], in1=tbk[:, 0:nt_], op=ALU.mult),
                 reads=[bankb[bA], tbB], writes=[tbB], n=nt_)
            proj_fm(bB, wb("wcg"), R_CG + cb_i * 128, tok0, nt_)
            P.op("scalar", lambda e: e.activation(out=c1k[:, 0:nt_], in_=bank[bB][:, 0:nt_], func=AF.Silu),
                 reads=[bankb[bB], c1B], writes=[c1B], n=nt_)
            P.op("vector", lambda e: e.tensor_tensor(out=mx[:, cb_i, 0:nt_], in0=tbk[:, 0:nt_], in1=c1k[:, 0:nt_], op=ALU.mult),
                 reads=[tbB, c1B, mxB], writes=[mxB], n=nt_)

        def rg_unit(s, h):
            tok0 = 16 + 128 * GROUPS[s][0]
            nt_ = 128 * GROUPS[s][1]
            bk = pA if h % 2 == 0 else pB_
            proj_fm(bk, wb("wrg"), R_RG + h * 128, tok0, nt_)
            P.op("scalar", lambda e: e.activation(out=sgr[s % 2][:, h, 0:nt_], in_=bank[bk][:, 0:nt_], func=AF.Silu),
                 reads=[bankb[bk], buf(f"sgr{s % 2}")], writes=[buf(f"sgr{s % 2}")], n=nt_)

        p2 = {"slot": 0, "cs": 0}
        xinfo = {}

        def issue_reload(t):
            sl = p2["slot"] % 3
            p2["slot"] += 1
            xinfo[t] = sl
            P.dma("sync", f"xl{sl}", xt[sl][:], x_d[t * 128:(t + 1) * 128, :],
                  writes=[buf(f"xt{sl}"), buf(f"xt{sl}_h0"), buf(f"xt{sl}_h1")], nbytes=524288)
            csl = p2["cs"] % 4
            p2["cs"] += 1
            xinfo[("cs", t)] = csl
            P.dma("sync", f"csl{csl}", cst[csl][:], cs_v[:, t + 1, :], writes=[buf(f"cs{csl}")])

        out_ops = []

        def ret_A(t, cur):
            csl = xinfo[("cs", t)]
            csb = buf(f"cs{csl}")
            bO = O0 + (t % 2)
            hT = hnT[:, :, 16 + t * 128:16 + (t + 1) * 128]
            P.op("tensor", [lambda e, kc=kc: e.matmul(bank[X1][:], lhsT=hT[:, kc, :], rhs=wrest[:, kc, R_Q:R_Q + 512],
                                                       start=(kc == 0), stop=(kc == 7)) for kc in range(8)],
                 reads=[buf(f"hnT{t}")] + wb("wq"), writes=[bankb[X1]], c=8 * MM512)
            rotary(X1, a_bc, csl, csb, qrot[:], buf("qrot"))
            pT = bf_view(X1)
            P.op("tensor", [lambda e, h=h: e.transpose(out=pT[:, h, :], in_=qrot[:, h * 128:(h + 1) * 128],
                                                        identity=identb[:]) for h in range(4)] +
                           [lambda e, h=h: e.transpose(out=pT[:, 4 + h, :], in_=ktok[:, t, h * 128:(h + 1) * 128],
                                                        identity=identb[:]) for h in range(4)],
                 reads=[buf("qrot"), buf(f"ktok{t}"), buf("identb")], writes=[bankb[X1]], c=8 * TR)
            P.op("scalar", lambda e: e.activation(out=qkT[:], in_=pT, func=AF.Copy),
                 reads=[bankb[X1]], writes=[buf("qkT")], n=1024)
            pSv = h_view(X2)
            P.op("tensor", [lambda e, h=h: e.matmul(pSv[:, h, :], lhsT=qkT[:, 4 + h, :], rhs=qkT[:, h, :],
                                                     start=True, stop=True) for h in range(4)],
                 reads=[buf("qkT")], writes=[bankb[X2]], c=4 * MM128)
            mask_bc = maskT[:].unsqueeze(1).broadcast_to([128, 4, 128])
            P.op("vector", lambda e: e.tensor_tensor(out=sT[:], in0=pSv, in1=mask_bc, op=ALU.mult),
                 reads=[bankb[X2], buf("maskT")], writes=[buf("sT")], n=512)
            pOv = h_view(bO)
            fl = []
            for h in range(4):
                fl.append(lambda e, h=h: e.matmul(pOv[:, h, :], lhsT=sT[:, h, :], rhs=vtok[:, t, h * 128:(h + 1) * 128],
                                                   start=True, stop=False))
                fl.append(lambda e, h=h: e.matmul(pOv[:, h, :], lhsT=qkT[:, h, :], rhs=Sbf[cur][:, h, :],
                                                   start=False, stop=True))
            P.op("tensor", fl, reads=[buf("sT"), buf(f"vtok{t}"), buf("qkT"), Sb[cur]], writes=[bankb[bO]],
                 c=8 * MM128)
            if t < NT - 1:
                pUv = h_view(X2)
                P.op("tensor", [lambda e, h=h: e.matmul(pUv[:, h, :], lhsT=ktok[:, t, h * 128:(h + 1) * 128],
                                                         rhs=vtok[:, t, h * 128:(h + 1) * 128], start=True, stop=True)
                                for h in range(4)],
                     reads=[buf(f"ktok{t}"), buf(f"vtok{t}")], writes=[bankb[X2]], c=4 * MM128)
                P.op("vector", [lambda e, h=h: e.scalar_tensor_tensor(out=Rst[:, h, :], in0=Rst[:, h, :],
                                                                       scalar=float(CD[h]), in1=pUv[:, h, :],
                                                                       op0=ALU.mult, op1=ALU.add) for h in range(4)],
                     reads=[bankb[X2], buf("R")], writes=[buf("R")], c=0.9)
                make_S(1 - cur)

        def ret_B(t):
            s, i = TILE_G[t]
            sl = xinfo[t]
            xb = buf(f"xt{sl}")
            xs = xt[sl]
            par = t % 2
            bO = O0 + par
            pOv = h_view(bO)
            sg = sgr[s % 2]
            sgB = buf(f"sgr{s % 2}")
            mx, mxB = mixc[s % 2], buf(f"mixc{s % 2}")
            stB = buf(f"gn{par}")
            P.op("vector", [lambda e, h=h: e.bn_stats(out=bns[:, par, h, :], in_=pOv[:, h, :]) for h in range(4)],
                 reads=[bankb[bO]], writes=[buf(f"bns{par}")], c=0.9)
            P.op("vector", [lambda e, h=h: e.bn_aggr(out=mv[:, par, h, :], in_=bns[:, par, h, :]) for h in range(4)],
                 reads=[buf(f"bns{par}")], writes=[buf(f"mv{par}")], c=0.4)
            P.op("vector", lambda e: e.tensor_scalar(out=gve[:, par, :], in0=mv[:, par, :, 1], scalar1=EPS,
                                                     scalar2=None, op0=ALU.add),
                 reads=[buf(f"mv{par}")], writes=[stB], n=4)
            P.op("gpsimd", lambda e: e.tensor_tensor(out=grs[:, par, :], in0=gve[:, par, :], in1=mhalf[:], op=ALU.pow),
                 reads=[stB, buf("mhalf")], writes=[buf(f"grs{par}")], c=1.0)
            P.op("vector", lambda e: e.scalar_tensor_tensor(out=gnm[:, par, :], in0=mv[:, par, :, 0], scalar=-1.0,
                                                            in1=grs[:, par, :], op0=ALU.mult, op1=ALU.mult),
                 reads=[buf(f"mv{par}"), buf(f"grs{par}")], writes=[buf(f"gnm{par}")], n=4)
            P.op("scalar", [lambda e, h=h: e.activation(out=ybuf[:, h, :], in_=pOv[:, h, :], func=AF.Identity,
                                                         bias=gnm[:, par, h:h + 1], scale=grs[:, par, h:h + 1])
                            for h in range(4)],
                 reads=[bankb[bO], buf(f"grs{par}"), buf(f"gnm{par}")], writes=[buf("ybuf")], c=1.5)
            for half in range(2):
                bo = pY if half == 0 else pZ
                P.op("tensor", [lambda e, ec=ec, half=half, bo=bo: e.matmul(bank[bo][:], lhsT=mx[:, ec, i * 128:(i + 1) * 128],
                                                                              rhs=wout[:, ec, half * 512:(half + 1) * 512],
                                                                              start=(ec == 0), stop=False)
                                for ec in range(4)],
                     reads=[mxB] + wb("wout0") + wb("wout1"), writes=[bankb[bo]], c=4 * MM512)
            pT = bf_view(bO)
            P.op("tensor", [lambda e, h=h: e.transpose(out=pT[:, h, :], in_=ybuf[:, h, :], identity=identb[:])
                            for h in range(4)],
                 reads=[buf("ybuf"), buf("identb")], writes=[bankb[bO]], c=4 * TR)
            P.op("vector", [lambda e, h=h: e.scalar_tensor_tensor(out=mixr[:, h, i * 128:(i + 1) * 128], in0=pT[:, h, :],
                                                                   scalar=gret[:, h:h + 1],
                                                                   in1=sg[:, h, i * 128:(i + 1) * 128],
                                                                   op0=ALU.mult, op1=ALU.mult) for h in range(4)],
                 reads=[bankb[bO], buf("gret"), sgB, buf("mixr")], writes=[buf("mixr")], c=0.9)
            for half in range(2):
                bo = pY if half == 0 else pZ
                P.op("tensor", [lambda e, ec=ec, half=half, bo=bo: e.matmul(bank[bo][:], lhsT=mixr[:, ec - 4, i * 128:(i + 1) * 128],
                                                                              rhs=wout[:, ec, half * 512:(half + 1) * 512],
                                                                              start=False, stop=(ec == 7))
                                for ec in range(4, 8)],
                     reads=[buf("mixr")] + wb("wout0") + wb("wout1"), writes=[bankb[bo]], c=4 * MM512)
                P.op("vector", lambda e, half=half, bo=bo: e.tensor_tensor(out=xs[:, half * 512:(half + 1) * 512],
                                                                           in0=bank[bo][:],
                                                                           in1=xs[:, half * 512:(half + 1) * 512], op=ALU.add),
                     reads=[bankb[bo], xb], writes=[xb], n=512)
            rs, rsb = rms_rstd(xs[:], xb, 32 + t)
            for half in range(2):
                cs_ = slice(half * 512, (half + 1) * 512)
                hb = buf(f"xt{sl}_h{half}")
                P.op("vector", lambda e, cs_=cs_: e.scalar_tensor_tensor(out=xs[:, cs_], in0=xs[:, cs_], scalar=rs,
                                                                          in1=gbc[:, cs_], op0=ALU.mult, op1=ALU.mult),
                     reads=[xb, rsb, buf("gbc")], writes=[hb], n=512)
                out_ops.append(P.dma("sync", f"st{sl}_{half}", y_d[t * 128:(t + 1) * 128, cs_], xs[:, cs_], reads=[hb],
                                     nbytes=262144))

        issue_reload(0)
        issue_reload(1)
        for cb_i in range(4):
            conv_unit(0, cb_i)
        for h in range(4):
            rg_unit(0, h)
        cur = 0
        for gi, (t0g, ntg) in enumerate(GROUPS):
            nxt = gi + 1 if gi + 1 < len(GROUPS) else None
            units = ([("c", cb) for cb in range(4)] + [("r", h) for h in range(4)]) if nxt is not None else []
            per = (len(units) + 2 * ntg - 1) // (2 * ntg) if units else 0
            for j in range(ntg):
                t = t0g + j
                if t + 2 < NT:
                    issue_reload(t + 2)
                ret_A(t, cur)
                cur = 1 - cur
                for _ in range(per):
                    if units:
                        kind_u, a_u = units.pop(0)
                        (conv_unit if kind_u == "c" else rg_unit)(nxt, a_u)
                ret_B(t)
                for _ in range(per):
                    if units:
                        kind_u, a_u = units.pop(0)
                        (conv_unit if kind_u == "c" else rg_unit)(nxt, a_u)
            assert not units
        P.final = ("sync", out_ops)
        P.run()
    return nc


def _host_constants():
    h = np.arange(NH)
    gam = (1.0 - 2.0 ** (-5.0 - h)).astype(np.float64)
    idx = np.arange(128, dtype=np.float64)
    a = (128.0 ** -0.5) * gam[None, :] ** (idx[:, None] + 1.0)
    b = gam[None, :] ** (-(idx[:, None] + 1.0))
    ab = np.concatenate([a, b], axis=1).astype(np.float32)
    maskT = (idx[None, :] >= idx[:, None]).astype(np.float32)
    ident = np.eye(128, dtype=np.float32)
    half = 64
    freqs = (1.0 / (np.float32(10000.0) ** (np.arange(half, dtype=np.float32) / np.float32(half)))).astype(np.float32)
    cd = gam ** 128
    return gam, cd, ab, maskT, ident, freqs


def _cs_table(freqs, positions):
    ang = (positions.astype(np.float32)[:, None] * freqs[None, :]).astype(np.float32)
    return np.concatenate([np.cos(ang), np.sin(ang)], axis=1).astype(np.float32)


def _cs_table_all():
    L = NMETA + SEQ
    try:
        import jax
        import jax.numpy as jnp
        with jax.default_device(jax.devices("cpu")[0]):
            half = 64
            fr = 1.0 / (10000.0 ** (jnp.arange(half, dtype=jnp.float32) / half))
            ang = jnp.arange(L, dtype=jnp.int32).astype(jnp.float32)[:, None] * fr[None, :]
            tab = np.concatenate([np.asarray(jnp.cos(ang)), np.asarray(jnp.sin(ang))], axis=1)
        return np.ascontiguousarray(tab.astype(np.float32))
    except Exception:
        freqs = (1.0 / (np.float32(10000.0) ** (np.arange(64, dtype=np.float32) / np.float32(64)))).astype(np.float32)
        return _cs_table(freqs, np.arange(L))


_NC_CACHE = {}


def kernel(x, meta, norm1_g, w_in, conv_w, ret_norm_g, w_out, final_g):
    x = np.ascontiguousarray(np.asarray(x, dtype=np.float32))
    meta = np.ascontiguousarray(np.asarray(meta, dtype=np.float32))
    w_in = np.ascontiguousarray(np.asarray(w_in, dtype=np.float32))
    w_out = np.ascontiguousarray(np.asarray(w_out, dtype=np.float32))
    norm1_g = np.asarray(norm1_g, dtype=np.float32).reshape(1, D)
    final_g = np.asarray(final_g, dtype=np.float32).reshape(1, D)
    conv_w = np.asarray(conv_w, dtype=np.float32)
    ret_norm_g = np.asarray(ret_norm_g, dtype=np.float32)

    gam, cd, ab, maskT, ident, freqs = _host_constants()
    convw_l = np.ascontiguousarray(conv_w.reshape(3, 4, 128).transpose(2, 1, 0).reshape(128, 12))
    gret_l = np.ascontiguousarray(ret_norm_g.reshape(4, 128).T)

    if "nc" not in _NC_CACHE:
        _NC_CACHE["nc"] = build_nc()
    nc = _NC_CACHE["nc"]

    cs_all = _cs_table_all()
    in_maps = []
    for core in range(NCORE):
        b, c = divmod(core, 4)
        xs = x[b, c * TOK:(c + 1) * TOK, :]
        xprev = meta if c == 0 else x[b, c * TOK - 16:c * TOK, :]
        cs = np.zeros((17 * 128, 128), np.float32)
        cs[112:128] = cs_all[0:NMETA]
        cs[128:] = cs_all[NMETA + c * TOK:NMETA + (c + 1) * TOK]
        coef = np.zeros((1, 20), np.float64)
        coef[0, 0:4] = cd ** (16 * c)
        for r in range(4):
            if r < c:
                coef[0, 4 + 4 * r:8 + 4 * r] = cd ** (16 * (c - 1 - r))
        in_maps.append({
            "x": np.ascontiguousarray(xs), "xprev": np.ascontiguousarray(xprev), "meta": meta,
            "w_in": w_in, "w_out": w_out, "norm1_g": norm1_g, "final_g": final_g,
            "convw": convw_l, "gret": gret_l, "cs": cs, "ab": ab, "maskT": maskT, "ident": ident,
            "coef": coef.astype(np.float32),
        })
    res = run_bass_kernel_spmd(nc, in_maps, core_ids=list(range(NCORE)))
    out = np.empty((BATCH, SEQ, D), np.float32)
    for core in range(NCORE):
        b, c = divmod(core, 4)
        out[b, c * TOK:(c + 1) * TOK, :] = res.results[core]["y"]
    return out
```
